# Optimizing a Trainium2 kernel written in Bass

```python
import math
import jax, jax.numpy as jnp
from jax import lax
import numpy as np

D_MODEL = 1024
BATCH = 8
SEQ = 4096
DEPTH = 2

S5_WIDTH = D_MODEL // 2
S5_GROUP = 16
S5_GROUPS = S5_WIDTH // S5_GROUP
S5_STATE = 64
DT_MIN = 0.001
DT_MAX = 0.1
HEAD_DIM = 64
N_Q_HEADS = 8
N_KV_HEADS = 2
Q_PER_KV = N_Q_HEADS // N_KV_HEADS
ATT_WIDTH = N_Q_HEADS * HEAD_DIM
KV_WIDTH = N_KV_HEADS * HEAD_DIM
WINDOW = 128
BLOCK = 128
NUM_BUCKETS = 32
MAX_DISTANCE = 128
D_FF = ((8 * D_MODEL // 3 + 255) // 256) * 256
RMS_EPS = 1e-6
NEG_INF = -1e30
IN_WIDTH = S5_WIDTH + ATT_WIDTH + 2 * KV_WIDTH + 2 * D_MODEL

kernel_name = "hybrid_s5_swa_gated_encoder"


def rmsnorm(x, g):
    xf = x.astype(jnp.float32)
    y = xf * lax.rsqrt(jnp.mean(xf * xf, axis=-1, keepdims=True) + RMS_EPS) * g.astype(jnp.float32)
    return y.astype(x.dtype)


def t5_bucket(rel):
    half = NUM_BUCKETS // 2
    max_exact = half // 2
    ret = jnp.where(rel > 0, half, 0)
    n = jnp.abs(rel)
    nf = jnp.maximum(n, 1).astype(jnp.float32)
    large = max_exact + (jnp.log(nf / max_exact) / math.log(MAX_DISTANCE / max_exact)
                         * (half - max_exact)).astype(jnp.int32)
    large = jnp.minimum(large, half - 1)
    return ret + jnp.where(n < max_exact, n, large)


def band_geometry(seq):
    n_blocks = seq // BLOCK
    q_loc = jnp.arange(BLOCK, dtype=jnp.int32)
    k_loc = jnp.arange(3 * BLOCK, dtype=jnp.int32)
    rel = (k_loc[None, :] - BLOCK) - q_loc[:, None]
    in_window = jnp.abs(rel) <= WINDOW
    k_glob = jnp.arange(n_blocks, dtype=jnp.int32)[:, None] * BLOCK - BLOCK + k_loc[None, :]
    k_valid = (k_glob >= 0) & (k_glob < seq)
    mask = in_window[None, :, :] & k_valid[:, None, :]
    return rel, mask


def _complex_affine_combine(e1, e2):
    a1r, a1i, b1r, b1i = e1
    a2r, a2i, b2r, b2i = e2
    ar = a2r * a1r - a2i * a1i
    ai = a2r * a1i + a2i * a1r
    br = a2r * b1r - a2i * b1i + b2r
    bi = a2r * b1i + a2i * b1r + b2i
    return (ar, ai, br, bi)


def s5_branch(u, lam_re, lam_im, log_dt, b_re, b_im, c_re, c_im, d, w_glu):
    bsz, seq, _ = u.shape
    uf = u.astype(jnp.float32)
    ug = uf.reshape(bsz, seq, S5_GROUPS, S5_GROUP)
    y = d.astype(jnp.float32) * uf
    for direction, reverse in enumerate((False, True)):
        lr = lam_re[direction].astype(jnp.float32)
        li = lam_im[direction].astype(jnp.float32)
        dt = jnp.exp(log_dt[direction].astype(jnp.float32))[:, None]
        mag = jnp.exp(lr * dt)
        ab_re = mag * jnp.cos(li * dt)
        ab_im = mag * jnp.sin(li * dt)
        nr = ab_re - 1.0
        den = lr * lr + li * li
        coef_re = (nr * lr + ab_im * li) / den
        coef_im = (ab_im * lr - nr * li) / den
        br = b_re[direction].astype(jnp.float32)
        bi = b_im[direction].astype(jnp.float32)
        bb_re = coef_re[..., None] * br - coef_im[..., None] * bi
        bb_im = coef_re[..., None] * bi + coef_im[..., None] * br
        bu_re = jnp.einsum('blgh,gph->blgp', ug, bb_re)
        bu_im = jnp.einsum('blgh,gph->blgp', ug, bb_im)
        a_re = jnp.broadcast_to(ab_re, bu_re.shape)
        a_im = jnp.broadcast_to(ab_im, bu_im.shape)
        _, _, s_re, s_im = lax.associative_scan(
            _complex_affine_combine, (a_re, a_im, bu_re, bu_im), axis=1, reverse=reverse)
        y_dir = (jnp.einsum('blgp,ghp->blgh', s_re, c_re[direction].astype(jnp.float32))
                 - jnp.einsum('blgp,ghp->blgh', s_im, c_im[direction].astype(jnp.float32)))
        y = y + y_dir.reshape(bsz, seq, S5_WIDTH)
    z = jax.nn.gelu(y)
    out = z * jax.nn.sigmoid(z @ w_glu.astype(jnp.float32))
    return out.astype(u.dtype)


def windowed_gqa(q, k, v, sink, bias, mask):
    bsz, seq, _ = q.shape
    nb = seq // BLOCK
    qb = q.reshape(bsz, nb, BLOCK, N_KV_HEADS, Q_PER_KV, HEAD_DIM)

    def blocks(t):
        t = t.reshape(bsz, seq, N_KV_HEADS, HEAD_DIM)
        t = jnp.pad(t, ((0, 0), (BLOCK, BLOCK), (0, 0), (0, 0)))
        t = t.reshape(bsz, nb + 2, BLOCK, N_KV_HEADS, HEAD_DIM)
        return jnp.concatenate([t[:, :-2], t[:, 1:-1], t[:, 2:]], axis=2)

    kb = blocks(k)
    vb = blocks(v)
    scale = HEAD_DIM ** -0.5
    logits = jnp.einsum('bnqkgd,bnskd->bnkgqs', qb, kb).astype(jnp.float32) * scale
    logits = logits + bias.reshape(N_KV_HEADS, Q_PER_KV, BLOCK, 3 * BLOCK).astype(jnp.float32)
    logits = jnp.where(mask[None, :, None, None, :, :], logits, NEG_INF)
    sink_col = jnp.broadcast_to(sink.astype(jnp.float32).reshape(N_KV_HEADS, Q_PER_KV, 1, 1),
                                logits.shape[:-1] + (1,))
    probs = jax.nn.softmax(jnp.concatenate([logits, sink_col], axis=-1), axis=-1)[..., :-1]
    out = jnp.einsum('bnkgqs,bnskd->bnqkgd', probs.astype(v.dtype), vb)
    return out.reshape(bsz, seq, ATT_WIDTH)


def setup_inputs(seed: int = 0) -> dict:
    key = jax.random.key(seed)
    ks = jax.random.split(key, 24)
    f32 = jnp.float32
    nrm = lambda k, shape, s: jax.random.normal(k, shape, f32) * s
    G, P, H = S5_GROUPS, S5_STATE, S5_GROUP
    lam_im = jnp.broadcast_to(math.pi * jnp.arange(P, dtype=f32), (DEPTH, 2, G, P))
    return {
        "x": jax.random.normal(ks[0], (BATCH, SEQ, D_MODEL), f32),
        "norm1_g": 1.0 + nrm(ks[1], (DEPTH, D_MODEL), 0.02),
        "norm2_g": 1.0 + nrm(ks[2], (DEPTH, D_MODEL), 0.02),
        "final_g": 1.0 + nrm(ks[3], (D_MODEL,), 0.02),
        "w_in": nrm(ks[4], (DEPTH, D_MODEL, IN_WIDTH), D_MODEL ** -0.5),
        "s5_lambda_re": -0.5 + nrm(ks[5], (DEPTH, 2, G, P), 0.01),
        "s5_lambda_im": lam_im + nrm(ks[6], (DEPTH, 2, G, P), 0.01),
        "s5_log_dt": jax.random.uniform(ks[7], (DEPTH, 2, G), f32, math.log(DT_MIN), math.log(DT_MAX)),
        "s5_b_re": nrm(ks[8], (DEPTH, 2, G, P, H), (2 * H) ** -0.5),
        "s5_b_im": nrm(ks[9], (DEPTH, 2, G, P, H), (2 * H) ** -0.5),
        "s5_c_re": nrm(ks[10], (DEPTH, 2, G, H, P), P ** -0.5),
        "s5_c_im": nrm(ks[11], (DEPTH, 2, G, H, P), P ** -0.5),
        "s5_d": nrm(ks[12], (DEPTH, S5_WIDTH), 1.0),
        "s5_w_glu": nrm(ks[13], (DEPTH, S5_WIDTH, S5_WIDTH), S5_WIDTH ** -0.5),
        "attn_sink": nrm(ks[14], (DEPTH, N_Q_HEADS), 0.1),
        "rel_bias": nrm(ks[15], (NUM_BUCKETS, N_Q_HEADS), 0.1),
        "w_branch_a": nrm(ks[16], (DEPTH, S5_WIDTH, D_MODEL), S5_WIDTH ** -0.5),
        "w_branch_b": nrm(ks[17], (DEPTH, ATT_WIDTH, D_MODEL), ATT_WIDTH ** -0.5),
        "w_out": nrm(ks[18], (DEPTH, D_MODEL, D_MODEL), D_MODEL ** -0.5),
        "ffn_w_gate": nrm(ks[19], (DEPTH, D_MODEL, D_FF), D_MODEL ** -0.5),
        "ffn_w_up": nrm(ks[20], (DEPTH, D_MODEL, D_FF), D_MODEL ** -0.5),
        "ffn_w_down": nrm(ks[21], (DEPTH, D_FF, D_MODEL), D_FF ** -0.5),
    }


def reference(x, norm1_g, norm2_g, final_g, w_in, s5_lambda_re, s5_lambda_im, s5_log_dt,
              s5_b_re, s5_b_im, s5_c_re, s5_c_im, s5_d, s5_w_glu, attn_sink, rel_bias,
              w_branch_a, w_branch_b, w_out, ffn_w_gate, ffn_w_up, ffn_w_down):
    seq = x.shape[1]
    rel, mask = band_geometry(seq)
    bias = jnp.transpose(rel_bias[t5_bucket(rel)], (2, 0, 1))
    o_q = S5_WIDTH
    o_k = o_q + ATT_WIDTH
    o_v = o_k + KV_WIDTH
    o_ga = o_v + KV_WIDTH
    o_gb = o_ga + D_MODEL
    h = x
    for layer in range(DEPTH):
        hn = rmsnorm(h, norm1_g[layer])
        proj = hn @ w_in[layer]
        u = proj[..., :o_q]
        q = proj[..., o_q:o_k]
        k = proj[..., o_k:o_v]
        v = proj[..., o_v:o_ga]
        gate_a = jax.nn.sigmoid(proj[..., o_ga:o_gb].astype(jnp.float32)).astype(h.dtype)
        gate_b = jax.nn.sigmoid(proj[..., o_gb:].astype(jnp.float32)).astype(h.dtype)
        y_a = s5_branch(u, s5_lambda_re[layer], s5_lambda_im[layer], s5_log_dt[layer],
                        s5_b_re[layer], s5_b_im[layer], s5_c_re[layer], s5_c_im[layer],
                        s5_d[layer], s5_w_glu[layer])
        y_b = windowed_gqa(q, k, v, attn_sink[layer], bias, mask)
        merged = gate_a * (y_a @ w_branch_a[layer]) + gate_b * (y_b @ w_branch_b[layer])
        h = h + merged @ w_out[layer]
        hn2 = rmsnorm(h, norm2_g[layer])
        ff = jax.nn.silu(hn2 @ ffn_w_gate[layer]) * (hn2 @ ffn_w_up[layer])
        h = h + ff @ ffn_w_down[layer]
    return rmsnorm(h, final_g)
```

```python
import math
import bisect
import numpy as np
import concourse.bass as bass
import concourse.mybir as mybir
from concourse.bass_utils import run_bass_kernel_spmd
from concourse.ap import AP

F32 = mybir.dt.float32
BF16 = mybir.dt.bfloat16
I32 = mybir.dt.int32
ALU = mybir.AluOpType
AF = mybir.ActivationFunctionType

D = 1024
KC = 8
DFF = 2816
NF = 22
DEPTH = 2
TWO_PI = 2.0 * math.pi
PI_LO = 3.1415925
NTK = 77


class Prog:
    ENGS = ["pe", "act", "dve", "pool", "sp"]

    def __init__(self, nc):
        self.nc = nc
        self.ops = []

    def add(self, eng, fn, r=(), w=(), dk=None):
        self.ops.append(dict(eng=eng, fn=fn, r=tuple(r) + ("PH",), w=tuple(w), dk=dk, sig=False))

    def barrier(self):
        self.ops.append(dict(eng="dve", fn=("bar",), r=(), w=("PH",), dk=None, sig=False))

    def mm(self, out, lhsT, rhs, start, stop, r, w):
        self.add("pe", lambda e: e.matmul(out, lhsT, rhs, start=start, stop=stop), r, w)

    def pe(self, fn, r, w):
        self.add("pe", fn, r, w)

    def act(self, out, in_, func, r, w, bias=None, scale=None):
        kw = {}
        if bias is not None:
            kw["bias"] = bias
        if scale is not None:
            kw["scale"] = scale
        self.add("act", lambda e: e.activation(out=out, in_=in_, func=func, **kw), r, w)

    def tt(self, out, in0, in1, op, r, w, eng="dve"):
        self.add(eng, lambda e: e.tensor_tensor(out=out, in0=in0, in1=in1, op=op), r, w)

    def ts(self, out, in0, s1, s2, op0, op1, r, w, eng="dve"):
        if op1 is None:
            self.add(eng, lambda e: e.tensor_scalar(out=out, in0=in0, scalar1=s1, scalar2=None, op0=op0), r, w)
        else:
            self.add(eng, lambda e: e.tensor_scalar(out=out, in0=in0, scalar1=s1, scalar2=s2, op0=op0, op1=op1), r, w)

    def stt(self, out, in0, scalar, in1, op0, op1, r, w, eng="dve"):
        self.add(eng, lambda e: e.scalar_tensor_tensor(out=out, in0=in0, scalar=scalar, in1=in1, op0=op0, op1=op1), r, w)

    def copy(self, out, in_, r, w, eng="dve"):
        self.add(eng, lambda e: e.tensor_copy(out=out, in_=in_), r, w)

    def memset(self, ap, val, w, eng="dve"):
        self.add(eng, lambda e: e.memset(ap, val), (), w)

    def dma(self, out, in_, key, r, w, eng="sp"):
        self.add(eng, lambda e: e.dma_start(out=out, in_=in_), r, w, dk=key + "_" + eng)

    def emit(self, final_keys):
        nc = self.nc
        ops = self.ops
        last_w = {}
        readers = {}
        for i, op in enumerate(ops):
            deps = set()
            for k in op["r"]:
                if k in last_w:
                    deps.add(last_w[k])
            for k in op["w"]:
                if k in last_w:
                    deps.add(last_w[k])
                deps.update(readers.get(k, ()))
            deps.discard(i)
            op["deps"] = deps
            for k in op["r"]:
                readers.setdefault(k, []).append(i)
            for k in op["w"]:
                last_w[k] = i
                readers[k] = []
        for i, op in enumerate(ops):
            for d in op["deps"]:
                src = ops[d]
                if src["dk"] is None:
                    if src["eng"] == "pe" and op["eng"] == "pe":
                        continue
                    src["sig"] = True
        cnt = {e: 0 for e in self.ENGS}
        dma_hist = {}
        for i, op in enumerate(ops):
            if op["dk"] is not None:
                h = dma_hist.setdefault(op["dk"], [[], []])
                prev = h[1][-1] if h[1] else 0
                h[0].append(i)
                h[1].append(prev + 16)
            elif op["sig"]:
                cnt[op["eng"]] += 1
                op["cnt"] = cnt[op["eng"]]
        waited = {e: {} for e in self.ENGS}
        for i, op in enumerate(ops):
            need = {}
            for d in op["deps"]:
                src = ops[d]
                if src["dk"] is not None:
                    h = dma_hist[src["dk"]]
                    j = bisect.bisect_left(h[0], i) - 1
                    sem = ("dma", src["dk"])
                    val = h[1][j]
                else:
                    if src["eng"] == "pe" and op["eng"] == "pe":
                        continue
                    sem = ("eng", src["eng"])
                    val = src["cnt"]
                need[sem] = max(need.get(sem, 0), val)
            wl = []
            wd = waited[op["eng"]]
            for sem, val in need.items():
                if wd.get(sem, 0) < val:
                    wd[sem] = val
                    wl.append((sem, val))
            op["waits"] = wl
        sem_names = [("eng", e) for e in self.ENGS] + [("dma", k) for k in dma_hist]
        self.n_sems = len(sem_names)
        from contextlib import ExitStack
        with ExitStack() as es:
            semh = {}
            for sn in sem_names:
                semh[sn] = es.enter_context(nc.semaphore("s_%s_%s" % sn))
            block = es.enter_context(nc.Block())

            def run_engine(engname):
                def body(e):
                    for op in ops:
                        if op["eng"] != engname:
                            continue
                        for sem, val in op["waits"]:
                            e.wait_ge(semh[sem], val)
                        if op["fn"] == ("bar",):
                            ins = e.engine_nop() if False else e.memset(self.bar_ap, 0.0)
                        else:
                            ins = op["fn"](e)
                        if op["dk"] is not None:
                            ins.then_inc(semh[("dma", op["dk"])], 16)
                        elif op["sig"]:
                            ins.then_inc(semh[("eng", engname)], 1)
                    if engname == "sp":
                        for k in final_keys:
                            e.wait_ge(semh[("dma", k)], dma_hist[k][1][-1])
                return body

            block.tensor(run_engine("pe"))
            block.scalar(run_engine("act"))
            block.vector(run_engine("dve"))
            block.gpsimd(run_engine("pool"))
            block.sync(run_engine("sp"))


def build(L, dbg_items=(), upto=99):
    NG = L // 512
    NCH = L // 32
    NB = L // 128
    GB = 4
    NBATCH = 32 // GB
    nc = bass.Bass("TRN2", target_bir_lowering=False)

    def din(name, shape, dt=F32):
        return nc.dram_tensor(name, list(shape), dt, kind="ExternalInput").ap()

    xT = din("xT", [NG, 128, KC, 512])
    gains = din("gains", [128, 5, KC])
    w_in = din("w_in", [DEPTH, D, 3456])
    s5a = din("s5a", [DEPTH, 128, 3, 32])
    s5b = din("s5b", [DEPTH, 128, 4, 32, 16])
    s5d = din("s5d", [DEPTH, 4, 32, 64])
    w_glu = din("w_glu", [DEPTH, 512, 512])
    w_a = din("w_a", [DEPTH, 512, D])
    w_b = din("w_b", [DEPTH, 512, D])
    w_out = din("w_out", [DEPTH, D, D])
    w_fg = din("w_fg", [DEPTH, D, DFF])
    w_fu = din("w_fu", [DEPTH, D, DFF])
    w_fd = din("w_fd", [DEPTH, DFF, D])
    sinkb = din("sinkb", [128, DEPTH, 8])
    biasg = din("biasg", [128, 3, 8, 128])
    maskc = din("maskc", [128, 3, 128])
    cst = din("cst", [128, NTK + 128 + 1 + 16])
    identd = din("identd", [128, 128])
    out = nc.dram_tensor("out", [NG * 2, 128, KC, 256], F32, kind="ExternalOutput").ap()
    dbg = None
    if dbg_items:
        dbg = nc.dram_tensor("dbg", [len(dbg_items), 128, 4096], F32, kind="ExternalOutput").ap()

    hscr = nc.dram_tensor("hscr", [NG, 128, KC, 512], F32).ap()
    KD = nc.dram_tensor("KD", [32, 4, 64, 16, 4], BF16).ap()
    WAs = nc.dram_tensor("WAs", [32, 128, 1024], BF16).ap()
    WCs = nc.dram_tensor("WCs", [32, 128, 1024], BF16).ap()
    TABd = nc.dram_tensor("TABd", [3, 128, 32 * NCH], F32).ap()

    P = Prog(nc)
    from contextlib import ExitStack
    es = ExitStack()
    AW = 45056
    arena = es.enter_context(nc.sbuf_tensor("arena", [128, AW], F32))
    small = es.enter_context(nc.sbuf_tensor("small", [128, 6144], F32))
    pst = [es.enter_context(nc.psum_tensor("ps%d" % i, [128, 512], F32)) for i in range(8)]
    a32 = arena[:]
    a16 = arena[:].bitcast(BF16)
    ai32 = arena[:].bitcast(I32)
    s32 = small[:]
    s16 = small[:].bitcast(BF16)
    P.bar_ap = s32[0:1, 4095:4096]

    def V(base, off, *shape, p0=0, pn=128):
        n = int(np.prod(shape))
        v = base[p0:p0 + pn, off:off + n]
        if len(shape) == 1:
            return v
        names = "abcde"[:len(shape)]
        kw = {names[i]: shape[i] for i in range(len(shape))}
        return v.rearrange("p (%s) -> p %s" % (" ".join(names), " ".join(names)), **kw)

    def f32v(offb, *shape, **kw):
        return V(a32, offb // 4, *shape, **kw)

    def b16v(offb, *shape, **kw):
        return V(a16, offb // 2, *shape, **kw)

    def ps32(i):
        return pst[i][:]

    def ps16(i):
        return pst[i][:].bitcast(BF16)

    KiB = 1024
    ident = V(s16, 0, 128)
    GN = V(s32, 64, 5, KC)
    ESK = V(s32, 104, DEPTH, 8)
    CST = V(s32, 120, NTK + 128 + 1 + 16)
    NTAB = CST[:, 0:NTK]
    CIDX = CST[:, NTK:NTK + NCH]
    SGN = CST[:, NTK + 128:NTK + 129]
    EYE = CST[0:16, NTK + 129:NTK + 145]
    o = 120 + NTK + 145
    o = (o + 1) // 2 * 2
    BM = V(s16, o * 2, 3, 8, 128)
    o += 1536
    ONES = V(s16, o * 2, 128)
    o += 64
    THS = V(s32, o, 32); o += 32
    XS = V(s32, o, 32); o += 32
    assert o < 2400

    P.dma(ident, identd, "cid", [], ["ident"], eng="pool")
    P.dma(GN, gains, "cgn", [], ["GN"])
    P.dma(CST, cst, "ccst", [], ["CST"])
    P.memset(ONES, 1.0, ["ONES"])
    tb = f32v(0, 3, 8, 128)
    tm = f32v(16 * KiB, 3, 128)
    P.dma(tb, biasg, "ctb", [], ["tb"])
    P.dma(tm, maskc, "ctm", [], ["tm"])
    P.tt(tb, tb, tm.unsqueeze(2).to_broadcast([128, 3, 8, 128]), ALU.add, ["tb", "tm"], ["tb"])
    P.ts(BM, tb, 8.0, None, ALU.mult, None, ["tb"], ["BM"])
    tsk = f32v(20 * KiB, DEPTH, 8)
    P.dma(tsk, sinkb, "ctsk", [], ["tsk"])
    P.act(ESK, tsk, AF.Exp, ["tsk"], ["ESK"])
    P.barrier()

    dbg_slot = {name: i for i, name in enumerate(dbg_items)}

    def wload(dst, srcw, c0, c1, key, wkey, d0=0, extra_w=()):
        nk = dst.shape[1]
        for kc in range(nk):
            P.dma(dst[:, kc, d0:d0 + (c1 - c0)], srcw[kc * 128:(kc + 1) * 128, c0:c1], key, [], [wkey] + list(extra_w), eng="pool")

    def dump(name, ap, keys, n):
        if name in dbg_slot:
            i = dbg_slot[name]
            t = f32v(170 * KiB, n) if False else None
            P.dma(dbg[i][0:ap.shape[0], 0:n], ap, "dbg", keys, ["dbgout"])

    import os
    PCSTAGE = int(os.environ.get("K_PCSTAGE", "9"))

    def s5_precompute(l):
        o = 0
        def nx(nbytes):
            nonlocal o
            r = o
            o += (nbytes + 63) // 64 * 64
            return r
        sm = f32v(nx(128 * 10), 10, 32)
        LAM = f32v(nx(384), 3, 32)
        BC = f32v(nx(8192), 4, 32, 16)
        DB = f32v(nx(8192), 32, 64, p0=0, pn=4)
        P.dma(LAM, s5a[l], "pcl", [], ["LAM"])
        P.dma(BC, s5b[l], "pcb", [], ["BC"])
        P.dma(DB, s5d[l], "pcd", [], ["DB"])
        LR, LI, LDT = LAM[:, 0, :], LAM[:, 1, :], LAM[:, 2, :]
        BR, BI, CR, CI = BC[:, 0], BC[:, 1], BC[:, 2], BC[:, 3]
        DT = f32v(nx(128), 32)
        P.act(DT, LDT, AF.Exp, ["LAM"], ["DT"])
        P.tt(XS, LR, DT, ALU.mult, ["LAM", "DT"], ["XS"])
        P.tt(THS, LI, DT, ALU.mult, ["LAM", "DT"], ["THS"])
        NW = 32 * NTK
        T0 = f32v(nx(NW * 4), 32, NTK)
        T1 = V(ai32, nx(NW * 4) // 4, 32, NTK)
        T2 = f32v(nx(NW * 4), 32, NTK)
        T3 = f32v(nx(NW * 4), 32, NTK)
        T4 = f32v(nx(NW * 4), 32, NTK)

        def bc_g(tab, K):
            return tab.unsqueeze(1).to_broadcast([128, 32, K])

        def bc_k(v, K):
            return v.unsqueeze(2).to_broadcast([128, 32, K])

        def sincos(ANG, TI, TF, SINO, COSO, pre):
            k = [pre + x for x in ("ang", "ti", "tf", "sin", "cos")]
            P.ts(TI, ANG, 1.0 / TWO_PI, None, ALU.mult, None, [k[0]], [k[1]])
            P.copy(TF, TI, [k[1]], [k[2]])
            P.stt(ANG, TF, -TWO_PI, ANG, ALU.mult, ALU.add, [k[2], k[0]], [k[0]])
            P.ts(ANG, ANG, PI_LO, -PI_LO, ALU.min, ALU.max, [k[0]], [k[0]])
            P.act(SINO, ANG, AF.Sin, [k[0]], [k[3]])
            P.ts(TF, ANG, math.pi / 2, None, ALU.is_gt, None, [k[0]], [k[2]])
            P.stt(ANG, TF, -TWO_PI, ANG, ALU.mult, ALU.add, [k[2], k[0], k[3]], [k[0]])
            P.ts(ANG, ANG, math.pi / 2, PI_LO, ALU.add, ALU.min, [k[0]], [k[0]])
            P.act(COSO, ANG, AF.Sin, [k[0]], [k[4]])

        P.tt(T0, bc_k(THS, NTK), bc_g(NTAB, NTK), ALU.mult, ["THS", "CST"], ["p_ang"])
        sincos(T0, T1, T2, T3, T4, "p_")
        P.tt(T0, bc_k(XS, NTK), bc_g(NTAB, NTK), ALU.mult, ["XS", "CST", "p_cos"], ["p_ang"])
        P.act(T2, T0, AF.Exp, ["p_ang"], ["p_tf"])
        P.tt(T4, T2, T4, ALU.mult, ["p_tf", "p_cos"], ["PR"])
        P.tt(T3, T2, T3, ALU.mult, ["p_tf", "p_sin"], ["PI"])
        PR, PI_ = T4, T3
        if PCSTAGE < 2:
            P.barrier(); return
        nr, den, u1, u2, cre, cim = sm[:, 0], sm[:, 1], sm[:, 2], sm[:, 3], sm[:, 4], sm[:, 5]
        abr, abi = PR[:, :, 76], PI_[:, :, 76]
        P.ts(nr, abr, -1.0, None, ALU.add, None, ["PR"], ["nr"])
        P.tt(u1, LR, LR, ALU.mult, ["LAM"], ["u1"])
        P.tt(u2, LI, LI, ALU.mult, ["LAM"], ["u2"])
        P.tt(den, u1, u2, ALU.add, ["u1", "u2"], ["den"])
        P.add("dve", lambda e: e.reciprocal(out=den, in_=den), ["den"], ["den"])
        P.tt(u1, nr, LR, ALU.mult, ["nr", "LAM", "den"], ["u1"])
        P.tt(u2, abi, LI, ALU.mult, ["PI", "LAM", "den"], ["u2"])
        P.tt(cre, u1, u2, ALU.add, ["u1", "u2"], ["cre"])
        P.tt(cre, cre, den, ALU.mult, ["cre", "den"], ["cre"])
        P.tt(u1, abi, LR, ALU.mult, ["PI", "LAM", "cre"], ["u1"])
        P.tt(u2, nr, LI, ALU.mult, ["nr", "LAM", "cre"], ["u2"])
        P.tt(cim, u1, u2, ALU.subtract, ["u1", "u2"], ["cim"])
        P.tt(cim, cim, den, ALU.mult, ["cim", "den"], ["cim"])
        BB = f32v(nx(4096), 2, 32, 16)
        bbr, bbi = BB[:, 0], BB[:, 1]
        W1 = f32v(nx(2048), 32, 16)
        W2 = f32v(nx(2048), 32, 16)

        def bc_h(v):
            return v.unsqueeze(2).to_broadcast([128, 32, 16])

        P.tt(W1, bc_h(cre), BR, ALU.mult, ["cre", "BC"], ["W1"])
        P.tt(W2, bc_h(cim), BI, ALU.mult, ["cim", "BC"], ["W2"])
        P.tt(bbr, W1, W2, ALU.subtract, ["W1", "W2"], ["bbr"])
        P.tt(W1, bc_h(cre), BI, ALU.mult, ["cre", "BC", "bbr"], ["W1"])
        P.tt(W2, bc_h(cim), BR, ALU.mult, ["cim", "BC", "bbr"], ["W2"])
        P.tt(bbi, W1, W2, ALU.add, ["W1", "W2"], ["bbi"])
        K0o = b16v(nx(4096), 4, 32, 16)
        P.copy(K0o[:, 0], bbr, ["bbr"], ["K0o"])
        P.copy(K0o[:, 1], bbi, ["bbi"], ["K0o"])
        P.copy(K0o[:, 2], CR, ["BC"], ["K0o"])
        P.ts(K0o[:, 3], CI, -1.0, None, ALU.mult, None, ["BC"], ["K0o"])
        X4 = b16v(nx(16384), 2, 32, 4, 4, 8)
        Y1 = b16v(nx(8192), 2, 32, 4, 16)
        BT1 = f32v(nx(32768), 4, 2048)
        tA = BT1[:, 0:2].rearrange("p a (g k h) -> p (a g) k h", g=16, k=8, h=16)
        tB = BT1[:, 2:4].rearrange("p a (g k h) -> p (a g) k h", g=16, k=8, h=16)
        kA, kB = ["bt0", "bt1"], ["bt2", "bt3"]

        def g4(tabl):
            return tabl[:, :, 64:72].unsqueeze(3).to_broadcast([128, 32, 8, 16])

        def bb8(v):
            return v.unsqueeze(2).to_broadcast([128, 32, 8, 16])

        P.tt(tA, g4(PR), bb8(bbr), ALU.mult, ["PR", "bbr"], kA)
        P.tt(tB, g4(PI_), bb8(bbi), ALU.mult, ["PI", "bbi"], kB)
        for ht in range(4):
            P.tt(X4[:, 0, :, ht].rearrange("p g l k -> p g k l"), tA[:, :, :, ht * 4:(ht + 1) * 4], tB[:, :, :, ht * 4:(ht + 1) * 4], ALU.subtract, kA + kB, ["X4"])
        P.tt(tA, g4(PR), bb8(bbi), ALU.mult, ["PR", "bbi"], kA)
        P.tt(tB, g4(PI_), bb8(bbr), ALU.mult, ["PI", "bbr"], kB)
        for ht in range(4):
            P.tt(X4[:, 1, :, ht].rearrange("p g l k -> p g k l"), tA[:, :, :, ht * 4:(ht + 1) * 4], tB[:, :, :, ht * 4:(ht + 1) * 4], ALU.add, kA + kB, ["X4"])
        tC = BT1[:, 0:1].rearrange("p a (g k h) -> p (a g) k h", g=32, k=4, h=16)
        tD = BT1[:, 1:2].rearrange("p a (g k h) -> p (a g) k h", g=32, k=4, h=16)

        def g1(tabl):
            return tabl[:, :, 72:76].unsqueeze(3).to_broadcast([128, 32, 4, 16])

        def c4(v):
            return v.unsqueeze(2).to_broadcast([128, 32, 4, 16])

        P.tt(tC, g1(PR), c4(CR), ALU.mult, ["PR", "BC"], ["bt0"])
        P.tt(tD, g1(PI_), c4(CI), ALU.mult, ["PI", "BC"], ["bt1"])
        P.tt(Y1[:, 0], tC, tD, ALU.subtract, ["bt0", "bt1"], ["Y1"])
        P.tt(tC, g1(PI_), c4(CR), ALU.mult, ["PI", "BC"], ["bt0"])
        P.tt(tD, g1(PR), c4(CI), ALU.mult, ["PR", "BC"], ["bt1"])
        P.stt(Y1[:, 1], tC, -1.0, tD, ALU.mult, ALU.subtract, ["bt0", "bt1"], ["Y1"])
        if PCSTAGE < 3:
            P.barrier(); return
        KCsb = [b16v(nx(4096), 8, 256, p0=0, pn=32), b16v(nx(4096), 8, 256, p0=0, pn=32)]
        Y1z = V(a16, (T0.offset * 2), 2, 2, 32, 64)
        P.memset(Y1z, 0.0, ["Y1z"])
        for pl in range(2):
            P.copy(Y1z[0:64, 0, pl], Y1[0:64, pl].rearrange("p g k h -> p g (k h)"), ["Y1", "Y1z"], ["Y1zf"])
            P.copy(Y1z[64:128, 1, pl], Y1[64:128, pl].rearrange("p g k h -> p g (k h)"), ["Y1", "Y1z"], ["Y1zb"])
        GS = 4 * 64 * 4 * 16
        it = 0
        for d in range(2):
            for q in range(4):
                ks = KCsb[it % 2]
                kk = "KCsb%d" % (it % 2)
                it += 1
                for pr in range(4):
                    bank = pr
                    pk = "ps%d" % bank
                    for g2 in range(2):
                        g = q * 8 + pr * 2 + g2
                        def f(e, g=g, g2=g2, d=d, bank=bank):
                            ins = None
                            for ht in range(4):
                                oap = ps32(bank)[0:32, g2 * 256:(g2 + 1) * 256].rearrange("p (c t) -> p c t", t=4)[:, :, ht]
                                e.matmul(oap, X4[:, 0, g, ht].rearrange("p l k -> p (l k)"), Y1z[:, d, 0, g], start=True, stop=False)
                                ins = e.matmul(oap, X4[:, 1, g, ht].rearrange("p l k -> p (l k)"), Y1z[:, d, 1, g], start=False, stop=True)
                            return ins
                        P.pe(f, ["X4", "Y1zf", "Y1zb"], [pk])
                    P.act(ks[:, pr * 2:pr * 2 + 2, :], ps32(bank)[0:32, :].rearrange("p (g c) -> p g c", g=2), AF.Copy, [pk], [kk])
                m0 = 31 if d == 1 else 0
                for hl in range(4):
                    dst = AP(KD.tensor, q * 8 * GS + hl * 4096 + m0 * 64, [[256, 8], [GS, 8], [1, 256]])
                    P.dma(dst, ks[hl * 8:(hl + 1) * 8], "kd", [kk], ["KDw%d_%d_%d" % (d, q, hl)])
        for g in range(32):
            bank = 4 + g // 8
            def f(e, g=g, bank=bank):
                ins = None
                for ht in range(4):
                    oap = ps32(bank)[0:4, (g % 8) * 64:(g % 8 + 1) * 64].rearrange("p (o t) -> p o t", t=4)[:, :, ht]
                    e.matmul(oap, K0o[:, 0, g, ht * 4:(ht + 1) * 4], K0o[:, 2, g], start=True, stop=False)
                    ins = e.matmul(oap, K0o[:, 1, g, ht * 4:(ht + 1) * 4], K0o[:, 3, g], start=False, stop=True)
                return ins
            P.pe(f, ["K0o"], ["ps%d" % bank])
        K0sb = b16v(nx(4096), 32, 64, p0=0, pn=4)
        for q in range(4):
            P.tt(K0sb[:, q * 8:(q + 1) * 8, :], ps32(4 + q)[0:4, :].rearrange("p (g c) -> p g c", g=8), DB[:, q * 8:(q + 1) * 8, :], ALU.add,
                 ["ps%d" % (4 + q), "DB"], ["K0sb"])
        dst = AP(KD.tensor, 31 * 64, [[4096, 4], [GS, 32], [1, 64]])
        P.dma(dst, K0sb, "kd", ["K0sb"], ["KD"] + ["KDw%d_%d_%d" % (d_, q_, h_) for d_ in range(2) for q_ in range(4) for h_ in range(4)])
        if PCSTAGE < 4:
            P.barrier(); return
        QG = 4
        Zb = b16v(nx(8192), 2, QG, 512)
        WAsb = [b16v(nx(2048), 8, 128), b16v(nx(2048), 8, 128)]
        WCsb = b16v(nx(8192), QG, 2, 512)
        t1 = BT1[:, 0].rearrange("p (g h j) -> p g h j", g=QG, h=16, j=32)
        t2 = BT1[:, 1].rearrange("p (g h j) -> p g h j", g=QG, h=16, j=32)
        t3 = BT1[:, 2].rearrange("p (g j h) -> p g j h", g=QG, j=32, h=16)
        t4 = BT1[:, 3].rearrange("p (g j h) -> p g j h", g=QG, j=32, h=16)
        for q in range(32 // QG):
            gs = slice(q * QG, (q + 1) * QG)

            def Gj(tabl):
                return tabl[:, gs, 0:32].unsqueeze(2).to_broadcast([128, QG, 16, 32])

            def bbj(v):
                return v[:, gs, :].unsqueeze(3).to_broadcast([128, QG, 16, 32])

            Zr = Zb[:, 0].rearrange("p g (h j) -> p g h j", h=16)
            Zi = Zb[:, 1].rearrange("p g (h j) -> p g h j", h=16)
            P.tt(t1, Gj(PR), bbj(bbr), ALU.mult, ["PR", "bbr"], ["bt0"])
            P.tt(t2, Gj(PI_), bbj(bbi), ALU.mult, ["PI", "bbi"], ["bt1"])
            P.tt(Zr, t1, t2, ALU.subtract, ["bt0", "bt1"], ["Zb"])
            P.tt(t1, Gj(PR), bbj(bbi), ALU.mult, ["PR", "bbi"], ["bt0"])
            P.tt(t2, Gj(PI_), bbj(bbr), ALU.mult, ["PI", "bbr"], ["bt1"])
            P.tt(Zi, t1, t2, ALU.add, ["bt0", "bt1"], ["Zb"])
            for gl in range(QG):
                g = q * QG + gl
                bank = 5 + (g % 2)
                pk = "ps%d" % bank
                def f(e, gl=gl, bank=bank):
                    ins = None
                    for ht in range(4):
                        for pl in range(2):
                            ins = e.transpose(ps16(bank)[:, (ht * 2 + pl) * 128:(ht * 2 + pl + 1) * 128],
                                              Zb[:, pl, gl, ht * 128:(ht + 1) * 128], ident)
                    return ins
                P.pe(f, ["Zb", "ident"], [pk])
                wk = "WAsb%d" % (g % 2)
                P.act(WAsb[g % 2].rearrange("p a b -> p (a b)"), ps16(bank), AF.Copy, [pk], [wk])
                P.dma(WAs[g], WAsb[g % 2].rearrange("p a b -> p (a b)"), "wa", [wk], ["WAs"])

            def Hj(tabl):
                return tabl[:, gs, 32:64].unsqueeze(3).to_broadcast([128, QG, 32, 16])

            def cj(v):
                return v[:, gs, :].unsqueeze(2).to_broadcast([128, QG, 32, 16])

            WCr = WCsb[:, :, 0].rearrange("p g (j h) -> p g j h", j=32)
            WCn = WCsb[:, :, 1].rearrange("p g (j h) -> p g j h", j=32)
            P.tt(t3, Hj(PR), cj(CR), ALU.mult, ["PR", "BC"], ["bt2"])
            P.tt(t4, Hj(PI_), cj(CI), ALU.mult, ["PI", "BC"], ["bt3"])
            P.tt(WCr, t3, t4, ALU.subtract, ["bt2", "bt3"], ["WCsb"])
            P.tt(t3, Hj(PI_), cj(CR), ALU.mult, ["PI", "BC"], ["bt2"])
            P.tt(t4, Hj(PR), cj(CI), ALU.mult, ["PR", "BC"], ["bt3"])
            P.stt(WCn, t3, -1.0, t4, ALU.mult, ALU.subtract, ["bt2", "bt3"], ["WCsb"])
            P.dma(WCs[q * QG:(q + 1) * QG].rearrange("g p f -> p g f"), WCsb.rearrange("p g a f -> p g (a f)"), "wc", ["WCsb"], ["WCs"])
        if PCSTAGE < 5:
            P.barrier(); return
        P.barrier()
        o = 2048
        TH32 = sm[:, 6]
        tq_i = V(ai32, nx(128) // 4, 32)
        tq_f = f32v(nx(128), 32)
        P.ts(TH32, THS, 32.0, None, ALU.mult, None, ["THS"], ["TH32"])
        P.ts(tq_i, TH32, 1.0 / TWO_PI, None, ALU.mult, None, ["TH32"], ["tqi"])
        P.copy(tq_f, tq_i, ["tqi"], ["tqf"])
        P.stt(TH32, tq_f, -TWO_PI, TH32, ALU.mult, ALU.add, ["tqf", "TH32"], ["TH32"])
        NS = 32 * NCH
        S0 = f32v(nx(NS * 4), 32, NCH)
        S1 = V(ai32, nx(NS * 4) // 4, 32, NCH)
        S2 = f32v(nx(NS * 4), 32, NCH)
        S3 = f32v(nx(NS * 4), 32, NCH)
        S4 = f32v(nx(NS * 4), 32, NCH)
        assert o <= AW * 4, o
        P.tt(S0, TH32.unsqueeze(2).to_broadcast([128, 32, NCH]), CIDX.unsqueeze(1).to_broadcast([128, 32, NCH]), ALU.mult,
             ["TH32", "CST"], ["s_ang"])
        sincos(S0, S1, S2, S3, S4, "s_")
        P.ts(S3, S3, SGN, None, ALU.mult, None, ["s_sin", "CST"], ["s_sin"])
        P.dma(TABd[0].rearrange("p (g c) -> p g c", g=32), S4, "tab", ["s_cos"], ["TABd"])
        P.dma(TABd[1].rearrange("p (g c) -> p g c", g=32), S3, "tab", ["s_sin"], ["TABd"])
        R = sm[:, 7]
        P.act(R, XS, AF.Exp, ["XS"], ["R"], scale=32.0)
        P.copy(S2, R.unsqueeze(2).to_broadcast([128, 32, NCH]), ["R", "s_tf", "s_ang"], ["RZ"])
        P.memset(S2[0:64, :, 0:1], 0.0, ["RZ"])
        P.memset(S2[64:128, :, NCH - 1:NCH], 0.0, ["RZ"])
        P.dma(TABd[2].rearrange("p (g c) -> p g c", g=32), S2, "tab", ["RZ"], ["TABd"])
        P.barrier()

    O_UT = 0
    O_QT = 32 * KiB
    O_KV = 64 * KiB
    O_YB = 96 * KiB
    O_YA = 128 * KiB
    O_W = 160 * KiB
    UT = b16v(O_UT, 4, NCH, 32)
    QT = b16v(O_QT, 4, L)
    ZT = QT
    KT = [b16v(O_KV, 2, L), b16v(O_W, 2, L)]
    VA = b16v(O_KV + 2 * L * 2, NB, 2, 128)
    YTOK = b16v(O_KV, 4, 32, 8, 16)
    YB = b16v(O_YB, 4, L)
    YA = b16v(O_YA, 4, L)

    def rmsnorm_group(hT, N, gi, SQ, HN, RSTD, bank, pfx, eng2="dve"):
        pk = "ps%d" % bank
        P.act(SQ, hT, AF.Square, [pfx + "h"], [pfx + "sq"])
        def f(e):
            ins = None
            for kc in range(KC):
                ins = e.matmul(ps32(bank)[:, 0:N], ONES, SQ[:, kc, :], start=(kc == 0), stop=(kc == KC - 1))
            return ins
        P.pe(f, [pfx + "sq", "ONES"], [pk])
        P.act(RSTD, ps32(bank)[:, 0:N], AF.Sqrt, [pk], [pfx + "rstd"], bias=EPSB, scale=1.0 / D)
        P.add("dve", lambda e: e.reciprocal(out=RSTD, in_=RSTD), [pfx + "rstd"], [pfx + "rstd"])
        for kc in range(KC):
            P.stt(HN[:, kc, :], hT[:, kc, :], GN[:, gi, kc:kc + 1], RSTD, ALU.mult, ALU.mult,
                  [pfx + "h", pfx + "rstd", "GN"], [pfx + "hn"], eng=("dve" if kc % 2 == 0 else eng2))

    EPSB = V(s32, 4000, 1)
    P.ops.insert(0, dict(eng="dve", fn=lambda e: e.memset(EPSB, 1e-6), r=("PH",), w=("EPSB",), dk=None, sig=False))

    def phase_A(l):
        O_WA1 = O_YB
        WA1 = b16v(O_WA1, KC, 1408)
        wload(WA1, w_in[l], 0, 1408, "wA", "WA1")
        o = O_WA1 + 22528
        HT = [f32v(o, KC, 512), f32v(o, KC, 512)]
        o += 16 * KiB
        SQ = b16v(o, KC, 512); o += 8 * KiB
        HN = b16v(o, KC, 512); o += 8 * KiB
        RSTD = f32v(o, 512); o += 2 * KiB
        UTMP = b16v(o, 512); o += 1 * KiB
        assert o <= O_W
        P.memset(KT[0][64:128], 0.0, ["KT"])
        P.memset(KT[1][0:64], 0.0, ["KT"])
        src = xT if l == 0 else hscr
        for tg in range(NG):
            hk = "A0h"
            hT = HT[tg % 2]
            P.dma(hT, src[tg], "hA0", [], [hk])
            pf = "A%d" % (tg % 2)
            P.act(SQ, hT, AF.Square, [hk], ["Asq"])
            def f(e):
                ins = None
                for kc in range(KC):
                    ins = e.matmul(ps32(0), ONES, SQ[:, kc, :], start=(kc == 0), stop=(kc == KC - 1))
                return ins
            P.pe(f, ["Asq", "ONES"], ["ps0"])
            P.act(RSTD, ps32(0), AF.Sqrt, ["ps0"], ["Arstd"], bias=EPSB, scale=1.0 / D)
            P.add("dve", lambda e: e.reciprocal(out=RSTD, in_=RSTD), ["Arstd"], ["Arstd"])
            for kc in range(KC):
                P.stt(HN[:, kc, :], hT[:, kc, :], GN[:, l, kc:kc + 1], RSTD, ALU.mult, ALU.mult,
                      [hk, "Arstd", "GN"], ["Ahn"])
            for t in range(10):
                bank = 1 + (t % 4)
                pk = "ps%d" % bank
                def f(e, t=t, bank=bank):
                    ins = None
                    for kc in range(KC):
                        ins = e.matmul(ps32(bank), WA1[:, kc, t * 128:(t + 1) * 128], HN[:, kc, :], start=(kc == 0), stop=(kc == KC - 1))
                    return ins
                P.pe(f, ["WA1", "Ahn"], [pk])
                tok = slice(tg * 512, (tg + 1) * 512)
                if t < 4:
                    P.act(UTMP, ps32(bank), AF.Copy, [pk], ["UTMP"])
                    dst = UT[:, t, tg * 16:(tg + 1) * 16, :].rearrange("p c g -> p (c g)")
                    P.add("dve", lambda e, dst=dst: e.transpose(out=dst, in_=UTMP), ["UTMP"], ["UT"])
                elif t < 8:
                    P.act(QT[:, t - 4, tok], ps32(bank), AF.Copy, [pk], ["QT"])
                else:
                    P.act(KT[0][0:64, t - 8, tok], ps32(bank)[0:64, :], AF.Copy, [pk], ["KTa"])
                    P.act(KT[1][64:128, t - 8, tok], ps32(bank)[64:128, :], AF.Copy, [pk], ["KTb"])
            for tt_ in range(4):
                bank = 5 + (tt_ % 2)
                pk = "ps%d" % bank
                def f(e, tt_=tt_, bank=bank):
                    ins = None
                    for kc in range(KC):
                        ins = e.matmul(ps32(bank)[:, 0:128], HN[:, kc, tt_ * 128:(tt_ + 1) * 128], WA1[:, kc, 1280:1408],
                                       start=(kc == 0), stop=(kc == KC - 1))
                    return ins
                P.pe(f, ["WA1", "Ahn"], [pk])
                blk = tg * 4 + tt_
                P.copy(VA[:, blk, :, 0:64], ps32(bank)[:, 0:128].rearrange("p (k d) -> p k d", k=2), [pk], ["VA"])
        P.memset(VA[:, :, :, 64:128], 1.0, ["VA1"])
        P.barrier()

    def phase_ATT(l):
        o = O_YA
        PT = [b16v(o, 512), b16v(o + KiB, 512), b16v(o + 2 * KiB, 512)]
        o += 3 * KiB
        RD = f32v(o, 512, p0=0, pn=64); o += 2 * KiB
        pend = []

        def att_norm(n, kh, obank, ok):
            esk = ESK[64:128, l, kh * 4:(kh + 1) * 4].unsqueeze(2).to_broadcast([64, 4, 128])
            P.tt(RD.rearrange("p (h q) -> p h q", h=4), ps32(obank)[64:128, :].rearrange("p (h q) -> p h q", h=4), esk, ALU.add,
                 [ok, "ESK"], ["RD"])
            P.act(RD, RD, AF.Ln, ["RD"], ["RD"])
            P.act(RD, RD, AF.Exp, ["RD"], ["RD"], scale=-1.0)
            for par in range(2):
                num = ps32(obank)[0:64, :].rearrange("p (a b q) -> p a b q", a=2, b=2)[:, :, par, :]
                rd = RD.rearrange("p (a b q) -> p a b q", a=2, b=2)[:, :, par, :]
                dst = YB[par * 64:(par + 1) * 64, kh * 2:kh * 2 + 2, n * 128:(n + 1) * 128]
                P.tt(dst, num, rd, ALU.mult, [ok, "RD"], ["YB"])

        for n in range(NB):
            for kh in range(2):
                kbs = [kb for kb in (n - 1, n, n + 1) if 0 <= kb < NB]
                obank = 6 + (kh % 2)
                ok = "ps%d" % obank
                def emit_qk(ki, kb, n=n, kh=kh):
                    it = (n * 2 + kh) * 3 + ki
                    lb = it % 4
                    lk = "ps%d" % lb
                    rel = kb - n + 1
                    def f(e, lb=lb, rel=rel, kb=kb, kh=kh, n=n):
                        e.matmul(ps32(lb), ident, BM[:, rel, kh * 4:(kh + 1) * 4, :].rearrange("p h q -> p (h q)"), start=True, stop=False)
                        ins = None
                        for hl in range(4):
                            h = kh * 4 + hl
                            i, half = h // 2, h % 2
                            ins = e.matmul(ps32(lb)[:, hl * 128:(hl + 1) * 128],
                                           KT[half][:, kh, kb * 128:(kb + 1) * 128],
                                           QT[:, i, n * 128:(n + 1) * 128],
                                           start=False, stop=(hl == 3))
                        return ins
                    P.pe(f, ["BM", "ident", "KT", "KTa", "KTb", "QT"], [lk])
                    P.act(PT[it % 3], ps32(lb), AF.Exp, [lk], ["PT%d" % (it % 3)], scale=0.125)

                def emit_pv(ki, kb, n=n, kh=kh, obank=obank, ok=ok, nk=len(kbs)):
                    it = (n * 2 + kh) * 3 + ki
                    P.mm(ps32(obank), VA[:, kb, kh, :], PT[it % 3], ki == 0, ki == nk - 1, ["VA", "VA1", "PT%d" % (it % 3)], [ok])

                for ki, kb in enumerate(kbs):
                    emit_qk(ki, kb)
                    if ki >= 1:
                        emit_pv(ki - 1, kbs[ki - 1])
                emit_pv(len(kbs) - 1, kbs[-1])
                pend.append((n, kh, obank, ok))
                if len(pend) > 1:
                    att_norm(*pend.pop(0))
        while pend:
            att_norm(*pend.pop(0))
        P.barrier()

    def phase_S(l):
        o = O_W
        def nx(nb):
            nonlocal o
            r = o
            o += nb
            return r
        NBW = GB * NCH * 4
        COSb, SINb, RZb = f32v(nx(NBW), GB * NCH), f32v(nx(NBW), GB * NCH), f32v(nx(NBW), GB * NCH)
        Mr, Mi = f32v(nx(NBW), GB * NCH), f32v(nx(NBW), GB * NCH)
        Wr, Wi = f32v(nx(NBW), GB * NCH), f32v(nx(NBW), GB * NCH)
        assert o <= AW * 4
        oy = O_YA
        WAb = [b16v(oy, 4, 2, 128), b16v(oy + 2 * KiB, 4, 2, 128)]; oy += 4 * KiB
        TZb = [b16v(oy, 4, 512), b16v(oy + 4 * KiB, 4, 512)]; oy += 8 * KiB
        WCb = [b16v(oy, 2, 512), b16v(oy + 2 * KiB, 2, 512)]; oy += 4 * KiB
        SS = b16v(oy, 2, 32, NCH)
        oy += 2 * 32 * NCH * 2
        assert oy <= O_W
        T1 = f32v(O_QT, GB * NCH)
        T2 = f32v(O_QT + NBW, GB * NCH)
        WGL = b16v(O_YB - 0, 1) if False else None
        P.memset(SS[0:64, :, :, 0:1], 0.0, ["SS"])
        P.memset(SS[64:128, :, :, NCH - 1:NCH], 0.0, ["SS"])
        for b in range(NBATCH):
            for i, tb_ in enumerate((COSb, SINb, RZb)):
                P.dma(tb_, TABd[i][:, b * GB * NCH:(b + 1) * GB * NCH], "tabl", ["TABd"], ["tabs"])
            for gl in range(GB):
                g = b * GB + gl
                sl = g % 2
                P.dma(WAb[sl].rearrange("p a b c -> p (a b c)"), WAs[g], "wal%d" % sl, ["WAs"], ["WAb%d" % sl])
                def f(e, g=g, gl=gl, sl=sl):
                    ins = None
                    for pl in range(2):
                        for ht in range(4):
                            ins = e.matmul(ps32(pl)[:, gl * NCH:(gl + 1) * NCH], WAb[sl][:, ht, pl, :], UT[:, ht, :, g],
                                           start=(ht == 0), stop=(ht == 3))
                    return ins
                P.pe(f, ["WAb%d" % sl, "UT"], ["ps0", "ps1"])
            N = GB * NCH
            Lr, Li = ps32(0)[:, 0:N], ps32(1)[:, 0:N]
            P.tt(T1, Lr, COSb, ALU.mult, ["ps0", "tabs"], ["T1"])
            P.tt(T2, Li, SINb, ALU.mult, ["ps1", "tabs"], ["T2"])
            P.tt(Mr, T1, T2, ALU.add, ["T1", "T2"], ["Mr"])
            P.tt(T1, Li, COSb, ALU.mult, ["ps1", "tabs", "Mr"], ["T1"])
            P.tt(T2, Lr, SINb, ALU.mult, ["ps0", "tabs", "Mr"], ["T2"])
            P.tt(Mi, T1, T2, ALU.subtract, ["T1", "T2"], ["Mi"])
            for (Mx, Wx, mk, wk) in ((Mr, Wr, "Mr", "Wr"), (Mi, Wi, "Mi", "Wi")):
                P.add("dve", lambda e, Mx=Mx, Wx=Wx: e.tensor_tensor_scan(out=Wx[0:64], data0=RZb[0:64], data1=Mx[0:64], initial=0.0,
                                                                          op0=ALU.mult, op1=ALU.add), [mk, "tabs"], [wk])
                def rev(ap):
                    full = ap[64:128]
                    return AP(full.tensor, full.offset + N - 1, [list(full.ap[0]), [-1, N]])
                P.add("dve", lambda e, Mx=Mx, Wx=Wx, rev=rev: e.tensor_tensor_scan(out=rev(Wx), data0=rev(RZb), data1=rev(Mx), initial=0.0,
                                                                                   op0=ALU.mult, op1=ALU.add), [mk, "tabs"], [wk])
            def g3(ap, p0):
                return ap[p0:p0 + 64].rearrange("p (g c) -> p g c", g=GB)
            gsl = slice(b * GB, (b + 1) * GB)
            P.tt(T1, Wr, COSb, ALU.mult, ["Wr", "tabs", "Mi"], ["T1"])
            P.tt(T2, Wi, SINb, ALU.mult, ["Wi", "tabs", "Mi"], ["T2"])
            if NCH > 1:
                P.tt(SS[0:64, 0, gsl, 1:NCH], g3(T1, 0)[:, :, 0:NCH - 1], g3(T2, 0)[:, :, 0:NCH - 1], ALU.subtract, ["T1", "T2"], ["SS"])
                P.tt(SS[64:128, 0, gsl, 0:NCH - 1], g3(T1, 64)[:, :, 1:NCH], g3(T2, 64)[:, :, 1:NCH], ALU.subtract, ["T1", "T2"], ["SS"])
            P.tt(T1, Wi, COSb, ALU.mult, ["Wi", "tabs", "SS"], ["T1"])
            P.tt(T2, Wr, SINb, ALU.mult, ["Wr", "tabs", "SS"], ["T2"])
            if NCH > 1:
                P.tt(SS[0:64, 1, gsl, 1:NCH], g3(T1, 0)[:, :, 0:NCH - 1], g3(T2, 0)[:, :, 0:NCH - 1], ALU.add, ["T1", "T2"], ["SS"])
                P.tt(SS[64:128, 1, gsl, 0:NCH - 1], g3(T1, 64)[:, :, 1:NCH], g3(T2, 64)[:, :, 1:NCH], ALU.add, ["T1", "T2"], ["SS"])
        NS3, PD3 = 4, 3
        TZs = [b16v(O_QT + 8 * KiB + i * 4 * KiB, 512, 4) for i in range(NS3)]
        WCs_ = [b16v(O_QT + 24 * KiB + i * 2 * KiB, 2, 512) for i in range(NS3)]

        def s3_load(g):
            sl = g % NS3
            srcap = AP(KD.tensor, g * 16384, [[4096, 4], [64, 32], [1, 2048]])
            P.dma(TZs[sl].rearrange("p c t -> p (c t)"), srcap, "tz%d" % sl, ["KD"], ["TZb%d" % sl], eng=("sp" if g % 2 == 0 else "act"))
            P.dma(WCs_[sl].rearrange("p a f -> p (a f)"), WCs[g], "wcl%d" % sl, ["WCs"], ["WCb%d" % sl])

        for g in range(min(PD3, 32)):
            s3_load(g)
        for g in range(32):
            if g + PD3 < 32:
                s3_load(g + PD3)
            sl = g % NS3
            bank = 2 + (g % 2)
            pk = "ps%d" % bank
            def f(e, g=g, sl=sl, bank=bank):
                for ht in range(4):
                    e.matmul(ps32(bank)[0:NCH, :], UT[:, ht, :, g], TZs[sl][:, :, ht], start=(ht == 0), stop=False)
                e.matmul(ps32(bank)[0:NCH, :], SS[:, 0, g, :], WCs_[sl][:, 0, :], start=False, stop=False)
                return e.matmul(ps32(bank)[0:NCH, :], SS[:, 1, g, :], WCs_[sl][:, 1, :], start=False, stop=True)
            P.pe(f, ["UT", "TZb%d" % sl, "SS", "WCb%d" % sl], [pk])
            P.act(YTOK[0:NCH, g // 8, :, g % 8, :], ps32(bank)[0:NCH, :].rearrange("p (j h) -> p j h", j=32), AF.Copy, [pk], ["YTOK"])
        if "ytok" in dbg_slot:
            tmpf = f32v(O_QT, 4096)
            for gq in range(4):
                pass
        it = 0
        for gt in range(4):
            for jb in range(4):
                bank = 4 + (it % 2)
                pk = "ps%d" % bank
                it += 1
                def f(e, gt=gt, jb=jb, bank=bank):
                    ins = None
                    for jj in range(8):
                        jr = jb * 8 + jj
                        src = YTOK[0:NCH, gt, jr].rearrange("p g h -> p (g h)")
                        ins = e.transpose(ps16(bank)[:, jj * NCH:(jj + 1) * NCH], src, ident[0:NCH, 0:NCH])
                    return ins
                P.pe(f, ["YTOK", "ident"], [pk])
                zt = ZT[:, gt, :]
                j0 = 31 - jb * 8
                dst = AP(zt.tensor, zt.offset + j0, [list(zt.ap[0]), [-1, 8], [32, NCH]])
                P.act(dst, ps16(bank)[:, 0:8 * NCH].rearrange("p (j c) -> p j c", j=8), AF.Gelu_apprx_tanh, [pk], ["ZT"])
        P.barrier()
        wload(b16v(O_KV, KC, 2048), w_in[l], 1408, 3456, "wc1g", "WGA")
        WG = b16v(O_UT, 4, 512)
        wload(WG, w_glu[l], 0, 512, "wg", "WG")
        SG = [b16v(O_UT + 4 * KiB, 512), b16v(O_UT + 5 * KiB, 512)]
        it = 0
        for tg in range(NG):
            tok = slice(tg * 512, (tg + 1) * 512)
            for ot in range(4):
                bank = 6 + (it % 2)
                pk = "ps%d" % bank
                sg = SG[it % 2]
                sk = "SG%d" % (it % 2)
                it += 1
                def f(e, ot=ot, bank=bank, tok=tok):
                    ins = None
                    for kt in range(4):
                        ins = e.matmul(ps32(bank), WG[:, kt, ot * 128:(ot + 1) * 128], ZT[:, kt, tok], start=(kt == 0), stop=(kt == 3))
                    return ins
                P.pe(f, ["WG", "ZT"], [pk])
                P.act(sg, ps32(bank), AF.Sigmoid, [pk], [sk])
                P.tt(YA[:, ot, tok], ZT[:, ot, tok], sg, ALU.mult, ["ZT", sk], ["YA"])
        P.barrier()

    def phase_C1(l):
        o = 0
        def nx(nb):
            nonlocal o
            r = o
            o += nb
            return r
        WGA = b16v(O_KV, KC, 2048)
        WBA = b16v(nx(8 * KiB), 4, D)
        WBB = b16v(nx(8 * KiB), 4, D)
        WO = b16v(nx(16 * KiB), KC, D)
        wload(WBA, w_a[l], 0, D, "wc1a", "WBA")
        wload(WBB, w_b[l], 0, D, "wc1b", "WBB")
        wload(WO, w_out[l], 0, D, "wc1o", "WO")
        assert o == 32 * KiB
        HTs = [f32v(nx(16 * KiB), KC, 512), f32v(nx(16 * KiB), KC, 512)]
        assert o == 64 * KiB
        o = O_W
        SQ = b16v(nx(8 * KiB), KC, 512)
        HN = b16v(nx(8 * KiB), KC, 512)
        MG = SQ
        src = xT if l == 0 else hscr
        RSTD = V(s32, 2400, 512)
        TA = V(s32, 2912, 512)
        TM = V(s32, 4096, 512)
        GAs = [V(s16, 2 * 4608 + i * 512, 512) for i in range(2)]
        GBs = [V(s16, 2 * 5120 + i * 512, 512) for i in range(2)]
        P.dma(HTs[0], src[0], "hC0", ["hscr"], ["Ch0"])
        for tg in range(NG):
            tok = slice(tg * 512, (tg + 1) * 512)
            HT = HTs[tg % 2]
            ck = "Ch%d" % (tg % 2)
            if tg + 1 < NG:
                P.dma(HTs[(tg + 1) % 2], src[tg + 1], "hC%d" % ((tg + 1) % 2), ["hscr"], ["Ch%d" % ((tg + 1) % 2)])
            P.act(SQ, HT, AF.Square, [ck], ["Csq"])
            def f(e):
                ins = None
                for kc in range(KC):
                    ins = e.matmul(ps32(0), ONES, SQ[:, kc, :], start=(kc == 0), stop=(kc == KC - 1))
                return ins
            P.pe(f, ["Csq", "ONES"], ["ps0"])
            P.act(RSTD, ps32(0), AF.Sqrt, ["ps0"], ["Crstd"], bias=EPSB, scale=1.0 / D)
            P.add("dve", lambda e: e.reciprocal(out=RSTD, in_=RSTD), ["Crstd"], ["Crstd"])
            for kc in range(KC):
                P.stt(HN[:, kc, :], HT[:, kc, :], GN[:, l, kc:kc + 1], RSTD, ALU.mult, ALU.mult, [ck, "Crstd", "GN"], ["Chn"])
            for ot in range(8):
                sl = ot % 2
                ba, bb_ = 1 + sl, 3 + sl
                def f(e, ot=ot, ba=ba, bb_=bb_):
                    ins = None
                    for kc in range(KC):
                        e.matmul(ps32(ba), WGA[:, kc, ot * 128:(ot + 1) * 128], HN[:, kc, :], start=(kc == 0), stop=(kc == KC - 1))
                    for kc in range(KC):
                        ins = e.matmul(ps32(bb_), WGA[:, kc, (8 + ot) * 128:(9 + ot) * 128], HN[:, kc, :], start=(kc == 0), stop=(kc == KC - 1))
                    return ins
                P.pe(f, ["WGA", "Chn"], ["ps%d" % ba, "ps%d" % bb_])
                P.act(GAs[sl], ps32(ba), AF.Sigmoid, ["ps%d" % ba], ["GA%d" % sl])
                P.act(GBs[sl], ps32(bb_), AF.Sigmoid, ["ps%d" % bb_], ["GB%d" % sl])
                def f(e, ot=ot, tok=tok):
                    ins = None
                    for kt in range(4):
                        e.matmul(ps32(5), WBA[:, kt, ot * 128:(ot + 1) * 128], YA[:, kt, tok], start=(kt == 0), stop=(kt == 3))
                    for kt in range(4):
                        ins = e.matmul(ps32(6), WBB[:, kt, ot * 128:(ot + 1) * 128], YB[:, kt, tok], start=(kt == 0), stop=(kt == 3))
                    return ins
                P.pe(f, ["WBA", "WBB", "YA", "YB"], ["ps5", "ps6"])
                P.tt(TA, ps32(5), GAs[sl], ALU.mult, ["ps5", "GA%d" % sl], ["TA"])
                P.tt(TM, ps32(6), GBs[sl], ALU.mult, ["ps6", "GB%d" % sl], ["TM"])
                P.tt(MG[:, ot, :], TM, TA, ALU.add, ["TM", "TA"], ["Csq"], eng="pool")
            for ot in range(8):
                bank = 5 + (ot % 3)
                pk = "ps%d" % bank
                def f(e, ot=ot, bank=bank):
                    ins = None
                    for kc in range(KC):
                        ins = e.matmul(ps32(bank), WO[:, kc, ot * 128:(ot + 1) * 128], MG[:, kc, :], start=(kc == 0), stop=(kc == KC - 1))
                    return ins
                P.pe(f, ["WO", "Csq"], [pk])
                P.tt(HT[:, ot, :], HT[:, ot, :], ps32(bank), ALU.add, [pk, ck, "Chn"], [ck + "2"])
            P.dma(hscr[tg], HT, "hCs", [ck + "2", ck], ["hscr"])
        P.barrier()

    def phase_C2(l):
        o = 0
        def nx(nb):
            nonlocal o
            r = o
            o += nb
            return r
        WFG = b16v(nx(KC * DFF * 2), KC, DFF)
        WFU = b16v(nx(KC * DFF * 2), KC, DFF)
        WFD = b16v(nx(NF * D * 2), NF, D)
        HC = 11 * 128
        wload(WFG, w_fg[l], 0, HC, "wfga", "WFGa")
        wload(WFU, w_fu[l], 0, HC, "wfua", "WFUa")
        wload(WFG, w_fg[l], HC, DFF, "wfgb", "WFGb", d0=HC)
        wload(WFU, w_fu[l], HC, DFF, "wfub", "WFUb", d0=HC)
        wload(WFD, w_fd[l], 0, D, "wfd", "WFD")
        TN = 256
        HT = [f32v(nx(8 * KiB), KC, TN), f32v(nx(8 * KiB), KC, TN)]
        SQ = b16v(nx(4 * KiB), KC, TN)
        HNs = [b16v(nx(4 * KiB), KC, TN), b16v(nx(4 * KiB), KC, TN)]
        FF = b16v(nx(NF * TN * 2), NF, TN)
        SL = [b16v(nx(512), TN), b16v(nx(512), TN)]
        assert o <= AW * 4, o
        RSTD = V(s32, 2400, TN)
        last = (l == DEPTH - 1)
        NT2 = NG * 2

        def load(t2):
            tg, hf = t2 // 2, t2 % 2
            P.dma(HT[t2 % 2], hscr[tg][:, :, hf * TN:(hf + 1) * TN], "hF%d" % (t2 % 2), ["hscr"], ["F%dh" % (t2 % 2)])

        def norm(t2):
            ht = HT[t2 % 2]
            hk = "F%dh" % (t2 % 2)
            HN = HNs[t2 % 2]
            nk = "Fhn%d" % (t2 % 2)
            P.act(SQ, ht, AF.Square, [hk], ["Fsq"])
            P.pe(f_final_ss(SQ, TN), ["Fsq", "ONES"], ["ps0"])
            P.act(RSTD, ps32(0)[:, 0:TN], AF.Sqrt, ["ps0"], ["Frstd"], bias=EPSB, scale=1.0 / D)
            P.add("dve", lambda e: e.reciprocal(out=RSTD, in_=RSTD), ["Frstd"], ["Frstd"])
            for kc in range(KC):
                P.stt(HN[:, kc, :], ht[:, kc, :], GN[:, 2 + l, kc:kc + 1], RSTD, ALU.mult, ALU.mult, [hk, "Frstd", "GN"], [nk])

        load(0)
        norm(0)
        for t2 in range(NT2):
            tg, hf = t2 // 2, t2 % 2
            ht = HT[t2 % 2]
            hk = "F%dh" % (t2 % 2)
            HN = HNs[t2 % 2]
            nk = "Fhn%d" % (t2 % 2)
            if t2 + 1 < NT2:
                load(t2 + 1)
            for fi in range(NF):
                bg = 1 + (fi % 2) * 2
                bu = bg + 1
                def f(e, fi=fi, bg=bg, bu=bu, HN=HN):
                    ins = None
                    for kc in range(KC):
                        e.matmul(ps32(bg)[:, 0:TN], WFG[:, kc, fi * 128:(fi + 1) * 128], HN[:, kc, :], start=(kc == 0), stop=(kc == KC - 1))
                    for kc in range(KC):
                        ins = e.matmul(ps32(bu)[:, 0:TN], WFU[:, kc, fi * 128:(fi + 1) * 128], HN[:, kc, :], start=(kc == 0), stop=(kc == KC - 1))
                    return ins
                hs = "a" if fi < 11 else "b"
                P.pe(f, ["WFG" + hs, "WFU" + hs, nk], ["ps%d" % bg, "ps%d" % bu])
                sl = SL[fi % 2]
                sk = "SL%d" % (fi % 2)
                P.act(sl, ps32(bg)[:, 0:TN], AF.Silu, ["ps%d" % bg], [sk])
                P.tt(FF[:, fi, :], ps32(bu)[:, 0:TN], sl, ALU.mult, ["ps%d" % bu, sk], ["FF"])
            if t2 + 1 < NT2:
                norm(t2 + 1)
            for ot in range(8):
                bank = 5 + (ot % 3)
                pk = "ps%d" % bank
                def f(e, ot=ot, bank=bank):
                    ins = None
                    for fi in range(NF):
                        ins = e.matmul(ps32(bank)[:, 0:TN], WFD[:, fi, ot * 128:(ot + 1) * 128], FF[:, fi, :], start=(fi == 0), stop=(fi == NF - 1))
                    return ins
                P.pe(f, ["WFD", "FF"], [pk])
                P.tt(ht[:, ot, :], ht[:, ot, :], ps32(bank)[:, 0:TN], ALU.add, [pk, hk, nk], [hk + "2"])
            if not last:
                P.dma(hscr[tg][:, :, hf * TN:(hf + 1) * TN], ht, "hFs", [hk + "2", hk], ["hscr"])
            else:
                P.act(SQ, ht, AF.Square, [hk + "2", hk], ["Fsq"])
                P.pe(f_final_ss(SQ, TN), ["Fsq", "ONES"], ["ps0"])
                P.act(RSTD, ps32(0)[:, 0:TN], AF.Sqrt, ["ps0"], ["Frstd"], bias=EPSB, scale=1.0 / D)
                P.add("dve", lambda e: e.reciprocal(out=RSTD, in_=RSTD), ["Frstd"], ["Frstd"])
                for kc in range(KC):
                    P.stt(ht[:, kc, :], ht[:, kc, :], GN[:, 4, kc:kc + 1], RSTD, ALU.mult, ALU.mult, [hk + "2", hk, "Frstd", "GN"], [hk + "3"])
                P.dma(out[t2], ht, "outd", [hk + "3", hk, hk + "2"], ["outd"])
        P.barrier()

    def f_final_ss(SQ, TN):
        def f(e):
            ins = None
            for kc in range(KC):
                ins = e.matmul(ps32(0)[:, 0:TN], ONES, SQ[:, kc, :], start=(kc == 0), stop=(kc == KC - 1))
            return ins
        return f

    step = 0
    for l in range(DEPTH):
        for ph in (s5_precompute, phase_A, phase_ATT, phase_S, phase_C1, phase_C2):
            step += 1
            if step <= upto:
                ph(l)
    if upto < 12:
        for t2 in range(NG * 2):
            P.dma(out[t2], f32v(0, KC, 256), "outd", [], ["outd"])

    P.emit(final_keys=["outd_sp"] + (["dbg_sp"] if dbg_items else []))
    es.close()
    return nc


def _t5_bucket_np(rel):
    import jax.numpy as jnp
    NUM_BUCKETS, MAX_DISTANCE = 32, 128
    rel = jnp.asarray(rel, dtype=jnp.int32)
    half = NUM_BUCKETS // 2
    max_exact = half // 2
    ret = jnp.where(rel > 0, half, 0)
    n = jnp.abs(rel)
    nf = jnp.maximum(n, 1).astype(jnp.float32)
    large = max_exact + (jnp.log(nf / max_exact) / math.log(MAX_DISTANCE / max_exact) * (half - max_exact)).astype(jnp.int32)
    large = jnp.minimum(large, half - 1)
    return np.asarray(ret + jnp.where(n < max_exact, n, large))


def prepare_shared(inp, L):
    f = np.float32
    NCH = L // 32
    sh = {}
    g = np.stack([inp["norm1_g"][0], inp["norm1_g"][1], inp["norm2_g"][0], inp["norm2_g"][1], inp["final_g"]], 0)
    sh["gains"] = np.ascontiguousarray(g.reshape(5, KC, 128).transpose(2, 0, 1)).astype(f)
    w = inp["w_in"]
    u_perm = np.array([(c % 32) * 16 + (c // 32) for c in range(512)])
    cols = [w[:, :, u_perm], w[:, :, 512:1024],
            w[:, :, 1024:1088], w[:, :, 1024:1088], w[:, :, 1088:1152], w[:, :, 1088:1152],
            w[:, :, 1152:1280], w[:, :, 1280:3328]]
    sh["w_in"] = np.ascontiguousarray(np.concatenate(cols, axis=2)).astype(f)
    lam_re, lam_im, log_dt = inp["s5_lambda_re"], inp["s5_lambda_im"], inp["s5_log_dt"]
    s5a = np.zeros((DEPTH, 128, 3, 32), f)
    for l in range(DEPTH):
        s5a[l, :, 0, :] = lam_re[l].transpose(0, 2, 1).reshape(128, 32)
        s5a[l, :, 1, :] = lam_im[l].transpose(0, 2, 1).reshape(128, 32)
        s5a[l, :, 2, :] = np.repeat(log_dt[l][:, None, :], 64, axis=1).reshape(128, 32)
    sh["s5a"] = s5a
    s5b = np.zeros((DEPTH, 128, 4, 32, 16), f)
    for l in range(DEPTH):
        s5b[l, :, 0] = inp["s5_b_re"][l].transpose(0, 2, 1, 3).reshape(128, 32, 16)
        s5b[l, :, 1] = inp["s5_b_im"][l].transpose(0, 2, 1, 3).reshape(128, 32, 16)
        s5b[l, :, 2] = inp["s5_c_re"][l].transpose(0, 3, 1, 2).reshape(128, 32, 16)
        s5b[l, :, 3] = inp["s5_c_im"][l].transpose(0, 3, 1, 2).reshape(128, 32, 16)
    sh["s5b"] = s5b
    s5d = np.zeros((DEPTH, 4, 32, 16, 4), f)
    dd = inp["s5_d"].reshape(DEPTH, 32, 16)
    for ho in range(16):
        s5d[:, ho % 4, :, ho, ho // 4] = dd[:, :, ho]
    sh["s5d"] = s5d.reshape(DEPTH, 4, 32, 64)
    sh["w_glu"] = inp["s5_w_glu"].astype(f)
    sh["w_a"] = inp["w_branch_a"].astype(f)
    sh["w_b"] = inp["w_branch_b"].astype(f)
    sh["w_out"] = inp["w_out"].astype(f)
    sh["w_fg"] = inp["ffn_w_gate"].astype(f)
    sh["w_fu"] = inp["ffn_w_up"].astype(f)
    sh["w_fd"] = inp["ffn_w_down"].astype(f)
    sh["sinkb"] = np.ascontiguousarray(np.broadcast_to(inp["attn_sink"][None], (128, DEPTH, 8))).astype(f)
    s_ = np.arange(128)[:, None, None]
    r_ = np.arange(3)[None, :, None]
    q_ = np.arange(128)[None, None, :]
    rel = (r_ - 1) * 128 + s_ - q_
    bucket = _t5_bucket_np(rel)
    bg = inp["rel_bias"][bucket]
    sh["biasg"] = np.ascontiguousarray(bg.transpose(0, 1, 3, 2)).astype(f)
    sh["maskc"] = np.where(np.abs(rel) <= 128, 0.0, -30000.0).astype(f)
    cst = np.zeros((128, NTK + 128 + 1 + 16), f)
    k = np.arange(32)
    for d in range(2):
        rows = slice(64 * d, 64 * d + 64)
        cst[rows, 0:32] = (31 - k) if d == 0 else k
        cst[rows, 32:64] = (32 - k) if d == 0 else (k + 1)
        k8 = np.arange(8)
        cst[rows, 64:72] = 4 * (7 - k8) if d == 0 else 4 * k8
        k4 = np.arange(4)
        cst[rows, 72:76] = (3 - k4) if d == 0 else k4
        cst[rows, 76] = 1.0
        cst[rows, NTK + 128] = 1.0 if d == 0 else -1.0
    cst[:, NTK:NTK + 128] = np.arange(128)[None, :]
    cst[0:16, NTK + 129:NTK + 145] = np.eye(16)
    sh["cst"] = cst
    import ml_dtypes
    sh["identd"] = np.eye(128, dtype=f)
    return sh


_NC_CACHE = {}


def kernel(**inputs):
    inp = {k: np.asarray(v) for k, v in inputs.items()}
    x = inp["x"].astype(np.float32)
    B, L, _ = x.shape
    NG = L // 512
    if L not in _NC_CACHE:
        import os
        _NC_CACHE[L] = build(L, upto=int(os.environ.get("K_UPTO", "99")))
    nc = _NC_CACHE[L]
    sh = prepare_shared(inp, L)
    in_maps = []
    for b in range(B):
        m = dict(sh)
        m["xT"] = np.ascontiguousarray(x[b].reshape(NG, 512, KC, 128).transpose(0, 3, 2, 1))
        in_maps.append(m)
    res = run_bass_kernel_spmd(nc, in_maps, core_ids=list(range(B)))
    outs = []
    for b in range(B):
        o = res.results[b]["out"]
        outs.append(o.transpose(0, 3, 2, 1).reshape(L, D))
    return np.stack(outs, 0).astype(np.float32)
```

```python
import math
import bisect
import numpy as np
import concourse.bass as bass
import concourse.mybir as mybir
from concourse.bass_utils import run_bass_kernel_spmd
from concourse.ap import AP

F32 = mybir.dt.float32
BF16 = mybir.dt.bfloat16
I32 = mybir.dt.int32
ALU = mybir.AluOpType
AF = mybir.ActivationFunctionType

D = 1024
KC = 8
DFF = 2816
NF = 22
DEPTH = 2
TWO_PI = 2.0 * math.pi
PI_LO = 3.1415925
NTK = 77


class Prog:
    ENGS = ["pe", "act", "dve", "pool", "sp"]

    def __init__(self, nc):
        self.nc = nc
        self.ops = []

    def add(self, eng, fn, r=(), w=(), dk=None, nobar=False):
        self.ops.append(dict(eng=eng, fn=fn, r=tuple(r) + (() if nobar else ("PH",)), w=tuple(w), dk=dk, sig=False))

    def barrier(self):
        self.ops.append(dict(eng="dve", fn=("bar",), r=(), w=("PH",), dk=None, sig=False))

    def mm(self, out, lhsT, rhs, start, stop, r, w):
        self.add("pe", lambda e: e.matmul(out, lhsT, rhs, start=start, stop=stop), r, w)

    def pe(self, fn, r, w):
        self.add("pe", fn, r, w)

    def act(self, out, in_, func, r, w, bias=None, scale=None):
        kw = {}
        if bias is not None:
            kw["bias"] = bias
        if scale is not None:
            kw["scale"] = scale
        self.add("act", lambda e: e.activation(out=out, in_=in_, func=func, **kw), r, w)

    def tt(self, out, in0, in1, op, r, w, eng="dve"):
        self.add(eng, lambda e: e.tensor_tensor(out=out, in0=in0, in1=in1, op=op), r, w)

    def ts(self, out, in0, s1, s2, op0, op1, r, w, eng="dve"):
        if op1 is None:
            self.add(eng, lambda e: e.tensor_scalar(out=out, in0=in0, scalar1=s1, scalar2=None, op0=op0), r, w)
        else:
            self.add(eng, lambda e: e.tensor_scalar(out=out, in0=in0, scalar1=s1, scalar2=s2, op0=op0, op1=op1), r, w)

    def stt(self, out, in0, scalar, in1, op0, op1, r, w, eng="dve"):
        self.add(eng, lambda e: e.scalar_tensor_tensor(out=out, in0=in0, scalar=scalar, in1=in1, op0=op0, op1=op1), r, w)

    def copy(self, out, in_, r, w, eng="dve"):
        self.add(eng, lambda e: e.tensor_copy(out=out, in_=in_), r, w)

    def memset(self, ap, val, w, eng="dve"):
        self.add(eng, lambda e: e.memset(ap, val), (), w)

    def dma(self, out, in_, key, r, w, eng="sp", nobar=False):
        self.add(eng, lambda e: e.dma_start(out=out, in_=in_), r, w, dk=key + "_" + eng, nobar=nobar)

    def emit(self, final_keys):
        nc = self.nc
        ops = self.ops
        last_w = {}
        readers = {}
        for i, op in enumerate(ops):
            deps = set()
            for k in op["r"]:
                if k in last_w:
                    deps.add(last_w[k])
            for k in op["w"]:
                if k in last_w:
                    deps.add(last_w[k])
                deps.update(readers.get(k, ()))
            deps.discard(i)
            op["deps"] = deps
            for k in op["r"]:
                readers.setdefault(k, []).append(i)
            for k in op["w"]:
                last_w[k] = i
                readers[k] = []
        for i, op in enumerate(ops):
            for d in op["deps"]:
                src = ops[d]
                if src["dk"] is None:
                    if src["eng"] == "pe" and op["eng"] == "pe":
                        continue
                    src["sig"] = True
        cnt = {e: 0 for e in self.ENGS}
        dma_hist = {}
        for i, op in enumerate(ops):
            if op["dk"] is not None:
                h = dma_hist.setdefault(op["dk"], [[], []])
                prev = h[1][-1] if h[1] else 0
                h[0].append(i)
                h[1].append(prev + 16)
            elif op["sig"]:
                cnt[op["eng"]] += 1
                op["cnt"] = cnt[op["eng"]]
        waited = {e: {} for e in self.ENGS}
        for i, op in enumerate(ops):
            need = {}
            for d in op["deps"]:
                src = ops[d]
                if src["dk"] is not None:
                    h = dma_hist[src["dk"]]
                    j = bisect.bisect_left(h[0], i) - 1
                    sem = ("dma", src["dk"])
                    val = h[1][j]
                else:
                    if src["eng"] == "pe" and op["eng"] == "pe":
                        continue
                    sem = ("eng", src["eng"])
                    val = src["cnt"]
                need[sem] = max(need.get(sem, 0), val)
            wl = []
            wd = waited[op["eng"]]
            for sem, val in need.items():
                if wd.get(sem, 0) < val:
                    wd[sem] = val
                    wl.append((sem, val))
            op["waits"] = wl
        sem_names = [("eng", e) for e in self.ENGS] + [("dma", k) for k in dma_hist]
        self.n_sems = len(sem_names)
        from contextlib import ExitStack
        with ExitStack() as es:
            semh = {}
            for sn in sem_names:
                semh[sn] = es.enter_context(nc.semaphore("s_%s_%s" % sn))
            block = es.enter_context(nc.Block())

            def run_engine(engname):
                def body(e):
                    for op in ops:
                        if op["eng"] != engname:
                            continue
                        for sem, val in op["waits"]:
                            e.wait_ge(semh[sem], val)
                        if op["fn"] == ("bar",):
                            ins = e.engine_nop() if False else e.memset(self.bar_ap, 0.0)
                        else:
                            ins = op["fn"](e)
                        if op["dk"] is not None:
                            ins.then_inc(semh[("dma", op["dk"])], 16)
                        elif op["sig"]:
                            ins.then_inc(semh[("eng", engname)], 1)
                    if engname == "sp":
                        for k in final_keys:
                            e.wait_ge(semh[("dma", k)], dma_hist[k][1][-1])
                return body

            block.tensor(run_engine("pe"))
            block.scalar(run_engine("act"))
            block.vector(run_engine("dve"))
            block.gpsimd(run_engine("pool"))
            block.sync(run_engine("sp"))


def build(L, dbg_items=(), upto=99):
    NG = L // 512
    NCH = L // 32
    NB = L // 128
    GB = 4
    NBATCH = 32 // GB
    nc = bass.Bass("TRN2", target_bir_lowering=False)

    def din(name, shape, dt=F32):
        return nc.dram_tensor(name, list(shape), dt, kind="ExternalInput").ap()

    xT = din("xT", [NG, 128, KC, 512])
    gains = din("gains", [128, 5, KC])
    w_in = din("w_in", [DEPTH, D, 3456])
    s5a = din("s5a", [DEPTH, 128, 3, 32])
    s5b = din("s5b", [DEPTH, 128, 4, 32, 16])
    s5d = din("s5d", [DEPTH, 4, 32, 64])
    w_glu = din("w_glu", [DEPTH, 512, 512])
    w_a = din("w_a", [DEPTH, 512, D])
    w_b = din("w_b", [DEPTH, 512, D])
    w_out = din("w_out", [DEPTH, D, D])
    w_fg = din("w_fg", [DEPTH, D, DFF])
    w_fu = din("w_fu", [DEPTH, D, DFF])
    w_fd = din("w_fd", [DEPTH, DFF, D])
    sinkb = din("sinkb", [128, DEPTH, 8])
    biasg = din("biasg", [128, 3, 8, 128])
    maskc = din("maskc", [128, 3, 128])
    cst = din("cst", [128, NTK + 128 + 1 + 16])
    identd = din("identd", [128, 128])
    out = nc.dram_tensor("out", [NG * 2, 128, KC, 256], F32, kind="ExternalOutput").ap()
    dbg = None
    if dbg_items:
        dbg = nc.dram_tensor("dbg", [len(dbg_items), 128, 4096], F32, kind="ExternalOutput").ap()

    hscr = nc.dram_tensor("hscr", [NG, 128, KC, 512], F32).ap()
    KD = nc.dram_tensor("KD", [32, 4, 64, 16, 4], BF16).ap()
    WAs = nc.dram_tensor("WAs", [32, 128, 1024], BF16).ap()
    WCs = nc.dram_tensor("WCs", [32, 128, 1024], BF16).ap()
    TABd = nc.dram_tensor("TABd", [3, 128, 32 * NCH], F32).ap()
    wfg_b = nc.dram_tensor("wfg_b", [D, DFF], BF16).ap()
    wfu_b = nc.dram_tensor("wfu_b", [D, DFF], BF16).ap()
    wfd_b = nc.dram_tensor("wfd_b", [DFF, D], BF16).ap()

    P = Prog(nc)
    from contextlib import ExitStack
    es = ExitStack()
    AW = 45056
    arena = es.enter_context(nc.sbuf_tensor("arena", [128, AW], F32))
    small = es.enter_context(nc.sbuf_tensor("small", [128, 6144], F32))
    pst = [es.enter_context(nc.psum_tensor("ps%d" % i, [128, 512], F32)) for i in range(8)]
    a32 = arena[:]
    a16 = arena[:].bitcast(BF16)
    ai32 = arena[:].bitcast(I32)
    s32 = small[:]
    s16 = small[:].bitcast(BF16)
    P.bar_ap = s32[0:1, 4095:4096]

    def V(base, off, *shape, p0=0, pn=128):
        n = int(np.prod(shape))
        v = base[p0:p0 + pn, off:off + n]
        if len(shape) == 1:
            return v
        names = "abcde"[:len(shape)]
        kw = {names[i]: shape[i] for i in range(len(shape))}
        return v.rearrange("p (%s) -> p %s" % (" ".join(names), " ".join(names)), **kw)

    def f32v(offb, *shape, **kw):
        return V(a32, offb // 4, *shape, **kw)

    def b16v(offb, *shape, **kw):
        return V(a16, offb // 2, *shape, **kw)

    def ps32(i):
        return pst[i][:]

    def ps16(i):
        return pst[i][:].bitcast(BF16)

    KiB = 1024
    ident = V(s16, 0, 128)
    GN = V(s32, 64, 5, KC)
    ESK = V(s32, 104, DEPTH, 8)
    CST = V(s32, 120, NTK + 128 + 1 + 16)
    NTAB = CST[:, 0:NTK]
    CIDX = CST[:, NTK:NTK + NCH]
    SGN = CST[:, NTK + 128:NTK + 129]
    EYE = CST[0:16, NTK + 129:NTK + 145]
    o = 120 + NTK + 145
    o = (o + 1) // 2 * 2
    BM = V(s16, o * 2, 3, 8, 128)
    o += 1536
    ONES = V(s16, o * 2, 128)
    o += 64
    THS = V(s32, o, 32); o += 32
    XS = V(s32, o, 32); o += 32
    assert o < 2400

    P.dma(ident, identd, "cid", [], ["ident"], eng="pool")
    P.dma(GN, gains, "cgn", [], ["GN"])
    P.dma(CST, cst, "ccst", [], ["CST"])
    P.memset(ONES, 1.0, ["ONES"])
    tb = f32v(0, 3, 8, 128)
    tm = f32v(16 * KiB, 3, 128)
    P.dma(tb, biasg, "ctb", [], ["tb"])
    P.dma(tm, maskc, "ctm", [], ["tm"])
    P.tt(tb, tb, tm.unsqueeze(2).to_broadcast([128, 3, 8, 128]), ALU.add, ["tb", "tm"], ["tb"])
    P.ts(BM, tb, 8.0, None, ALU.mult, None, ["tb"], ["BM"])
    tsk = f32v(20 * KiB, DEPTH, 8)
    P.dma(tsk, sinkb, "ctsk", [], ["tsk"])
    P.act(ESK, tsk, AF.Exp, ["tsk"], ["ESK"])
    P.barrier()

    dbg_slot = {name: i for i, name in enumerate(dbg_items)}

    def wload(dst, srcw, c0, c1, key, wkey, d0=0, extra_w=()):
        nk = dst.shape[1]
        for kc in range(nk):
            P.dma(dst[:, kc, d0:d0 + (c1 - c0)], srcw[kc * 128:(kc + 1) * 128, c0:c1], key, [], [wkey] + list(extra_w), eng="pool")

    def dump(name, ap, keys, n):
        if name in dbg_slot:
            i = dbg_slot[name]
            t = f32v(170 * KiB, n) if False else None
            P.dma(dbg[i][0:ap.shape[0], 0:n], ap, "dbg", keys, ["dbgout"])

    import os
    PCSTAGE = int(os.environ.get("K_PCSTAGE", "9"))

    def s5_precompute(l):
        o = 0
        def nx(nbytes):
            nonlocal o
            r = o
            o += (nbytes + 63) // 64 * 64
            return r
        sm = f32v(nx(128 * 10), 10, 32)
        LAM = f32v(nx(384), 3, 32)
        BC = f32v(nx(8192), 4, 32, 16)
        DB = f32v(nx(8192), 32, 64, p0=0, pn=4)
        P.dma(LAM, s5a[l], "pcl", [], ["LAM"])
        P.dma(BC, s5b[l], "pcb", [], ["BC"])
        P.dma(DB, s5d[l], "pcd", [], ["DB"])
        LR, LI, LDT = LAM[:, 0, :], LAM[:, 1, :], LAM[:, 2, :]
        BR, BI, CR, CI = BC[:, 0], BC[:, 1], BC[:, 2], BC[:, 3]
        DT = f32v(nx(128), 32)
        P.act(DT, LDT, AF.Exp, ["LAM"], ["DT"])
        P.tt(XS, LR, DT, ALU.mult, ["LAM", "DT"], ["XS"])
        P.tt(THS, LI, DT, ALU.mult, ["LAM", "DT"], ["THS"])
        NW = 32 * NTK
        T0 = f32v(nx(NW * 4), 32, NTK)
        T1 = V(ai32, nx(NW * 4) // 4, 32, NTK)
        T2 = f32v(nx(NW * 4), 32, NTK)
        T3 = f32v(nx(NW * 4), 32, NTK)
        T4 = f32v(nx(NW * 4), 32, NTK)

        def bc_g(tab, K):
            return tab.unsqueeze(1).to_broadcast([128, 32, K])

        def bc_k(v, K):
            return v.unsqueeze(2).to_broadcast([128, 32, K])

        def sincos(ANG, TI, TF, SINO, COSO, pre):
            k = [pre + x for x in ("ang", "ti", "tf", "sin", "cos")]
            P.ts(TI, ANG, 1.0 / TWO_PI, None, ALU.mult, None, [k[0]], [k[1]])
            P.copy(TF, TI, [k[1]], [k[2]])
            P.stt(ANG, TF, -TWO_PI, ANG, ALU.mult, ALU.add, [k[2], k[0]], [k[0]])
            P.ts(ANG, ANG, PI_LO, -PI_LO, ALU.min, ALU.max, [k[0]], [k[0]])
            P.act(SINO, ANG, AF.Sin, [k[0]], [k[3]])
            P.ts(TF, ANG, math.pi / 2, None, ALU.is_gt, None, [k[0]], [k[2]])
            P.stt(ANG, TF, -TWO_PI, ANG, ALU.mult, ALU.add, [k[2], k[0], k[3]], [k[0]])
            P.ts(ANG, ANG, math.pi / 2, PI_LO, ALU.add, ALU.min, [k[0]], [k[0]])
            P.act(COSO, ANG, AF.Sin, [k[0]], [k[4]])

        P.tt(T0, bc_k(THS, NTK), bc_g(NTAB, NTK), ALU.mult, ["THS", "CST"], ["p_ang"])
        sincos(T0, T1, T2, T3, T4, "p_")
        P.tt(T0, bc_k(XS, NTK), bc_g(NTAB, NTK), ALU.mult, ["XS", "CST", "p_cos"], ["p_ang"])
        P.act(T2, T0, AF.Exp, ["p_ang"], ["p_tf"])
        P.tt(T4, T2, T4, ALU.mult, ["p_tf", "p_cos"], ["PR"])
        P.tt(T3, T2, T3, ALU.mult, ["p_tf", "p_sin"], ["PI"])
        PR, PI_ = T4, T3
        if PCSTAGE < 2:
            P.barrier(); return
        nr, den, u1, u2, cre, cim = sm[:, 0], sm[:, 1], sm[:, 2], sm[:, 3], sm[:, 4], sm[:, 5]
        abr, abi = PR[:, :, 76], PI_[:, :, 76]
        P.ts(nr, abr, -1.0, None, ALU.add, None, ["PR"], ["nr"])
        P.tt(u1, LR, LR, ALU.mult, ["LAM"], ["u1"])
        P.tt(u2, LI, LI, ALU.mult, ["LAM"], ["u2"])
        P.tt(den, u1, u2, ALU.add, ["u1", "u2"], ["den"])
        P.add("dve", lambda e: e.reciprocal(out=den, in_=den), ["den"], ["den"])
        P.tt(u1, nr, LR, ALU.mult, ["nr", "LAM", "den"], ["u1"])
        P.tt(u2, abi, LI, ALU.mult, ["PI", "LAM", "den"], ["u2"])
        P.tt(cre, u1, u2, ALU.add, ["u1", "u2"], ["cre"])
        P.tt(cre, cre, den, ALU.mult, ["cre", "den"], ["cre"])
        P.tt(u1, abi, LR, ALU.mult, ["PI", "LAM", "cre"], ["u1"])
        P.tt(u2, nr, LI, ALU.mult, ["nr", "LAM", "cre"], ["u2"])
        P.tt(cim, u1, u2, ALU.subtract, ["u1", "u2"], ["cim"])
        P.tt(cim, cim, den, ALU.mult, ["cim", "den"], ["cim"])
        BB = f32v(nx(4096), 2, 32, 16)
        bbr, bbi = BB[:, 0], BB[:, 1]
        W1 = f32v(nx(2048), 32, 16)
        W2 = f32v(nx(2048), 32, 16)

        def bc_h(v):
            return v.unsqueeze(2).to_broadcast([128, 32, 16])

        P.tt(W1, bc_h(cre), BR, ALU.mult, ["cre", "BC"], ["W1"])
        P.tt(W2, bc_h(cim), BI, ALU.mult, ["cim", "BC"], ["W2"])
        P.tt(bbr, W1, W2, ALU.subtract, ["W1", "W2"], ["bbr"])
        P.tt(W1, bc_h(cre), BI, ALU.mult, ["cre", "BC", "bbr"], ["W1"])
        P.tt(W2, bc_h(cim), BR, ALU.mult, ["cim", "BC", "bbr"], ["W2"])
        P.tt(bbi, W1, W2, ALU.add, ["W1", "W2"], ["bbi"])
        K0o = b16v(nx(4096), 4, 32, 16)
        P.copy(K0o[:, 0], bbr, ["bbr"], ["K0o"])
        P.copy(K0o[:, 1], bbi, ["bbi"], ["K0o"])
        P.copy(K0o[:, 2], CR, ["BC"], ["K0o"])
        P.ts(K0o[:, 3], CI, -1.0, None, ALU.mult, None, ["BC"], ["K0o"])
        X4 = b16v(nx(16384), 2, 32, 4, 4, 8)
        Y1 = b16v(nx(8192), 2, 32, 4, 16)
        BT1 = f32v(nx(32768), 4, 2048)
        tA = BT1[:, 0:2].rearrange("p a (g k h) -> p (a g) k h", g=16, k=8, h=16)
        tB = BT1[:, 2:4].rearrange("p a (g k h) -> p (a g) k h", g=16, k=8, h=16)
        kA, kB = ["bt0", "bt1"], ["bt2", "bt3"]

        def g4(tabl):
            return tabl[:, :, 64:72].unsqueeze(3).to_broadcast([128, 32, 8, 16])

        def bb8(v):
            return v.unsqueeze(2).to_broadcast([128, 32, 8, 16])

        P.tt(tA, g4(PR), bb8(bbr), ALU.mult, ["PR", "bbr"], kA)
        P.tt(tB, g4(PI_), bb8(bbi), ALU.mult, ["PI", "bbi"], kB)
        for ht in range(4):
            P.tt(X4[:, 0, :, ht].rearrange("p g l k -> p g k l"), tA[:, :, :, ht * 4:(ht + 1) * 4], tB[:, :, :, ht * 4:(ht + 1) * 4], ALU.subtract, kA + kB, ["X4"])
        P.tt(tA, g4(PR), bb8(bbi), ALU.mult, ["PR", "bbi"], kA)
        P.tt(tB, g4(PI_), bb8(bbr), ALU.mult, ["PI", "bbr"], kB)
        for ht in range(4):
            P.tt(X4[:, 1, :, ht].rearrange("p g l k -> p g k l"), tA[:, :, :, ht * 4:(ht + 1) * 4], tB[:, :, :, ht * 4:(ht + 1) * 4], ALU.add, kA + kB, ["X4"])
        tC = BT1[:, 0:1].rearrange("p a (g k h) -> p (a g) k h", g=32, k=4, h=16)
        tD = BT1[:, 1:2].rearrange("p a (g k h) -> p (a g) k h", g=32, k=4, h=16)

        def g1(tabl):
            return tabl[:, :, 72:76].unsqueeze(3).to_broadcast([128, 32, 4, 16])

        def c4(v):
            return v.unsqueeze(2).to_broadcast([128, 32, 4, 16])

        P.tt(tC, g1(PR), c4(CR), ALU.mult, ["PR", "BC"], ["bt0"])
        P.tt(tD, g1(PI_), c4(CI), ALU.mult, ["PI", "BC"], ["bt1"])
        P.tt(Y1[:, 0], tC, tD, ALU.subtract, ["bt0", "bt1"], ["Y1"])
        P.tt(tC, g1(PI_), c4(CR), ALU.mult, ["PI", "BC"], ["bt0"])
        P.tt(tD, g1(PR), c4(CI), ALU.mult, ["PR", "BC"], ["bt1"])
        P.stt(Y1[:, 1], tC, -1.0, tD, ALU.mult, ALU.subtract, ["bt0", "bt1"], ["Y1"])
        if PCSTAGE < 3:
            P.barrier(); return
        KCsb = [b16v(nx(4096), 8, 256, p0=0, pn=32), b16v(nx(4096), 8, 256, p0=0, pn=32)]
        Y1z = V(a16, (T0.offset * 2), 2, 2, 32, 64)
        P.memset(Y1z, 0.0, ["Y1z"])
        for pl in range(2):
            P.copy(Y1z[0:64, 0, pl], Y1[0:64, pl].rearrange("p g k h -> p g (k h)"), ["Y1", "Y1z"], ["Y1zf"])
            P.copy(Y1z[64:128, 1, pl], Y1[64:128, pl].rearrange("p g k h -> p g (k h)"), ["Y1", "Y1z"], ["Y1zb"])
        GS = 4 * 64 * 4 * 16
        it = 0
        for d in range(2):
            for q in range(4):
                ks = KCsb[it % 2]
                kk = "KCsb%d" % (it % 2)
                it += 1
                for pr in range(4):
                    bank = pr
                    pk = "ps%d" % bank
                    for g2 in range(2):
                        g = q * 8 + pr * 2 + g2
                        def f(e, g=g, g2=g2, d=d, bank=bank):
                            ins = None
                            for ht in range(4):
                                oap = ps32(bank)[0:32, g2 * 256:(g2 + 1) * 256].rearrange("p (c t) -> p c t", t=4)[:, :, ht]
                                e.matmul(oap, X4[:, 0, g, ht].rearrange("p l k -> p (l k)"), Y1z[:, d, 0, g], start=True, stop=False)
                                ins = e.matmul(oap, X4[:, 1, g, ht].rearrange("p l k -> p (l k)"), Y1z[:, d, 1, g], start=False, stop=True)
                            return ins
                        P.pe(f, ["X4", "Y1zf", "Y1zb"], [pk])
                    P.act(ks[:, pr * 2:pr * 2 + 2, :], ps32(bank)[0:32, :].rearrange("p (g c) -> p g c", g=2), AF.Copy, [pk], [kk])
                m0 = 31 if d == 1 else 0
                for hl in range(4):
                    dst = AP(KD.tensor, q * 8 * GS + hl * 4096 + m0 * 64, [[256, 8], [GS, 8], [1, 256]])
                    P.dma(dst, ks[hl * 8:(hl + 1) * 8], "kd", [kk], ["KDw%d_%d_%d" % (d, q, hl)])
        for g in range(32):
            bank = 4 + g // 8
            def f(e, g=g, bank=bank):
                ins = None
                for ht in range(4):
                    oap = ps32(bank)[0:4, (g % 8) * 64:(g % 8 + 1) * 64].rearrange("p (o t) -> p o t", t=4)[:, :, ht]
                    e.matmul(oap, K0o[:, 0, g, ht * 4:(ht + 1) * 4], K0o[:, 2, g], start=True, stop=False)
                    ins = e.matmul(oap, K0o[:, 1, g, ht * 4:(ht + 1) * 4], K0o[:, 3, g], start=False, stop=True)
                return ins
            P.pe(f, ["K0o"], ["ps%d" % bank])
        K0sb = b16v(nx(4096), 32, 64, p0=0, pn=4)
        for q in range(4):
            P.tt(K0sb[:, q * 8:(q + 1) * 8, :], ps32(4 + q)[0:4, :].rearrange("p (g c) -> p g c", g=8), DB[:, q * 8:(q + 1) * 8, :], ALU.add,
                 ["ps%d" % (4 + q), "DB"], ["K0sb"])
        dst = AP(KD.tensor, 31 * 64, [[4096, 4], [GS, 32], [1, 64]])
        P.dma(dst, K0sb, "kd", ["K0sb"], ["KD"] + ["KDw%d_%d_%d" % (d_, q_, h_) for d_ in range(2) for q_ in range(4) for h_ in range(4)])
        if PCSTAGE < 4:
            P.barrier(); return
        QG = 4
        Zb = b16v(nx(8192), 2, QG, 512)
        WAsb = [b16v(nx(2048), 8, 128), b16v(nx(2048), 8, 128)]
        WCsb = b16v(nx(8192), QG, 2, 512)
        t1 = BT1[:, 0].rearrange("p (g h j) -> p g h j", g=QG, h=16, j=32)
        t2 = BT1[:, 1].rearrange("p (g h j) -> p g h j", g=QG, h=16, j=32)
        t3 = BT1[:, 2].rearrange("p (g j h) -> p g j h", g=QG, j=32, h=16)
        t4 = BT1[:, 3].rearrange("p (g j h) -> p g j h", g=QG, j=32, h=16)
        for q in range(32 // QG):
            gs = slice(q * QG, (q + 1) * QG)

            def Gj(tabl):
                return tabl[:, gs, 0:32].unsqueeze(2).to_broadcast([128, QG, 16, 32])

            def bbj(v):
                return v[:, gs, :].unsqueeze(3).to_broadcast([128, QG, 16, 32])

            Zr = Zb[:, 0].rearrange("p g (h j) -> p g h j", h=16)
            Zi = Zb[:, 1].rearrange("p g (h j) -> p g h j", h=16)
            P.tt(t1, Gj(PR), bbj(bbr), ALU.mult, ["PR", "bbr"], ["bt0"])
            P.tt(t2, Gj(PI_), bbj(bbi), ALU.mult, ["PI", "bbi"], ["bt1"])
            P.tt(Zr, t1, t2, ALU.subtract, ["bt0", "bt1"], ["Zb"])
            P.tt(t1, Gj(PR), bbj(bbi), ALU.mult, ["PR", "bbi"], ["bt0"])
            P.tt(t2, Gj(PI_), bbj(bbr), ALU.mult, ["PI", "bbr"], ["bt1"])
            P.tt(Zi, t1, t2, ALU.add, ["bt0", "bt1"], ["Zb"])
            for gl in range(QG):
                g = q * QG + gl
                bank = 5 + (g % 2)
                pk = "ps%d" % bank
                def f(e, gl=gl, bank=bank):
                    ins = None
                    for ht in range(4):
                        for pl in range(2):
                            ins = e.transpose(ps16(bank)[:, (ht * 2 + pl) * 128:(ht * 2 + pl + 1) * 128],
                                              Zb[:, pl, gl, ht * 128:(ht + 1) * 128], ident)
                    return ins
                P.pe(f, ["Zb", "ident"], [pk])
                wk = "WAsb%d" % (g % 2)
                P.act(WAsb[g % 2].rearrange("p a b -> p (a b)"), ps16(bank), AF.Copy, [pk], [wk])
                P.dma(WAs[g], WAsb[g % 2].rearrange("p a b -> p (a b)"), "wa", [wk], ["WAs"])

            def Hj(tabl):
                return tabl[:, gs, 32:64].unsqueeze(3).to_broadcast([128, QG, 32, 16])

            def cj(v):
                return v[:, gs, :].unsqueeze(2).to_broadcast([128, QG, 32, 16])

            WCr = WCsb[:, :, 0].rearrange("p g (j h) -> p g j h", j=32)
            WCn = WCsb[:, :, 1].rearrange("p g (j h) -> p g j h", j=32)
            P.tt(t3, Hj(PR), cj(CR), ALU.mult, ["PR", "BC"], ["bt2"])
            P.tt(t4, Hj(PI_), cj(CI), ALU.mult, ["PI", "BC"], ["bt3"])
            P.tt(WCr, t3, t4, ALU.subtract, ["bt2", "bt3"], ["WCsb"])
            P.tt(t3, Hj(PI_), cj(CR), ALU.mult, ["PI", "BC"], ["bt2"])
            P.tt(t4, Hj(PR), cj(CI), ALU.mult, ["PR", "BC"], ["bt3"])
            P.stt(WCn, t3, -1.0, t4, ALU.mult, ALU.subtract, ["bt2", "bt3"], ["WCsb"])
            P.dma(WCs[q * QG:(q + 1) * QG].rearrange("g p f -> p g f"), WCsb.rearrange("p g a f -> p g (a f)"), "wc", ["WCsb"], ["WCs"])
        if PCSTAGE < 5:
            P.barrier(); return
        P.barrier()
        o = 2048
        TH32 = sm[:, 6]
        tq_i = V(ai32, nx(128) // 4, 32)
        tq_f = f32v(nx(128), 32)
        P.ts(TH32, THS, 32.0, None, ALU.mult, None, ["THS"], ["TH32"])
        P.ts(tq_i, TH32, 1.0 / TWO_PI, None, ALU.mult, None, ["TH32"], ["tqi"])
        P.copy(tq_f, tq_i, ["tqi"], ["tqf"])
        P.stt(TH32, tq_f, -TWO_PI, TH32, ALU.mult, ALU.add, ["tqf", "TH32"], ["TH32"])
        NS = 32 * NCH
        S0 = f32v(nx(NS * 4), 32, NCH)
        S1 = V(ai32, nx(NS * 4) // 4, 32, NCH)
        S2 = f32v(nx(NS * 4), 32, NCH)
        S3 = f32v(nx(NS * 4), 32, NCH)
        S4 = f32v(nx(NS * 4), 32, NCH)
        assert o <= AW * 4, o
        P.tt(S0, TH32.unsqueeze(2).to_broadcast([128, 32, NCH]), CIDX.unsqueeze(1).to_broadcast([128, 32, NCH]), ALU.mult,
             ["TH32", "CST"], ["s_ang"])
        sincos(S0, S1, S2, S3, S4, "s_")
        P.ts(S3, S3, SGN, None, ALU.mult, None, ["s_sin", "CST"], ["s_sin"])
        P.dma(TABd[0].rearrange("p (g c) -> p g c", g=32), S4, "tab", ["s_cos"], ["TABd"])
        P.dma(TABd[1].rearrange("p (g c) -> p g c", g=32), S3, "tab", ["s_sin"], ["TABd"])
        R = sm[:, 7]
        P.act(R, XS, AF.Exp, ["XS"], ["R"], scale=32.0)
        P.copy(S2, R.unsqueeze(2).to_broadcast([128, 32, NCH]), ["R", "s_tf", "s_ang"], ["RZ"])
        P.memset(S2[0:64, :, 0:1], 0.0, ["RZ"])
        P.memset(S2[64:128, :, NCH - 1:NCH], 0.0, ["RZ"])
        P.dma(TABd[2].rearrange("p (g c) -> p g c", g=32), S2, "tab", ["RZ"], ["TABd"])
        P.barrier()

    O_UT = 0
    O_QT = 32 * KiB
    O_KV = 64 * KiB
    O_YB = 96 * KiB
    O_YA = 128 * KiB
    O_W = 160 * KiB
    UT = b16v(O_UT, 4, NCH, 32)
    QT = b16v(O_QT, 4, L)
    ZT = QT
    KT = [b16v(O_KV, 2, L), b16v(O_W, 2, L)]
    VA = b16v(O_KV + 2 * L * 2, NB, 2, 128)
    YTOK = b16v(O_KV, 4, 32, 8, 16)
    YB = b16v(O_YB, 4, L)
    YA = b16v(O_YA, 4, L)

    def rmsnorm_group(hT, N, gi, SQ, HN, RSTD, bank, pfx, eng2="dve"):
        pk = "ps%d" % bank
        P.act(SQ, hT, AF.Square, [pfx + "h"], [pfx + "sq"])
        def f(e):
            ins = None
            for kc in range(KC):
                ins = e.matmul(ps32(bank)[:, 0:N], ONES, SQ[:, kc, :], start=(kc == 0), stop=(kc == KC - 1))
            return ins
        P.pe(f, [pfx + "sq", "ONES"], [pk])
        P.act(RSTD, ps32(bank)[:, 0:N], AF.Sqrt, [pk], [pfx + "rstd"], bias=EPSB, scale=1.0 / D)
        P.add("dve", lambda e: e.reciprocal(out=RSTD, in_=RSTD), [pfx + "rstd"], [pfx + "rstd"])
        for kc in range(KC):
            P.stt(HN[:, kc, :], hT[:, kc, :], GN[:, gi, kc:kc + 1], RSTD, ALU.mult, ALU.mult,
                  [pfx + "h", pfx + "rstd", "GN"], [pfx + "hn"], eng=("dve" if kc % 2 == 0 else eng2))

    EPSB = V(s32, 4000, 1)
    P.ops.insert(0, dict(eng="dve", fn=lambda e: e.memset(EPSB, 1e-6), r=("PH",), w=("EPSB",), dk=None, sig=False))

    def phase_A(l):
        O_WA1 = O_YB
        WA1 = b16v(O_WA1, KC, 1408)
        wload(WA1, w_in[l], 0, 1408, "wA", "WA1")
        o = O_WA1 + 22528
        HT = [f32v(o, KC, 512), f32v(o, KC, 512)]
        o += 16 * KiB
        SQ = b16v(o, KC, 512); o += 8 * KiB
        HN = b16v(o, KC, 512); o += 8 * KiB
        RSTD = f32v(o, 512); o += 2 * KiB
        UTMP = b16v(o, 512); o += 1 * KiB
        assert o <= O_W
        P.memset(KT[0][64:128], 0.0, ["KT"])
        P.memset(KT[1][0:64], 0.0, ["KT"])
        src = xT if l == 0 else hscr
        for tg in range(NG):
            hk = "A0h"
            hT = HT[tg % 2]
            P.dma(hT, src[tg], "hA0", [], [hk])
            pf = "A%d" % (tg % 2)
            P.act(SQ, hT, AF.Square, [hk], ["Asq"])
            def f(e):
                ins = None
                for kc in range(KC):
                    ins = e.matmul(ps32(0), ONES, SQ[:, kc, :], start=(kc == 0), stop=(kc == KC - 1))
                return ins
            P.pe(f, ["Asq", "ONES"], ["ps0"])
            P.act(RSTD, ps32(0), AF.Sqrt, ["ps0"], ["Arstd"], bias=EPSB, scale=1.0 / D)
            P.add("dve", lambda e: e.reciprocal(out=RSTD, in_=RSTD), ["Arstd"], ["Arstd"])
            for kc in range(KC):
                P.stt(HN[:, kc, :], hT[:, kc, :], GN[:, l, kc:kc + 1], RSTD, ALU.mult, ALU.mult,
                      [hk, "Arstd", "GN"], ["Ahn"])
            for t in range(10):
                bank = 1 + (t % 4)
                pk = "ps%d" % bank
                def f(e, t=t, bank=bank):
                    ins = None
                    for kc in range(KC):
                        ins = e.matmul(ps32(bank), WA1[:, kc, t * 128:(t + 1) * 128], HN[:, kc, :], start=(kc == 0), stop=(kc == KC - 1))
                    return ins
                P.pe(f, ["WA1", "Ahn"], [pk])
                tok = slice(tg * 512, (tg + 1) * 512)
                if t < 4:
                    P.act(UTMP, ps32(bank), AF.Copy, [pk], ["UTMP"])
                    dst = UT[:, t, tg * 16:(tg + 1) * 16, :].rearrange("p c g -> p (c g)")
                    P.add("dve", lambda e, dst=dst: e.transpose(out=dst, in_=UTMP), ["UTMP"], ["UT"])
                elif t < 8:
                    P.act(QT[:, t - 4, tok], ps32(bank), AF.Copy, [pk], ["QT"])
                else:
                    P.act(KT[0][0:64, t - 8, tok], ps32(bank)[0:64, :], AF.Copy, [pk], ["KTa"])
                    P.act(KT[1][64:128, t - 8, tok], ps32(bank)[64:128, :], AF.Copy, [pk], ["KTb"])
            for tt_ in range(4):
                bank = 5 + (tt_ % 2)
                pk = "ps%d" % bank
                def f(e, tt_=tt_, bank=bank):
                    ins = None
                    for kc in range(KC):
                        ins = e.matmul(ps32(bank)[:, 0:128], HN[:, kc, tt_ * 128:(tt_ + 1) * 128], WA1[:, kc, 1280:1408],
                                       start=(kc == 0), stop=(kc == KC - 1))
                    return ins
                P.pe(f, ["WA1", "Ahn"], [pk])
                blk = tg * 4 + tt_
                P.copy(VA[:, blk, :, 0:64], ps32(bank)[:, 0:128].rearrange("p (k d) -> p k d", k=2), [pk], ["VA"])
        P.memset(VA[:, :, :, 64:128], 1.0, ["VA1"])
        P.barrier()

    def phase_ATT(l):
        o = O_YA
        PT = [b16v(o, 512), b16v(o + KiB, 512), b16v(o + 2 * KiB, 512)]
        o += 3 * KiB
        RD = f32v(o, 512, p0=0, pn=64); o += 2 * KiB
        pend = []

        def att_norm(n, kh, obank, ok):
            esk = ESK[64:128, l, kh * 4:(kh + 1) * 4].unsqueeze(2).to_broadcast([64, 4, 128])
            P.tt(RD.rearrange("p (h q) -> p h q", h=4), ps32(obank)[64:128, :].rearrange("p (h q) -> p h q", h=4), esk, ALU.add,
                 [ok, "ESK"], ["RD"])
            P.act(RD, RD, AF.Ln, ["RD"], ["RD"])
            P.act(RD, RD, AF.Exp, ["RD"], ["RD"], scale=-1.0)
            for par in range(2):
                num = ps32(obank)[0:64, :].rearrange("p (a b q) -> p a b q", a=2, b=2)[:, :, par, :]
                rd = RD.rearrange("p (a b q) -> p a b q", a=2, b=2)[:, :, par, :]
                dst = YB[par * 64:(par + 1) * 64, kh * 2:kh * 2 + 2, n * 128:(n + 1) * 128]
                P.tt(dst, num, rd, ALU.mult, [ok, "RD"], ["YB"])

        for n in range(NB):
            for kh in range(2):
                kbs = [kb for kb in (n - 1, n, n + 1) if 0 <= kb < NB]
                obank = 6 + (kh % 2)
                ok = "ps%d" % obank
                def emit_qk(ki, kb, n=n, kh=kh):
                    it = (n * 2 + kh) * 3 + ki
                    lb = it % 4
                    lk = "ps%d" % lb
                    rel = kb - n + 1
                    def f(e, lb=lb, rel=rel, kb=kb, kh=kh, n=n):
                        e.matmul(ps32(lb), ident, BM[:, rel, kh * 4:(kh + 1) * 4, :].rearrange("p h q -> p (h q)"), start=True, stop=False)
                        ins = None
                        for hl in range(4):
                            h = kh * 4 + hl
                            i, half = h // 2, h % 2
                            ins = e.matmul(ps32(lb)[:, hl * 128:(hl + 1) * 128],
                                           KT[half][:, kh, kb * 128:(kb + 1) * 128],
                                           QT[:, i, n * 128:(n + 1) * 128],
                                           start=False, stop=(hl == 3))
                        return ins
                    P.pe(f, ["BM", "ident", "KT", "KTa", "KTb", "QT"], [lk])
                    P.act(PT[it % 3], ps32(lb), AF.Exp, [lk], ["PT%d" % (it % 3)], scale=0.125)

                def emit_pv(ki, kb, n=n, kh=kh, obank=obank, ok=ok, nk=len(kbs)):
                    it = (n * 2 + kh) * 3 + ki
                    P.mm(ps32(obank), VA[:, kb, kh, :], PT[it % 3], ki == 0, ki == nk - 1, ["VA", "VA1", "PT%d" % (it % 3)], [ok])

                for ki, kb in enumerate(kbs):
                    emit_qk(ki, kb)
                    if ki >= 1:
                        emit_pv(ki - 1, kbs[ki - 1])
                emit_pv(len(kbs) - 1, kbs[-1])
                pend.append((n, kh, obank, ok))
                if len(pend) > 1:
                    att_norm(*pend.pop(0))
        while pend:
            att_norm(*pend.pop(0))
        P.barrier()

    def phase_S(l):
        o = O_W
        def nx(nb):
            nonlocal o
            r = o
            o += nb
            return r
        NBW = GB * NCH * 4
        COSb, SINb, RZb = f32v(nx(NBW), GB * NCH), f32v(nx(NBW), GB * NCH), f32v(nx(NBW), GB * NCH)
        Mr, Mi = f32v(nx(NBW), GB * NCH), f32v(nx(NBW), GB * NCH)
        Wr, Wi = f32v(nx(NBW), GB * NCH), f32v(nx(NBW), GB * NCH)
        assert o <= AW * 4
        oy = O_YA
        WAb = [b16v(oy, 4, 2, 128), b16v(oy + 2 * KiB, 4, 2, 128)]; oy += 4 * KiB
        TZb = [b16v(oy, 4, 512), b16v(oy + 4 * KiB, 4, 512)]; oy += 8 * KiB
        WCb = [b16v(oy, 2, 512), b16v(oy + 2 * KiB, 2, 512)]; oy += 4 * KiB
        SS = b16v(oy, 2, 32, NCH)
        oy += 2 * 32 * NCH * 2
        assert oy <= O_W
        T1 = f32v(O_QT, GB * NCH)
        T2 = f32v(O_QT + NBW, GB * NCH)
        WGL = b16v(O_YB - 0, 1) if False else None
        P.memset(SS[0:64, :, :, 0:1], 0.0, ["SS"])
        P.memset(SS[64:128, :, :, NCH - 1:NCH], 0.0, ["SS"])
        for b in range(NBATCH):
            for i, tb_ in enumerate((COSb, SINb, RZb)):
                P.dma(tb_, TABd[i][:, b * GB * NCH:(b + 1) * GB * NCH], "tabl", ["TABd"], ["tabs"])
            for gl in range(GB):
                g = b * GB + gl
                sl = g % 2
                P.dma(WAb[sl].rearrange("p a b c -> p (a b c)"), WAs[g], "wal%d" % sl, ["WAs"], ["WAb%d" % sl])
                def f(e, g=g, gl=gl, sl=sl):
                    ins = None
                    for pl in range(2):
                        for ht in range(4):
                            ins = e.matmul(ps32(pl)[:, gl * NCH:(gl + 1) * NCH], WAb[sl][:, ht, pl, :], UT[:, ht, :, g],
                                           start=(ht == 0), stop=(ht == 3))
                    return ins
                P.pe(f, ["WAb%d" % sl, "UT"], ["ps0", "ps1"])
            N = GB * NCH
            Lr, Li = ps32(0)[:, 0:N], ps32(1)[:, 0:N]
            P.tt(T1, Lr, COSb, ALU.mult, ["ps0", "tabs"], ["T1"])
            P.tt(T2, Li, SINb, ALU.mult, ["ps1", "tabs"], ["T2"])
            P.tt(Mr, T1, T2, ALU.add, ["T1", "T2"], ["Mr"])
            P.tt(T1, Li, COSb, ALU.mult, ["ps1", "tabs", "Mr"], ["T1"])
            P.tt(T2, Lr, SINb, ALU.mult, ["ps0", "tabs", "Mr"], ["T2"])
            P.tt(Mi, T1, T2, ALU.subtract, ["T1", "T2"], ["Mi"])
            for (Mx, Wx, mk, wk) in ((Mr, Wr, "Mr", "Wr"), (Mi, Wi, "Mi", "Wi")):
                P.add("dve", lambda e, Mx=Mx, Wx=Wx: e.tensor_tensor_scan(out=Wx[0:64], data0=RZb[0:64], data1=Mx[0:64], initial=0.0,
                                                                          op0=ALU.mult, op1=ALU.add), [mk, "tabs"], [wk])
                def rev(ap):
                    full = ap[64:128]
                    return AP(full.tensor, full.offset + N - 1, [list(full.ap[0]), [-1, N]])
                P.add("dve", lambda e, Mx=Mx, Wx=Wx, rev=rev: e.tensor_tensor_scan(out=rev(Wx), data0=rev(RZb), data1=rev(Mx), initial=0.0,
                                                                                   op0=ALU.mult, op1=ALU.add), [mk, "tabs"], [wk])
            def g3(ap, p0):
                return ap[p0:p0 + 64].rearrange("p (g c) -> p g c", g=GB)
            gsl = slice(b * GB, (b + 1) * GB)
            P.tt(T1, Wr, COSb, ALU.mult, ["Wr", "tabs", "Mi"], ["T1"])
            P.tt(T2, Wi, SINb, ALU.mult, ["Wi", "tabs", "Mi"], ["T2"])
            if NCH > 1:
                P.tt(SS[0:64, 0, gsl, 1:NCH], g3(T1, 0)[:, :, 0:NCH - 1], g3(T2, 0)[:, :, 0:NCH - 1], ALU.subtract, ["T1", "T2"], ["SS"])
                P.tt(SS[64:128, 0, gsl, 0:NCH - 1], g3(T1, 64)[:, :, 1:NCH], g3(T2, 64)[:, :, 1:NCH], ALU.subtract, ["T1", "T2"], ["SS"])
            P.tt(T1, Wi, COSb, ALU.mult, ["Wi", "tabs", "SS"], ["T1"])
            P.tt(T2, Wr, SINb, ALU.mult, ["Wr", "tabs", "SS"], ["T2"])
            if NCH > 1:
                P.tt(SS[0:64, 1, gsl, 1:NCH], g3(T1, 0)[:, :, 0:NCH - 1], g3(T2, 0)[:, :, 0:NCH - 1], ALU.add, ["T1", "T2"], ["SS"])
                P.tt(SS[64:128, 1, gsl, 0:NCH - 1], g3(T1, 64)[:, :, 1:NCH], g3(T2, 64)[:, :, 1:NCH], ALU.add, ["T1", "T2"], ["SS"])
        NS3, PD3 = 4, 3
        TZs = [b16v(O_QT + 8 * KiB + i * 4 * KiB, 512, 4) for i in range(NS3)]
        WCs_ = [b16v(O_QT + 24 * KiB + i * 2 * KiB, 2, 512) for i in range(NS3)]

        def s3_load(g):
            sl = g % NS3
            srcap = AP(KD.tensor, g * 16384, [[4096, 4], [64, 32], [1, 2048]])
            P.dma(TZs[sl].rearrange("p c t -> p (c t)"), srcap, "tz%d" % sl, ["KD"], ["TZb%d" % sl], eng=("sp" if g % 2 == 0 else "act"))
            P.dma(WCs_[sl].rearrange("p a f -> p (a f)"), WCs[g], "wcl%d" % sl, ["WCs"], ["WCb%d" % sl])

        for g in range(min(PD3, 32)):
            s3_load(g)
        for g in range(32):
            if g + PD3 < 32:
                s3_load(g + PD3)
            sl = g % NS3
            bank = 2 + (g % 2)
            pk = "ps%d" % bank
            def f(e, g=g, sl=sl, bank=bank):
                for ht in range(4):
                    e.matmul(ps32(bank)[0:NCH, :], UT[:, ht, :, g], TZs[sl][:, :, ht], start=(ht == 0), stop=False)
                e.matmul(ps32(bank)[0:NCH, :], SS[:, 0, g, :], WCs_[sl][:, 0, :], start=False, stop=False)
                return e.matmul(ps32(bank)[0:NCH, :], SS[:, 1, g, :], WCs_[sl][:, 1, :], start=False, stop=True)
            P.pe(f, ["UT", "TZb%d" % sl, "SS", "WCb%d" % sl], [pk])
            P.act(YTOK[0:NCH, g // 8, :, g % 8, :], ps32(bank)[0:NCH, :].rearrange("p (j h) -> p j h", j=32), AF.Copy, [pk], ["YTOK"])
        if "ytok" in dbg_slot:
            tmpf = f32v(O_QT, 4096)
            for gq in range(4):
                pass
        it = 0
        for gt in range(4):
            for jb in range(4):
                bank = 4 + (it % 2)
                pk = "ps%d" % bank
                it += 1
                def f(e, gt=gt, jb=jb, bank=bank):
                    ins = None
                    for jj in range(8):
                        jr = jb * 8 + jj
                        src = YTOK[0:NCH, gt, jr].rearrange("p g h -> p (g h)")
                        ins = e.transpose(ps16(bank)[:, jj * NCH:(jj + 1) * NCH], src, ident[0:NCH, 0:NCH])
                    return ins
                P.pe(f, ["YTOK", "ident"], [pk])
                zt = ZT[:, gt, :]
                j0 = 31 - jb * 8
                dst = AP(zt.tensor, zt.offset + j0, [list(zt.ap[0]), [-1, 8], [32, NCH]])
                P.act(dst, ps16(bank)[:, 0:8 * NCH].rearrange("p (j c) -> p j c", j=8), AF.Gelu_apprx_tanh, [pk], ["ZT"])
        P.barrier()
        wload(b16v(O_KV, KC, 2048), w_in[l], 1408, 3456, "wc1g", "WGA")
        WG = b16v(O_UT, 4, 512)
        wload(WG, w_glu[l], 0, 512, "wg", "WG")
        SG = [b16v(O_UT + 4 * KiB, 512), b16v(O_UT + 5 * KiB, 512)]
        it = 0
        for tg in range(NG):
            tok = slice(tg * 512, (tg + 1) * 512)
            for ot in range(4):
                bank = 6 + (it % 2)
                pk = "ps%d" % bank
                sg = SG[it % 2]
                sk = "SG%d" % (it % 2)
                it += 1
                def f(e, ot=ot, bank=bank, tok=tok):
                    ins = None
                    for kt in range(4):
                        ins = e.matmul(ps32(bank), WG[:, kt, ot * 128:(ot + 1) * 128], ZT[:, kt, tok], start=(kt == 0), stop=(kt == 3))
                    return ins
                P.pe(f, ["WG", "ZT"], [pk])
                P.act(sg, ps32(bank), AF.Sigmoid, [pk], [sk])
                P.tt(YA[:, ot, tok], ZT[:, ot, tok], sg, ALU.mult, ["ZT", sk], ["YA"])
        P.barrier()

    def phase_C1(l):
        o = 0
        def nx(nb):
            nonlocal o
            r = o
            o += nb
            return r
        WGA = b16v(O_KV, KC, 2048)
        WBA = b16v(nx(8 * KiB), 4, D)
        WBB = b16v(nx(8 * KiB), 4, D)
        WO = b16v(nx(16 * KiB), KC, D)
        wload(WBA, w_a[l], 0, D, "wc1a", "WBA")
        wload(WBB, w_b[l], 0, D, "wc1b", "WBB")
        wload(WO, w_out[l], 0, D, "wc1o", "WO")
        assert o == 32 * KiB
        HTs = [f32v(nx(16 * KiB), KC, 512), f32v(nx(16 * KiB), KC, 512)]
        assert o == 64 * KiB
        o = O_W
        SQ = b16v(nx(8 * KiB), KC, 512)
        HN = b16v(nx(8 * KiB), KC, 512)
        MG = SQ
        src = xT if l == 0 else hscr
        RSTD = V(s32, 2400, 512)
        TA = V(s32, 2912, 512)
        TM = V(s32, 4096, 512)
        GAs = [V(s16, 2 * 4608 + i * 512, 512) for i in range(2)]
        GBs = [V(s16, 2 * 5120 + i * 512, 512) for i in range(2)]
        P.dma(HTs[0], src[0], "hC0", ["hscr"], ["Ch0"])
        for tg in range(NG):
            tok = slice(tg * 512, (tg + 1) * 512)
            HT = HTs[tg % 2]
            ck = "Ch%d" % (tg % 2)
            if tg + 1 < NG:
                P.dma(HTs[(tg + 1) % 2], src[tg + 1], "hC%d" % ((tg + 1) % 2), ["hscr"], ["Ch%d" % ((tg + 1) % 2)])
            P.act(SQ, HT, AF.Square, [ck], ["Csq"])
            def f(e):
                ins = None
                for kc in range(KC):
                    ins = e.matmul(ps32(0), ONES, SQ[:, kc, :], start=(kc == 0), stop=(kc == KC - 1))
                return ins
            P.pe(f, ["Csq", "ONES"], ["ps0"])
            P.act(RSTD, ps32(0), AF.Sqrt, ["ps0"], ["Crstd"], bias=EPSB, scale=1.0 / D)
            P.add("dve", lambda e: e.reciprocal(out=RSTD, in_=RSTD), ["Crstd"], ["Crstd"])
            for kc in range(KC):
                P.stt(HN[:, kc, :], HT[:, kc, :], GN[:, l, kc:kc + 1], RSTD, ALU.mult, ALU.mult, [ck, "Crstd", "GN"], ["Chn"])
            for ot in range(8):
                sl = ot % 2
                ba, bb_ = 1 + sl, 3 + sl
                def f(e, ot=ot, ba=ba, bb_=bb_):
                    ins = None
                    for kc in range(KC):
                        e.matmul(ps32(ba), WGA[:, kc, ot * 128:(ot + 1) * 128], HN[:, kc, :], start=(kc == 0), stop=(kc == KC - 1))
                    for kc in range(KC):
                        ins = e.matmul(ps32(bb_), WGA[:, kc, (8 + ot) * 128:(9 + ot) * 128], HN[:, kc, :], start=(kc == 0), stop=(kc == KC - 1))
                    return ins
                P.pe(f, ["WGA", "Chn"], ["ps%d" % ba, "ps%d" % bb_])
                P.act(GAs[sl], ps32(ba), AF.Sigmoid, ["ps%d" % ba], ["GA%d" % sl])
                P.act(GBs[sl], ps32(bb_), AF.Sigmoid, ["ps%d" % bb_], ["GB%d" % sl])
                def f(e, ot=ot, tok=tok):
                    ins = None
                    for kt in range(4):
                        e.matmul(ps32(5), WBA[:, kt, ot * 128:(ot + 1) * 128], YA[:, kt, tok], start=(kt == 0), stop=(kt == 3))
                    for kt in range(4):
                        ins = e.matmul(ps32(6), WBB[:, kt, ot * 128:(ot + 1) * 128], YB[:, kt, tok], start=(kt == 0), stop=(kt == 3))
                    return ins
                P.pe(f, ["WBA", "WBB", "YA", "YB"], ["ps5", "ps6"])
                P.tt(TA, ps32(5), GAs[sl], ALU.mult, ["ps5", "GA%d" % sl], ["TA"])
                P.tt(TM, ps32(6), GBs[sl], ALU.mult, ["ps6", "GB%d" % sl], ["TM"])
                P.tt(MG[:, ot, :], TM, TA, ALU.add, ["TM", "TA"], ["Csq"], eng="pool")
            for ot in range(8):
                bank = 5 + (ot % 3)
                pk = "ps%d" % bank
                def f(e, ot=ot, bank=bank):
                    ins = None
                    for kc in range(KC):
                        ins = e.matmul(ps32(bank), WO[:, kc, ot * 128:(ot + 1) * 128], MG[:, kc, :], start=(kc == 0), stop=(kc == KC - 1))
                    return ins
                P.pe(f, ["WO", "Csq"], [pk])
                P.tt(HT[:, ot, :], HT[:, ot, :], ps32(bank), ALU.add, [pk, ck, "Chn"], [ck + "2"])
            P.dma(hscr[tg], HT, "hCs", [ck + "2", ck], ["hscr"])
        P.barrier()

    def phase_C2(l):
        o = 0
        def nx(nb):
            nonlocal o
            r = o
            o += nb
            return r
        WFG = b16v(nx(KC * DFF * 2), KC, DFF)
        WFU = b16v(nx(KC * DFF * 2), KC, DFF)
        WFD = b16v(nx(NF * D * 2), NF, D)
        qi = 0
        for (dst, srcb, nk, wk) in ((WFG, wfg_b, KC, "WFG"), (WFU, wfu_b, KC, "WFU"), (WFD, wfd_b, NF, "WFD")):
            for kc in range(nk):
                P.dma(dst[:, kc, :], srcb[kc * 128:(kc + 1) * 128, :], "l" + wk, ["WFb"], [wk], eng=("sp" if qi % 2 == 0 else "act"))
                qi += 1
        TN = 256
        HT = [f32v(nx(8 * KiB), KC, TN), f32v(nx(8 * KiB), KC, TN)]
        SQ = b16v(nx(4 * KiB), KC, TN)
        HNs = [b16v(nx(4 * KiB), KC, TN), b16v(nx(4 * KiB), KC, TN)]
        FF = b16v(nx(NF * TN * 2), NF, TN)
        SL = [b16v(nx(512), TN), b16v(nx(512), TN)]
        assert o <= AW * 4, o
        RSTD = V(s32, 2400, TN)
        last = (l == DEPTH - 1)
        NT2 = NG * 2

        def load(t2):
            tg, hf = t2 // 2, t2 % 2
            P.dma(HT[t2 % 2], hscr[tg][:, :, hf * TN:(hf + 1) * TN], "hF%d" % (t2 % 2), ["hscr"], ["F%dh" % (t2 % 2)])

        def norm(t2):
            ht = HT[t2 % 2]
            hk = "F%dh" % (t2 % 2)
            HN = HNs[t2 % 2]
            nk = "Fhn%d" % (t2 % 2)
            P.act(SQ, ht, AF.Square, [hk], ["Fsq"])
            P.pe(f_final_ss(SQ, TN), ["Fsq", "ONES"], ["ps0"])
            P.act(RSTD, ps32(0)[:, 0:TN], AF.Sqrt, ["ps0"], ["Frstd"], bias=EPSB, scale=1.0 / D)
            P.add("dve", lambda e: e.reciprocal(out=RSTD, in_=RSTD), ["Frstd"], ["Frstd"])
            for kc in range(KC):
                P.stt(HN[:, kc, :], ht[:, kc, :], GN[:, 2 + l, kc:kc + 1], RSTD, ALU.mult, ALU.mult, [hk, "Frstd", "GN"], [nk])

        load(0)
        norm(0)
        for t2 in range(NT2):
            tg, hf = t2 // 2, t2 % 2
            ht = HT[t2 % 2]
            hk = "F%dh" % (t2 % 2)
            HN = HNs[t2 % 2]
            nk = "Fhn%d" % (t2 % 2)
            if t2 + 1 < NT2:
                load(t2 + 1)
            for fi in range(NF):
                bg = 1 + (fi % 2) * 2
                bu = bg + 1
                def f(e, fi=fi, bg=bg, bu=bu, HN=HN):
                    ins = None
                    for kc in range(KC):
                        e.matmul(ps32(bg)[:, 0:TN], WFG[:, kc, fi * 128:(fi + 1) * 128], HN[:, kc, :], start=(kc == 0), stop=(kc == KC - 1))
                    for kc in range(KC):
                        ins = e.matmul(ps32(bu)[:, 0:TN], WFU[:, kc, fi * 128:(fi + 1) * 128], HN[:, kc, :], start=(kc == 0), stop=(kc == KC - 1))
                    return ins
                P.pe(f, ["WFG", "WFU", nk], ["ps%d" % bg, "ps%d" % bu])
                sl = SL[fi % 2]
                sk = "SL%d" % (fi % 2)
                P.act(sl, ps32(bg)[:, 0:TN], AF.Silu, ["ps%d" % bg], [sk])
                P.tt(FF[:, fi, :], ps32(bu)[:, 0:TN], sl, ALU.mult, ["ps%d" % bu, sk], ["FF"])
            if t2 + 1 < NT2:
                norm(t2 + 1)
            for ot in range(8):
                bank = 5 + (ot % 3)
                pk = "ps%d" % bank
                def f(e, ot=ot, bank=bank):
                    ins = None
                    for fi in range(NF):
                        ins = e.matmul(ps32(bank)[:, 0:TN], WFD[:, fi, ot * 128:(ot + 1) * 128], FF[:, fi, :], start=(fi == 0), stop=(fi == NF - 1))
                    return ins
                P.pe(f, ["WFD", "FF"], [pk])
                P.tt(ht[:, ot, :], ht[:, ot, :], ps32(bank)[:, 0:TN], ALU.add, [pk, hk, nk], [hk + "2"])
            if not last:
                P.dma(hscr[tg][:, :, hf * TN:(hf + 1) * TN], ht, "hFs", [hk + "2", hk], ["hscr"])
            else:
                P.act(SQ, ht, AF.Square, [hk + "2", hk], ["Fsq"])
                P.pe(f_final_ss(SQ, TN), ["Fsq", "ONES"], ["ps0"])
                P.act(RSTD, ps32(0)[:, 0:TN], AF.Sqrt, ["ps0"], ["Frstd"], bias=EPSB, scale=1.0 / D)
                P.add("dve", lambda e: e.reciprocal(out=RSTD, in_=RSTD), ["Frstd"], ["Frstd"])
                for kc in range(KC):
                    P.stt(ht[:, kc, :], ht[:, kc, :], GN[:, 4, kc:kc + 1], RSTD, ALU.mult, ALU.mult, [hk + "2", hk, "Frstd", "GN"], [hk + "3"])
                P.dma(out[t2], ht, "outd", [hk + "3", hk, hk + "2"], ["outd"])
        P.barrier()

    def f_final_ss(SQ, TN):
        def f(e):
            ins = None
            for kc in range(KC):
                ins = e.matmul(ps32(0)[:, 0:TN], ONES, SQ[:, kc, :], start=(kc == 0), stop=(kc == KC - 1))
            return ins
        return f

    def ffn_convert(l):
        for (dst, srcw, nk) in ((wfg_b, w_fg[l], KC), (wfu_b, w_fu[l], KC), (wfd_b, w_fd[l], NF)):
            for kc in range(nk):
                P.dma(dst[kc * 128:(kc + 1) * 128, :], srcw[kc * 128:(kc + 1) * 128, :], "wcv", [], ["WFb"], eng="pool", nobar=True)

    step = 0
    for l in range(DEPTH):
        for ph in (s5_precompute, phase_A, phase_ATT, phase_S, phase_C1, phase_C2):
            step += 1
            if step <= upto:
                ph(l)
                if ph is phase_A:
                    ffn_convert(l)
    if upto < 12:
        for t2 in range(NG * 2):
            P.dma(out[t2], f32v(0, KC, 256), "outd", [], ["outd"])

    P.emit(final_keys=["outd_sp"] + (["dbg_sp"] if dbg_items else []))
    es.close()
    return nc


def _t5_bucket_np(rel):
    import jax.numpy as jnp
    NUM_BUCKETS, MAX_DISTANCE = 32, 128
    rel = jnp.asarray(rel, dtype=jnp.int32)
    half = NUM_BUCKETS // 2
    max_exact = half // 2
    ret = jnp.where(rel > 0, half, 0)
    n = jnp.abs(rel)
    nf = jnp.maximum(n, 1).astype(jnp.float32)
    large = max_exact + (jnp.log(nf / max_exact) / math.log(MAX_DISTANCE / max_exact) * (half - max_exact)).astype(jnp.int32)
    large = jnp.minimum(large, half - 1)
    return np.asarray(ret + jnp.where(n < max_exact, n, large))


def prepare_shared(inp, L):
    f = np.float32
    NCH = L // 32
    sh = {}
    g = np.stack([inp["norm1_g"][0], inp["norm1_g"][1], inp["norm2_g"][0], inp["norm2_g"][1], inp["final_g"]], 0)
    sh["gains"] = np.ascontiguousarray(g.reshape(5, KC, 128).transpose(2, 0, 1)).astype(f)
    w = inp["w_in"]
    u_perm = np.array([(c % 32) * 16 + (c // 32) for c in range(512)])
    cols = [w[:, :, u_perm], w[:, :, 512:1024],
            w[:, :, 1024:1088], w[:, :, 1024:1088], w[:, :, 1088:1152], w[:, :, 1088:1152],
            w[:, :, 1152:1280], w[:, :, 1280:3328]]
    sh["w_in"] = np.ascontiguousarray(np.concatenate(cols, axis=2)).astype(f)
    lam_re, lam_im, log_dt = inp["s5_lambda_re"], inp["s5_lambda_im"], inp["s5_log_dt"]
    s5a = np.zeros((DEPTH, 128, 3, 32), f)
    for l in range(DEPTH):
        s5a[l, :, 0, :] = lam_re[l].transpose(0, 2, 1).reshape(128, 32)
        s5a[l, :, 1, :] = lam_im[l].transpose(0, 2, 1).reshape(128, 32)
        s5a[l, :, 2, :] = np.repeat(log_dt[l][:, None, :], 64, axis=1).reshape(128, 32)
    sh["s5a"] = s5a
    s5b = np.zeros((DEPTH, 128, 4, 32, 16), f)
    for l in range(DEPTH):
        s5b[l, :, 0] = inp["s5_b_re"][l].transpose(0, 2, 1, 3).reshape(128, 32, 16)
        s5b[l, :, 1] = inp["s5_b_im"][l].transpose(0, 2, 1, 3).reshape(128, 32, 16)
        s5b[l, :, 2] = inp["s5_c_re"][l].transpose(0, 3, 1, 2).reshape(128, 32, 16)
        s5b[l, :, 3] = inp["s5_c_im"][l].transpose(0, 3, 1, 2).reshape(128, 32, 16)
    sh["s5b"] = s5b
    s5d = np.zeros((DEPTH, 4, 32, 16, 4), f)
    dd = inp["s5_d"].reshape(DEPTH, 32, 16)
    for ho in range(16):
        s5d[:, ho % 4, :, ho, ho // 4] = dd[:, :, ho]
    sh["s5d"] = s5d.reshape(DEPTH, 4, 32, 64)
    sh["w_glu"] = inp["s5_w_glu"].astype(f)
    sh["w_a"] = inp["w_branch_a"].astype(f)
    sh["w_b"] = inp["w_branch_b"].astype(f)
    sh["w_out"] = inp["w_out"].astype(f)
    sh["w_fg"] = inp["ffn_w_gate"].astype(f)
    sh["w_fu"] = inp["ffn_w_up"].astype(f)
    sh["w_fd"] = inp["ffn_w_down"].astype(f)
    sh["sinkb"] = np.ascontiguousarray(np.broadcast_to(inp["attn_sink"][None], (128, DEPTH, 8))).astype(f)
    s_ = np.arange(128)[:, None, None]
    r_ = np.arange(3)[None, :, None]
    q_ = np.arange(128)[None, None, :]
    rel = (r_ - 1) * 128 + s_ - q_
    bucket = _t5_bucket_np(rel)
    bg = inp["rel_bias"][bucket]
    sh["biasg"] = np.ascontiguousarray(bg.transpose(0, 1, 3, 2)).astype(f)
    sh["maskc"] = np.where(np.abs(rel) <= 128, 0.0, -30000.0).astype(f)
    cst = np.zeros((128, NTK + 128 + 1 + 16), f)
    k = np.arange(32)
    for d in range(2):
        rows = slice(64 * d, 64 * d + 64)
        cst[rows, 0:32] = (31 - k) if d == 0 else k
        cst[rows, 32:64] = (32 - k) if d == 0 else (k + 1)
        k8 = np.arange(8)
        cst[rows, 64:72] = 4 * (7 - k8) if d == 0 else 4 * k8
        k4 = np.arange(4)
        cst[rows, 72:76] = (3 - k4) if d == 0 else k4
        cst[rows, 76] = 1.0
        cst[rows, NTK + 128] = 1.0 if d == 0 else -1.0
    cst[:, NTK:NTK + 128] = np.arange(128)[None, :]
    cst[0:16, NTK + 129:NTK + 145] = np.eye(16)
    sh["cst"] = cst
    import ml_dtypes
    sh["identd"] = np.eye(128, dtype=f)
    return sh


_NC_CACHE = {}


def kernel(**inputs):
    inp = {k: np.asarray(v) for k, v in inputs.items()}
    x = inp["x"].astype(np.float32)
    B, L, _ = x.shape
    NG = L // 512
    if L not in _NC_CACHE:
        import os
        _NC_CACHE[L] = build(L, upto=int(os.environ.get("K_UPTO", "99")))
    nc = _NC_CACHE[L]
    sh = prepare_shared(inp, L)
    in_maps = []
    for b in range(B):
        m = dict(sh)
        m["xT"] = np.ascontiguousarray(x[b].reshape(NG, 512, KC, 128).transpose(0, 3, 2, 1))
        in_maps.append(m)
    res = run_bass_kernel_spmd(nc, in_maps, core_ids=list(range(B)))
    outs = []
    for b in range(B):
        o = res.results[b]["out"]
        outs.append(o.transpose(0, 3, 2, 1).reshape(L, D))
    return np.stack(outs, 0).astype(np.float32)
```

```python
import math
import bisect
import numpy as np
import concourse.bass as bass
import concourse.mybir as mybir
from concourse.bass_utils import run_bass_kernel_spmd
from concourse.ap import AP

F32 = mybir.dt.float32
BF16 = mybir.dt.bfloat16
I32 = mybir.dt.int32
ALU = mybir.AluOpType
AF = mybir.ActivationFunctionType

D = 1024
KC = 8
DFF = 2816
NF = 22
DEPTH = 2
TWO_PI = 2.0 * math.pi
PI_LO = 3.1415925
NTK = 77


class Prog:
    ENGS = ["pe", "act", "dve", "pool", "sp"]

    def __init__(self, nc):
        self.nc = nc
        self.ops = []

    def add(self, eng, fn, r=(), w=(), dk=None, nobar=False):
        self.ops.append(dict(eng=eng, fn=fn, r=tuple(r) + (() if nobar else ("PH",)), w=tuple(w), dk=dk, sig=False))

    def barrier(self):
        self.ops.append(dict(eng="dve", fn=("bar",), r=(), w=("PH",), dk=None, sig=False))

    def mm(self, out, lhsT, rhs, start, stop, r, w):
        self.add("pe", lambda e: e.matmul(out, lhsT, rhs, start=start, stop=stop), r, w)

    def pe(self, fn, r, w):
        self.add("pe", fn, r, w)

    def act(self, out, in_, func, r, w, bias=None, scale=None):
        kw = {}
        if bias is not None:
            kw["bias"] = bias
        if scale is not None:
            kw["scale"] = scale
        self.add("act", lambda e: e.activation(out=out, in_=in_, func=func, **kw), r, w)

    def tt(self, out, in0, in1, op, r, w, eng="dve"):
        self.add(eng, lambda e: e.tensor_tensor(out=out, in0=in0, in1=in1, op=op), r, w)

    def ts(self, out, in0, s1, s2, op0, op1, r, w, eng="dve"):
        if op1 is None:
            self.add(eng, lambda e: e.tensor_scalar(out=out, in0=in0, scalar1=s1, scalar2=None, op0=op0), r, w)
        else:
            self.add(eng, lambda e: e.tensor_scalar(out=out, in0=in0, scalar1=s1, scalar2=s2, op0=op0, op1=op1), r, w)

    def stt(self, out, in0, scalar, in1, op0, op1, r, w, eng="dve"):
        self.add(eng, lambda e: e.scalar_tensor_tensor(out=out, in0=in0, scalar=scalar, in1=in1, op0=op0, op1=op1), r, w)

    def copy(self, out, in_, r, w, eng="dve"):
        self.add(eng, lambda e: e.tensor_copy(out=out, in_=in_), r, w)

    def memset(self, ap, val, w, eng="dve"):
        self.add(eng, lambda e: e.memset(ap, val), (), w)

    def dma(self, out, in_, key, r, w, eng="sp", nobar=False):
        self.add(eng, lambda e: e.dma_start(out=out, in_=in_), r, w, dk=key + "_" + eng, nobar=nobar)

    def emit(self, final_keys):
        nc = self.nc
        ops = self.ops
        last_w = {}
        readers = {}
        for i, op in enumerate(ops):
            deps = set()
            for k in op["r"]:
                if k in last_w:
                    deps.add(last_w[k])
            for k in op["w"]:
                if k in last_w:
                    deps.add(last_w[k])
                deps.update(readers.get(k, ()))
            deps.discard(i)
            op["deps"] = deps
            for k in op["r"]:
                readers.setdefault(k, []).append(i)
            for k in op["w"]:
                last_w[k] = i
                readers[k] = []
        for i, op in enumerate(ops):
            for d in op["deps"]:
                src = ops[d]
                if src["dk"] is None:
                    if src["eng"] == "pe" and op["eng"] == "pe":
                        continue
                    src["sig"] = True
        cnt = {e: 0 for e in self.ENGS}
        dma_hist = {}
        for i, op in enumerate(ops):
            if op["dk"] is not None:
                h = dma_hist.setdefault(op["dk"], [[], []])
                prev = h[1][-1] if h[1] else 0
                h[0].append(i)
                h[1].append(prev + 16)
            elif op["sig"]:
                cnt[op["eng"]] += 1
                op["cnt"] = cnt[op["eng"]]
        waited = {e: {} for e in self.ENGS}
        for i, op in enumerate(ops):
            need = {}
            for d in op["deps"]:
                src = ops[d]
                if src["dk"] is not None:
                    h = dma_hist[src["dk"]]
                    j = bisect.bisect_left(h[0], i) - 1
                    sem = ("dma", src["dk"])
                    val = h[1][j]
                else:
                    if src["eng"] == "pe" and op["eng"] == "pe":
                        continue
                    sem = ("eng", src["eng"])
                    val = src["cnt"]
                need[sem] = max(need.get(sem, 0), val)
            wl = []
            wd = waited[op["eng"]]
            for sem, val in need.items():
                if wd.get(sem, 0) < val:
                    wd[sem] = val
                    wl.append((sem, val))
            op["waits"] = wl
        sem_names = [("eng", e) for e in self.ENGS] + [("dma", k) for k in dma_hist]
        self.n_sems = len(sem_names)
        from contextlib import ExitStack
        with ExitStack() as es:
            semh = {}
            for sn in sem_names:
                semh[sn] = es.enter_context(nc.semaphore("s_%s_%s" % sn))
            block = es.enter_context(nc.Block())

            def run_engine(engname):
                def body(e):
                    for op in ops:
                        if op["eng"] != engname:
                            continue
                        for sem, val in op["waits"]:
                            e.wait_ge(semh[sem], val)
                        if op["fn"] == ("bar",):
                            ins = e.engine_nop() if False else e.memset(self.bar_ap, 0.0)
                        else:
                            ins = op["fn"](e)
                        if op["dk"] is not None:
                            ins.then_inc(semh[("dma", op["dk"])], 16)
                        elif op["sig"]:
                            ins.then_inc(semh[("eng", engname)], 1)
                    if engname == "sp":
                        for k in final_keys:
                            e.wait_ge(semh[("dma", k)], dma_hist[k][1][-1])
                return body

            block.tensor(run_engine("pe"))
            block.scalar(run_engine("act"))
            block.vector(run_engine("dve"))
            block.gpsimd(run_engine("pool"))
            block.sync(run_engine("sp"))


def build(L, dbg_items=(), upto=99):
    NG = L // 512
    NCH = L // 32
    NB = L // 128
    GB = 4
    NBATCH = 32 // GB
    nc = bass.Bass("TRN2", target_bir_lowering=False)

    def din(name, shape, dt=F32):
        return nc.dram_tensor(name, list(shape), dt, kind="ExternalInput").ap()

    xT = din("xT", [NG, 128, KC, 512])
    gains = din("gains", [128, 5, KC])
    w_in = din("w_in", [DEPTH, D, 3456])
    s5a = din("s5a", [DEPTH, 128, 3, 32])
    s5b = din("s5b", [DEPTH, 128, 4, 32, 16])
    s5d = din("s5d", [DEPTH, 4, 32, 64])
    w_glu = din("w_glu", [DEPTH, 512, 512])
    w_a = din("w_a", [DEPTH, 512, D])
    w_b = din("w_b", [DEPTH, 512, D])
    w_out = din("w_out", [DEPTH, D, D])
    w_fg = din("w_fg", [DEPTH, D, DFF])
    w_fu = din("w_fu", [DEPTH, D, DFF])
    w_fd = din("w_fd", [DEPTH, DFF, D])
    sinkb = din("sinkb", [128, DEPTH, 8])
    biasg = din("biasg", [128, 3, 8, 128])
    maskc = din("maskc", [128, 3, 128])
    cst = din("cst", [128, NTK + 128 + 1 + 16])
    identd = din("identd", [128, 128])
    out = nc.dram_tensor("out", [NG * 2, 128, KC, 256], F32, kind="ExternalOutput").ap()
    dbg = None
    if dbg_items:
        dbg = nc.dram_tensor("dbg", [len(dbg_items), 128, 4096], F32, kind="ExternalOutput").ap()

    hscr = nc.dram_tensor("hscr", [NG, 128, KC, 512], F32).ap()
    KD = nc.dram_tensor("KD", [32, 4, 64, 16, 4], BF16).ap()
    WAs = nc.dram_tensor("WAs", [32, 128, 1024], BF16).ap()
    WCs = nc.dram_tensor("WCs", [32, 128, 1024], BF16).ap()
    TABd = nc.dram_tensor("TABd", [3, 128, 32 * NCH], F32).ap()
    wfg_b = nc.dram_tensor("wfg_b", [D, DFF], BF16).ap()
    wfu_b = nc.dram_tensor("wfu_b", [D, DFF], BF16).ap()
    wfd_b = nc.dram_tensor("wfd_b", [DFF, D], BF16).ap()

    P = Prog(nc)
    from contextlib import ExitStack
    es = ExitStack()
    AW = 45056
    arena = es.enter_context(nc.sbuf_tensor("arena", [128, AW], F32))
    small = es.enter_context(nc.sbuf_tensor("small", [128, 6144], F32))
    pst = [es.enter_context(nc.psum_tensor("ps%d" % i, [128, 512], F32)) for i in range(8)]
    a32 = arena[:]
    a16 = arena[:].bitcast(BF16)
    ai32 = arena[:].bitcast(I32)
    s32 = small[:]
    s16 = small[:].bitcast(BF16)
    P.bar_ap = s32[0:1, 4095:4096]

    def V(base, off, *shape, p0=0, pn=128):
        n = int(np.prod(shape))
        v = base[p0:p0 + pn, off:off + n]
        if len(shape) == 1:
            return v
        names = "abcde"[:len(shape)]
        kw = {names[i]: shape[i] for i in range(len(shape))}
        return v.rearrange("p (%s) -> p %s" % (" ".join(names), " ".join(names)), **kw)

    def f32v(offb, *shape, **kw):
        return V(a32, offb // 4, *shape, **kw)

    def b16v(offb, *shape, **kw):
        return V(a16, offb // 2, *shape, **kw)

    def ps32(i):
        return pst[i][:]

    def ps16(i):
        return pst[i][:].bitcast(BF16)

    KiB = 1024
    ident = V(s16, 0, 128)
    GN = V(s32, 64, 5, KC)
    ESK = V(s32, 104, DEPTH, 8)
    CST = V(s32, 120, NTK + 128 + 1 + 16)
    NTAB = CST[:, 0:NTK]
    CIDX = CST[:, NTK:NTK + NCH]
    SGN = CST[:, NTK + 128:NTK + 129]
    EYE = CST[0:16, NTK + 129:NTK + 145]
    o = 120 + NTK + 145
    o = (o + 1) // 2 * 2
    BM = V(s16, o * 2, 3, 8, 128)
    o += 1536
    ONES = V(s16, o * 2, 128)
    o += 64
    THS = V(s32, o, 32); o += 32
    XS = V(s32, o, 32); o += 32
    assert o < 2400

    P.dma(ident, identd, "cid", [], ["ident"], eng="pool")
    P.dma(GN, gains, "cgn", [], ["GN"])
    P.dma(CST, cst, "ccst", [], ["CST"])
    P.memset(ONES, 1.0, ["ONES"])
    tb = f32v(0, 3, 8, 128)
    tm = f32v(16 * KiB, 3, 128)
    P.dma(tb, biasg, "ctb", [], ["tb"])
    P.dma(tm, maskc, "ctm", [], ["tm"])
    P.tt(tb, tb, tm.unsqueeze(2).to_broadcast([128, 3, 8, 128]), ALU.add, ["tb", "tm"], ["tb"])
    P.ts(BM, tb, 8.0, None, ALU.mult, None, ["tb"], ["BM"])
    tsk = f32v(20 * KiB, DEPTH, 8)
    P.dma(tsk, sinkb, "ctsk", [], ["tsk"])
    P.act(ESK, tsk, AF.Exp, ["tsk"], ["ESK"])
    P.barrier()

    dbg_slot = {name: i for i, name in enumerate(dbg_items)}

    def wload(dst, srcw, c0, c1, key, wkey, d0=0, extra_w=()):
        nk = dst.shape[1]
        for kc in range(nk):
            P.dma(dst[:, kc, d0:d0 + (c1 - c0)], srcw[kc * 128:(kc + 1) * 128, c0:c1], key, [], [wkey] + list(extra_w), eng="pool")

    def dump(name, ap, keys, n):
        if name in dbg_slot:
            i = dbg_slot[name]
            t = f32v(170 * KiB, n) if False else None
            P.dma(dbg[i][0:ap.shape[0], 0:n], ap, "dbg", keys, ["dbgout"])

    import os
    PCSTAGE = int(os.environ.get("K_PCSTAGE", "9"))

    def s5_precompute(l):
        o = 0
        def nx(nbytes):
            nonlocal o
            r = o
            o += (nbytes + 63) // 64 * 64
            return r
        sm = f32v(nx(128 * 10), 10, 32)
        LAM = f32v(nx(384), 3, 32)
        BC = f32v(nx(8192), 4, 32, 16)
        DB = f32v(nx(8192), 32, 64, p0=0, pn=4)
        P.dma(LAM, s5a[l], "pcl", [], ["LAM"])
        P.dma(BC, s5b[l], "pcb", [], ["BC"])
        P.dma(DB, s5d[l], "pcd", [], ["DB"])
        LR, LI, LDT = LAM[:, 0, :], LAM[:, 1, :], LAM[:, 2, :]
        BR, BI, CR, CI = BC[:, 0], BC[:, 1], BC[:, 2], BC[:, 3]
        DT = f32v(nx(128), 32)
        P.act(DT, LDT, AF.Exp, ["LAM"], ["DT"])
        P.tt(XS, LR, DT, ALU.mult, ["LAM", "DT"], ["XS"])
        P.tt(THS, LI, DT, ALU.mult, ["LAM", "DT"], ["THS"])
        NW = 32 * NTK
        T0 = f32v(nx(NW * 4), 32, NTK)
        T1 = V(ai32, nx(NW * 4) // 4, 32, NTK)
        T2 = f32v(nx(NW * 4), 32, NTK)
        T3 = f32v(nx(NW * 4), 32, NTK)
        T4 = f32v(nx(NW * 4), 32, NTK)

        def bc_g(tab, K):
            return tab.unsqueeze(1).to_broadcast([128, 32, K])

        def bc_k(v, K):
            return v.unsqueeze(2).to_broadcast([128, 32, K])

        def sincos(ANG, TI, TF, SINO, COSO, pre):
            k = [pre + x for x in ("ang", "ti", "tf", "sin", "cos")]
            P.ts(TI, ANG, 1.0 / TWO_PI, None, ALU.mult, None, [k[0]], [k[1]])
            P.copy(TF, TI, [k[1]], [k[2]])
            P.stt(ANG, TF, -TWO_PI, ANG, ALU.mult, ALU.add, [k[2], k[0]], [k[0]])
            P.ts(ANG, ANG, PI_LO, -PI_LO, ALU.min, ALU.max, [k[0]], [k[0]])
            P.act(SINO, ANG, AF.Sin, [k[0]], [k[3]])
            P.ts(TF, ANG, math.pi / 2, None, ALU.is_gt, None, [k[0]], [k[2]])
            P.stt(ANG, TF, -TWO_PI, ANG, ALU.mult, ALU.add, [k[2], k[0], k[3]], [k[0]])
            P.ts(ANG, ANG, math.pi / 2, PI_LO, ALU.add, ALU.min, [k[0]], [k[0]])
            P.act(COSO, ANG, AF.Sin, [k[0]], [k[4]])

        P.tt(T0, bc_k(THS, NTK), bc_g(NTAB, NTK), ALU.mult, ["THS", "CST"], ["p_ang"])
        sincos(T0, T1, T2, T3, T4, "p_")
        P.tt(T0, bc_k(XS, NTK), bc_g(NTAB, NTK), ALU.mult, ["XS", "CST", "p_cos"], ["p_ang"])
        P.act(T2, T0, AF.Exp, ["p_ang"], ["p_tf"])
        P.tt(T4, T2, T4, ALU.mult, ["p_tf", "p_cos"], ["PR"])
        P.tt(T3, T2, T3, ALU.mult, ["p_tf", "p_sin"], ["PI"])
        PR, PI_ = T4, T3
        if PCSTAGE < 2:
            P.barrier(); return
        nr, den, u1, u2, cre, cim = sm[:, 0], sm[:, 1], sm[:, 2], sm[:, 3], sm[:, 4], sm[:, 5]
        abr, abi = PR[:, :, 76], PI_[:, :, 76]
        P.ts(nr, abr, -1.0, None, ALU.add, None, ["PR"], ["nr"])
        P.tt(u1, LR, LR, ALU.mult, ["LAM"], ["u1"])
        P.tt(u2, LI, LI, ALU.mult, ["LAM"], ["u2"])
        P.tt(den, u1, u2, ALU.add, ["u1", "u2"], ["den"])
        P.add("dve", lambda e: e.reciprocal(out=den, in_=den), ["den"], ["den"])
        P.tt(u1, nr, LR, ALU.mult, ["nr", "LAM", "den"], ["u1"])
        P.tt(u2, abi, LI, ALU.mult, ["PI", "LAM", "den"], ["u2"])
        P.tt(cre, u1, u2, ALU.add, ["u1", "u2"], ["cre"])
        P.tt(cre, cre, den, ALU.mult, ["cre", "den"], ["cre"])
        P.tt(u1, abi, LR, ALU.mult, ["PI", "LAM", "cre"], ["u1"])
        P.tt(u2, nr, LI, ALU.mult, ["nr", "LAM", "cre"], ["u2"])
        P.tt(cim, u1, u2, ALU.subtract, ["u1", "u2"], ["cim"])
        P.tt(cim, cim, den, ALU.mult, ["cim", "den"], ["cim"])
        BB = f32v(nx(4096), 2, 32, 16)
        bbr, bbi = BB[:, 0], BB[:, 1]
        W1 = f32v(nx(2048), 32, 16)
        W2 = f32v(nx(2048), 32, 16)

        def bc_h(v):
            return v.unsqueeze(2).to_broadcast([128, 32, 16])

        P.tt(W1, bc_h(cre), BR, ALU.mult, ["cre", "BC"], ["W1"])
        P.tt(W2, bc_h(cim), BI, ALU.mult, ["cim", "BC"], ["W2"])
        P.tt(bbr, W1, W2, ALU.subtract, ["W1", "W2"], ["bbr"])
        P.tt(W1, bc_h(cre), BI, ALU.mult, ["cre", "BC", "bbr"], ["W1"])
        P.tt(W2, bc_h(cim), BR, ALU.mult, ["cim", "BC", "bbr"], ["W2"])
        P.tt(bbi, W1, W2, ALU.add, ["W1", "W2"], ["bbi"])
        K0o = b16v(nx(4096), 4, 32, 16)
        P.copy(K0o[:, 0], bbr, ["bbr"], ["K0o"])
        P.copy(K0o[:, 1], bbi, ["bbi"], ["K0o"])
        P.copy(K0o[:, 2], CR, ["BC"], ["K0o"])
        P.ts(K0o[:, 3], CI, -1.0, None, ALU.mult, None, ["BC"], ["K0o"])
        X4 = b16v(nx(16384), 2, 32, 4, 4, 8)
        Y1 = b16v(nx(8192), 2, 32, 4, 16)
        BT1 = f32v(nx(32768), 4, 2048)
        tA = BT1[:, 0:2].rearrange("p a (g k h) -> p (a g) k h", g=16, k=8, h=16)
        tB = BT1[:, 2:4].rearrange("p a (g k h) -> p (a g) k h", g=16, k=8, h=16)
        kA, kB = ["bt0", "bt1"], ["bt2", "bt3"]

        def g4(tabl):
            return tabl[:, :, 64:72].unsqueeze(3).to_broadcast([128, 32, 8, 16])

        def bb8(v):
            return v.unsqueeze(2).to_broadcast([128, 32, 8, 16])

        P.tt(tA, g4(PR), bb8(bbr), ALU.mult, ["PR", "bbr"], kA)
        P.tt(tB, g4(PI_), bb8(bbi), ALU.mult, ["PI", "bbi"], kB)
        for ht in range(4):
            P.tt(X4[:, 0, :, ht].rearrange("p g l k -> p g k l"), tA[:, :, :, ht * 4:(ht + 1) * 4], tB[:, :, :, ht * 4:(ht + 1) * 4], ALU.subtract, kA + kB, ["X4"])
        P.tt(tA, g4(PR), bb8(bbi), ALU.mult, ["PR", "bbi"], kA)
        P.tt(tB, g4(PI_), bb8(bbr), ALU.mult, ["PI", "bbr"], kB)
        for ht in range(4):
            P.tt(X4[:, 1, :, ht].rearrange("p g l k -> p g k l"), tA[:, :, :, ht * 4:(ht + 1) * 4], tB[:, :, :, ht * 4:(ht + 1) * 4], ALU.add, kA + kB, ["X4"])
        tC = BT1[:, 0:1].rearrange("p a (g k h) -> p (a g) k h", g=32, k=4, h=16)
        tD = BT1[:, 1:2].rearrange("p a (g k h) -> p (a g) k h", g=32, k=4, h=16)

        def g1(tabl):
            return tabl[:, :, 72:76].unsqueeze(3).to_broadcast([128, 32, 4, 16])

        def c4(v):
            return v.unsqueeze(2).to_broadcast([128, 32, 4, 16])

        P.tt(tC, g1(PR), c4(CR), ALU.mult, ["PR", "BC"], ["bt0"])
        P.tt(tD, g1(PI_), c4(CI), ALU.mult, ["PI", "BC"], ["bt1"])
        P.tt(Y1[:, 0], tC, tD, ALU.subtract, ["bt0", "bt1"], ["Y1"])
        P.tt(tC, g1(PI_), c4(CR), ALU.mult, ["PI", "BC"], ["bt0"])
        P.tt(tD, g1(PR), c4(CI), ALU.mult, ["PR", "BC"], ["bt1"])
        P.stt(Y1[:, 1], tC, -1.0, tD, ALU.mult, ALU.subtract, ["bt0", "bt1"], ["Y1"])
        if PCSTAGE < 3:
            P.barrier(); return
        KCsb = [b16v(nx(4096), 8, 256, p0=0, pn=32), b16v(nx(4096), 8, 256, p0=0, pn=32)]
        Y1z = V(a16, (T0.offset * 2), 2, 2, 32, 64)
        P.memset(Y1z, 0.0, ["Y1z"])
        for pl in range(2):
            P.copy(Y1z[0:64, 0, pl], Y1[0:64, pl].rearrange("p g k h -> p g (k h)"), ["Y1", "Y1z"], ["Y1zf"])
            P.copy(Y1z[64:128, 1, pl], Y1[64:128, pl].rearrange("p g k h -> p g (k h)"), ["Y1", "Y1z"], ["Y1zb"])
        GS = 4 * 64 * 4 * 16
        it = 0
        for d in range(2):
            for q in range(4):
                ks = KCsb[it % 2]
                kk = "KCsb%d" % (it % 2)
                it += 1
                for pr in range(4):
                    bank = pr
                    pk = "ps%d" % bank
                    for g2 in range(2):
                        g = q * 8 + pr * 2 + g2
                        def f(e, g=g, g2=g2, d=d, bank=bank):
                            ins = None
                            for ht in range(4):
                                oap = ps32(bank)[0:32, g2 * 256:(g2 + 1) * 256].rearrange("p (c t) -> p c t", t=4)[:, :, ht]
                                e.matmul(oap, X4[:, 0, g, ht].rearrange("p l k -> p (l k)"), Y1z[:, d, 0, g], start=True, stop=False)
                                ins = e.matmul(oap, X4[:, 1, g, ht].rearrange("p l k -> p (l k)"), Y1z[:, d, 1, g], start=False, stop=True)
                            return ins
                        P.pe(f, ["X4", "Y1zf", "Y1zb"], [pk])
                    P.act(ks[:, pr * 2:pr * 2 + 2, :], ps32(bank)[0:32, :].rearrange("p (g c) -> p g c", g=2), AF.Copy, [pk], [kk])
                m0 = 31 if d == 1 else 0
                for hl in range(4):
                    dst = AP(KD.tensor, q * 8 * GS + hl * 4096 + m0 * 64, [[256, 8], [GS, 8], [1, 256]])
                    P.dma(dst, ks[hl * 8:(hl + 1) * 8], "kd", [kk], ["KDw%d_%d_%d" % (d, q, hl)])
        for g in range(32):
            bank = 4 + g // 8
            def f(e, g=g, bank=bank):
                ins = None
                for ht in range(4):
                    oap = ps32(bank)[0:4, (g % 8) * 64:(g % 8 + 1) * 64].rearrange("p (o t) -> p o t", t=4)[:, :, ht]
                    e.matmul(oap, K0o[:, 0, g, ht * 4:(ht + 1) * 4], K0o[:, 2, g], start=True, stop=False)
                    ins = e.matmul(oap, K0o[:, 1, g, ht * 4:(ht + 1) * 4], K0o[:, 3, g], start=False, stop=True)
                return ins
            P.pe(f, ["K0o"], ["ps%d" % bank])
        K0sb = b16v(nx(4096), 32, 64, p0=0, pn=4)
        for q in range(4):
            P.tt(K0sb[:, q * 8:(q + 1) * 8, :], ps32(4 + q)[0:4, :].rearrange("p (g c) -> p g c", g=8), DB[:, q * 8:(q + 1) * 8, :], ALU.add,
                 ["ps%d" % (4 + q), "DB"], ["K0sb"])
        dst = AP(KD.tensor, 31 * 64, [[4096, 4], [GS, 32], [1, 64]])
        P.dma(dst, K0sb, "kd", ["K0sb"], ["KD"] + ["KDw%d_%d_%d" % (d_, q_, h_) for d_ in range(2) for q_ in range(4) for h_ in range(4)])
        if PCSTAGE < 4:
            P.barrier(); return
        QG = 4
        Zb = b16v(nx(8192), 2, QG, 512)
        WAsb = [b16v(nx(2048), 8, 128), b16v(nx(2048), 8, 128)]
        WCsb = b16v(nx(8192), QG, 2, 512)
        t1 = BT1[:, 0].rearrange("p (g h j) -> p g h j", g=QG, h=16, j=32)
        t2 = BT1[:, 1].rearrange("p (g h j) -> p g h j", g=QG, h=16, j=32)
        t3 = BT1[:, 2].rearrange("p (g j h) -> p g j h", g=QG, j=32, h=16)
        t4 = BT1[:, 3].rearrange("p (g j h) -> p g j h", g=QG, j=32, h=16)
        for q in range(32 // QG):
            gs = slice(q * QG, (q + 1) * QG)

            def Gj(tabl):
                return tabl[:, gs, 0:32].unsqueeze(2).to_broadcast([128, QG, 16, 32])

            def bbj(v):
                return v[:, gs, :].unsqueeze(3).to_broadcast([128, QG, 16, 32])

            Zr = Zb[:, 0].rearrange("p g (h j) -> p g h j", h=16)
            Zi = Zb[:, 1].rearrange("p g (h j) -> p g h j", h=16)
            P.tt(t1, Gj(PR), bbj(bbr), ALU.mult, ["PR", "bbr"], ["bt0"])
            P.tt(t2, Gj(PI_), bbj(bbi), ALU.mult, ["PI", "bbi"], ["bt1"])
            P.tt(Zr, t1, t2, ALU.subtract, ["bt0", "bt1"], ["Zb"])
            P.tt(t1, Gj(PR), bbj(bbi), ALU.mult, ["PR", "bbi"], ["bt0"])
            P.tt(t2, Gj(PI_), bbj(bbr), ALU.mult, ["PI", "bbr"], ["bt1"])
            P.tt(Zi, t1, t2, ALU.add, ["bt0", "bt1"], ["Zb"])
            for gl in range(QG):
                g = q * QG + gl
                bank = 5 + (g % 2)
                pk = "ps%d" % bank
                def f(e, gl=gl, bank=bank):
                    ins = None
                    for ht in range(4):
                        for pl in range(2):
                            ins = e.transpose(ps16(bank)[:, (ht * 2 + pl) * 128:(ht * 2 + pl + 1) * 128],
                                              Zb[:, pl, gl, ht * 128:(ht + 1) * 128], ident)
                    return ins
                P.pe(f, ["Zb", "ident"], [pk])
                wk = "WAsb%d" % (g % 2)
                P.act(WAsb[g % 2].rearrange("p a b -> p (a b)"), ps16(bank), AF.Copy, [pk], [wk])
                P.dma(WAs[g], WAsb[g % 2].rearrange("p a b -> p (a b)"), "wa", [wk], ["WAs"])

            def Hj(tabl):
                return tabl[:, gs, 32:64].unsqueeze(3).to_broadcast([128, QG, 32, 16])

            def cj(v):
                return v[:, gs, :].unsqueeze(2).to_broadcast([128, QG, 32, 16])

            WCr = WCsb[:, :, 0].rearrange("p g (j h) -> p g j h", j=32)
            WCn = WCsb[:, :, 1].rearrange("p g (j h) -> p g j h", j=32)
            P.tt(t3, Hj(PR), cj(CR), ALU.mult, ["PR", "BC"], ["bt2"])
            P.tt(t4, Hj(PI_), cj(CI), ALU.mult, ["PI", "BC"], ["bt3"])
            P.tt(WCr, t3, t4, ALU.subtract, ["bt2", "bt3"], ["WCsb"])
            P.tt(t3, Hj(PI_), cj(CR), ALU.mult, ["PI", "BC"], ["bt2"])
            P.tt(t4, Hj(PR), cj(CI), ALU.mult, ["PR", "BC"], ["bt3"])
            P.stt(WCn, t3, -1.0, t4, ALU.mult, ALU.subtract, ["bt2", "bt3"], ["WCsb"])
            P.dma(WCs[q * QG:(q + 1) * QG].rearrange("g p f -> p g f"), WCsb.rearrange("p g a f -> p g (a f)"), "wc", ["WCsb"], ["WCs"])
        if PCSTAGE < 5:
            P.barrier(); return
        P.barrier()
        o = 2048
        TH32 = sm[:, 6]
        tq_i = V(ai32, nx(128) // 4, 32)
        tq_f = f32v(nx(128), 32)
        P.ts(TH32, THS, 32.0, None, ALU.mult, None, ["THS"], ["TH32"])
        P.ts(tq_i, TH32, 1.0 / TWO_PI, None, ALU.mult, None, ["TH32"], ["tqi"])
        P.copy(tq_f, tq_i, ["tqi"], ["tqf"])
        P.stt(TH32, tq_f, -TWO_PI, TH32, ALU.mult, ALU.add, ["tqf", "TH32"], ["TH32"])
        NS = 32 * NCH
        S0 = f32v(nx(NS * 4), 32, NCH)
        S1 = V(ai32, nx(NS * 4) // 4, 32, NCH)
        S2 = f32v(nx(NS * 4), 32, NCH)
        S3 = f32v(nx(NS * 4), 32, NCH)
        S4 = f32v(nx(NS * 4), 32, NCH)
        assert o <= AW * 4, o
        P.tt(S0, TH32.unsqueeze(2).to_broadcast([128, 32, NCH]), CIDX.unsqueeze(1).to_broadcast([128, 32, NCH]), ALU.mult,
             ["TH32", "CST"], ["s_ang"])
        sincos(S0, S1, S2, S3, S4, "s_")
        P.ts(S3, S3, SGN, None, ALU.mult, None, ["s_sin", "CST"], ["s_sin"])
        P.dma(TABd[0].rearrange("p (g c) -> p g c", g=32), S4, "tab", ["s_cos"], ["TABd"])
        P.dma(TABd[1].rearrange("p (g c) -> p g c", g=32), S3, "tab", ["s_sin"], ["TABd"])
        R = sm[:, 7]
        P.act(R, XS, AF.Exp, ["XS"], ["R"], scale=32.0)
        P.copy(S2, R.unsqueeze(2).to_broadcast([128, 32, NCH]), ["R", "s_tf", "s_ang"], ["RZ"])
        P.memset(S2[0:64, :, 0:1], 0.0, ["RZ"])
        P.memset(S2[64:128, :, NCH - 1:NCH], 0.0, ["RZ"])
        P.dma(TABd[2].rearrange("p (g c) -> p g c", g=32), S2, "tab", ["RZ"], ["TABd"])
        P.barrier()

    O_UT = 0
    O_QT = 32 * KiB
    O_KV = 64 * KiB
    O_YB = 96 * KiB
    O_YA = 128 * KiB
    O_W = 160 * KiB
    UT = b16v(O_UT, 4, NCH, 32)
    QT = b16v(O_QT, 4, L)
    ZT = QT
    KT = [b16v(O_KV, 2, L), b16v(O_W, 2, L)]
    VA = b16v(O_KV + 2 * L * 2, NB, 2, 128)
    YTOK = b16v(O_KV, 4, 32, 8, 16)
    YB = b16v(O_YB, 4, L)
    YA = b16v(O_YA, 4, L)

    def rmsnorm_group(hT, N, gi, SQ, HN, RSTD, bank, pfx, eng2="dve"):
        pk = "ps%d" % bank
        P.act(SQ, hT, AF.Square, [pfx + "h"], [pfx + "sq"])
        def f(e):
            ins = None
            for kc in range(KC):
                ins = e.matmul(ps32(bank)[:, 0:N], ONES, SQ[:, kc, :], start=(kc == 0), stop=(kc == KC - 1))
            return ins
        P.pe(f, [pfx + "sq", "ONES"], [pk])
        P.act(RSTD, ps32(bank)[:, 0:N], AF.Sqrt, [pk], [pfx + "rstd"], bias=EPSB, scale=1.0 / D)
        P.add("dve", lambda e: e.reciprocal(out=RSTD, in_=RSTD), [pfx + "rstd"], [pfx + "rstd"])
        for kc in range(KC):
            P.stt(HN[:, kc, :], hT[:, kc, :], GN[:, gi, kc:kc + 1], RSTD, ALU.mult, ALU.mult,
                  [pfx + "h", pfx + "rstd", "GN"], [pfx + "hn"], eng=("dve" if kc % 2 == 0 else eng2))

    EPSB = V(s32, 4000, 1)
    P.ops.insert(0, dict(eng="dve", fn=lambda e: e.memset(EPSB, 1e-6), r=("PH",), w=("EPSB",), dk=None, sig=False))

    def phase_A(l):
        O_WA1 = O_YB
        WA1 = b16v(O_WA1, KC, 1408)
        wload(WA1, w_in[l], 0, 1408, "wA", "WA1")
        o = O_WA1 + 22528
        HT = [f32v(o, KC, 512), f32v(o, KC, 512)]
        o += 16 * KiB
        SQ = b16v(o, KC, 512); o += 8 * KiB
        HN = b16v(o, KC, 512); o += 8 * KiB
        RSTD = f32v(o, 512); o += 2 * KiB
        UTMP = b16v(o, 512); o += 1 * KiB
        assert o <= O_W
        P.memset(KT[0][64:128], 0.0, ["KT"])
        P.memset(KT[1][0:64], 0.0, ["KT"])
        src = xT if l == 0 else hscr
        for tg in range(NG):
            hk = "A0h"
            hT = HT[tg % 2]
            P.dma(hT, src[tg], "hA0", [], [hk])
            pf = "A%d" % (tg % 2)
            P.act(SQ, hT, AF.Square, [hk], ["Asq"])
            def f(e):
                ins = None
                for kc in range(KC):
                    ins = e.matmul(ps32(0), ONES, SQ[:, kc, :], start=(kc == 0), stop=(kc == KC - 1))
                return ins
            P.pe(f, ["Asq", "ONES"], ["ps0"])
            P.act(RSTD, ps32(0), AF.Sqrt, ["ps0"], ["Arstd"], bias=EPSB, scale=1.0 / D)
            P.add("dve", lambda e: e.reciprocal(out=RSTD, in_=RSTD), ["Arstd"], ["Arstd"])
            for kc in range(KC):
                P.stt(HN[:, kc, :], hT[:, kc, :], GN[:, l, kc:kc + 1], RSTD, ALU.mult, ALU.mult,
                      [hk, "Arstd", "GN"], ["Ahn"])
            for t in range(10):
                bank = 1 + (t % 4)
                pk = "ps%d" % bank
                def f(e, t=t, bank=bank):
                    ins = None
                    for kc in range(KC):
                        ins = e.matmul(ps32(bank), WA1[:, kc, t * 128:(t + 1) * 128], HN[:, kc, :], start=(kc == 0), stop=(kc == KC - 1))
                    return ins
                P.pe(f, ["WA1", "Ahn"], [pk])
                tok = slice(tg * 512, (tg + 1) * 512)
                if t < 4:
                    P.act(UTMP, ps32(bank), AF.Copy, [pk], ["UTMP"])
                    dst = UT[:, t, tg * 16:(tg + 1) * 16, :].rearrange("p c g -> p (c g)")
                    P.add("dve", lambda e, dst=dst: e.transpose(out=dst, in_=UTMP), ["UTMP"], ["UT"])
                elif t < 8:
                    P.act(QT[:, t - 4, tok], ps32(bank), AF.Copy, [pk], ["QT"])
                else:
                    P.act(KT[0][0:64, t - 8, tok], ps32(bank)[0:64, :], AF.Copy, [pk], ["KTa"])
                    P.act(KT[1][64:128, t - 8, tok], ps32(bank)[64:128, :], AF.Copy, [pk], ["KTb"])
            for tt_ in range(4):
                bank = 5 + (tt_ % 2)
                pk = "ps%d" % bank
                def f(e, tt_=tt_, bank=bank):
                    ins = None
                    for kc in range(KC):
                        ins = e.matmul(ps32(bank)[:, 0:128], HN[:, kc, tt_ * 128:(tt_ + 1) * 128], WA1[:, kc, 1280:1408],
                                       start=(kc == 0), stop=(kc == KC - 1))
                    return ins
                P.pe(f, ["WA1", "Ahn"], [pk])
                blk = tg * 4 + tt_
                P.copy(VA[:, blk, :, 0:64], ps32(bank)[:, 0:128].rearrange("p (k d) -> p k d", k=2), [pk], ["VA"])
        P.memset(VA[:, :, :, 64:128], 1.0, ["VA1"])
        P.barrier()

    def phase_ATT(l):
        o = O_YA
        PT = [b16v(o, 512), b16v(o + KiB, 512), b16v(o + 2 * KiB, 512)]
        o += 3 * KiB
        RD = f32v(o, 512, p0=0, pn=64); o += 2 * KiB
        pend = []

        def att_norm(n, kh, obank, ok):
            esk = ESK[64:128, l, kh * 4:(kh + 1) * 4].unsqueeze(2).to_broadcast([64, 4, 128])
            P.tt(RD.rearrange("p (h q) -> p h q", h=4), ps32(obank)[64:128, :].rearrange("p (h q) -> p h q", h=4), esk, ALU.add,
                 [ok, "ESK"], ["RD"])
            P.act(RD, RD, AF.Ln, ["RD"], ["RD"])
            P.act(RD, RD, AF.Exp, ["RD"], ["RD"], scale=-1.0)
            for par in range(2):
                num = ps32(obank)[0:64, :].rearrange("p (a b q) -> p a b q", a=2, b=2)[:, :, par, :]
                rd = RD.rearrange("p (a b q) -> p a b q", a=2, b=2)[:, :, par, :]
                dst = YB[par * 64:(par + 1) * 64, kh * 2:kh * 2 + 2, n * 128:(n + 1) * 128]
                P.tt(dst, num, rd, ALU.mult, [ok, "RD"], ["YB"])

        for n in range(NB):
            for kh in range(2):
                kbs = [kb for kb in (n - 1, n, n + 1) if 0 <= kb < NB]
                obank = 6 + (kh % 2)
                ok = "ps%d" % obank
                def emit_qk(ki, kb, n=n, kh=kh):
                    it = (n * 2 + kh) * 3 + ki
                    lb = it % 4
                    lk = "ps%d" % lb
                    rel = kb - n + 1
                    def f(e, lb=lb, rel=rel, kb=kb, kh=kh, n=n):
                        e.matmul(ps32(lb), ident, BM[:, rel, kh * 4:(kh + 1) * 4, :].rearrange("p h q -> p (h q)"), start=True, stop=False)
                        ins = None
                        for hl in range(4):
                            h = kh * 4 + hl
                            i, half = h // 2, h % 2
                            ins = e.matmul(ps32(lb)[:, hl * 128:(hl + 1) * 128],
                                           KT[half][:, kh, kb * 128:(kb + 1) * 128],
                                           QT[:, i, n * 128:(n + 1) * 128],
                                           start=False, stop=(hl == 3))
                        return ins
                    P.pe(f, ["BM", "ident", "KT", "KTa", "KTb", "QT"], [lk])
                    P.act(PT[it % 3], ps32(lb), AF.Exp, [lk], ["PT%d" % (it % 3)], scale=0.125)

                def emit_pv(ki, kb, n=n, kh=kh, obank=obank, ok=ok, nk=len(kbs)):
                    it = (n * 2 + kh) * 3 + ki
                    P.mm(ps32(obank), VA[:, kb, kh, :], PT[it % 3], ki == 0, ki == nk - 1, ["VA", "VA1", "PT%d" % (it % 3)], [ok])

                for ki, kb in enumerate(kbs):
                    emit_qk(ki, kb)
                    if ki >= 1:
                        emit_pv(ki - 1, kbs[ki - 1])
                emit_pv(len(kbs) - 1, kbs[-1])
                pend.append((n, kh, obank, ok))
                if len(pend) > 1:
                    att_norm(*pend.pop(0))
        while pend:
            att_norm(*pend.pop(0))
        P.barrier()

    def phase_S(l):
        o = O_W
        def nx(nb):
            nonlocal o
            r = o
            o += nb
            return r
        NBW = GB * NCH * 4
        N = GB * NCH
        TAB = [[f32v(nx(NBW), N) for _ in range(3)] for _ in range(2)]
        Mr, Mi = f32v(nx(NBW), N), f32v(nx(NBW), N)
        assert o <= AW * 4
        oy = O_YA
        WAb = [b16v(oy, 4, 2, 128), b16v(oy + 2 * KiB, 4, 2, 128)]; oy += 4 * KiB
        Wr, Wi = f32v(oy, N), f32v(oy + NBW, N); oy += 2 * NBW
        oy = O_YA + 16 * KiB
        SS = b16v(oy, 2, 32, NCH)
        oy += 2 * 32 * NCH * 2
        assert oy <= O_W
        T1 = f32v(O_QT, N)
        T2 = f32v(O_QT + NBW, N)
        P.memset(SS[0:64, :, :, 0:1], 0.0, ["SS"])
        P.memset(SS[64:128, :, :, NCH - 1:NCH], 0.0, ["SS"])

        def s2_front(b):
            par = b % 2
            for i in range(3):
                P.dma(TAB[par][i], TABd[i][:, b * N:(b + 1) * N], "tabl%d" % par, ["TABd"], ["tabs%d" % par])
            for gl in range(GB):
                g = b * GB + gl
                sl = g % 2
                P.dma(WAb[sl].rearrange("p a b c -> p (a b c)"), WAs[g], "wal%d" % sl, ["WAs"], ["WAb%d" % sl])
                def f(e, g=g, gl=gl, sl=sl, par=par):
                    ins = None
                    for pl in range(2):
                        for ht in range(4):
                            ins = e.matmul(ps32(2 * par + pl)[:, gl * NCH:(gl + 1) * NCH], WAb[sl][:, ht, pl, :], UT[:, ht, :, g],
                                           start=(ht == 0), stop=(ht == 3))
                    return ins
                P.pe(f, ["WAb%d" % sl, "UT"], ["ps%d" % (2 * par), "ps%d" % (2 * par + 1)])

        def s2_back(b):
            par = b % 2
            COSb, SINb, RZb = TAB[par]
            tk = "tabs%d" % par
            kr, ki_ = "ps%d" % (2 * par), "ps%d" % (2 * par + 1)
            Lr, Li = ps32(2 * par)[:, 0:N], ps32(2 * par + 1)[:, 0:N]
            P.tt(T1, Lr, COSb, ALU.mult, [kr, tk], ["T1"])
            P.tt(T2, Li, SINb, ALU.mult, [ki_, tk], ["T2"])
            P.tt(Mr, T1, T2, ALU.add, ["T1", "T2"], ["Mr"])
            P.tt(T1, Li, COSb, ALU.mult, [ki_, tk, "Mr"], ["T1"])
            P.tt(T2, Lr, SINb, ALU.mult, [kr, tk, "Mr"], ["T2"])
            P.tt(Mi, T1, T2, ALU.subtract, ["T1", "T2"], ["Mi"])
            for (Mx, Wx, mk, wk) in ((Mr, Wr, "Mr", "Wr"), (Mi, Wi, "Mi", "Wi")):
                P.add("dve", lambda e, Mx=Mx, Wx=Wx, RZb=RZb: e.tensor_tensor_scan(out=Wx[0:64], data0=RZb[0:64], data1=Mx[0:64], initial=0.0,
                                                                                   op0=ALU.mult, op1=ALU.add), [mk, tk], [wk])
                def rev(ap):
                    full = ap[64:128]
                    return AP(full.tensor, full.offset + N - 1, [list(full.ap[0]), [-1, N]])
                P.add("dve", lambda e, Mx=Mx, Wx=Wx, rev=rev, RZb=RZb: e.tensor_tensor_scan(out=rev(Wx), data0=rev(RZb), data1=rev(Mx), initial=0.0,
                                                                                            op0=ALU.mult, op1=ALU.add), [mk, tk], [wk])
            def g3(ap, p0):
                return ap[p0:p0 + 64].rearrange("p (g c) -> p g c", g=GB)
            gsl = slice(b * GB, (b + 1) * GB)
            P.tt(T1, Wr, COSb, ALU.mult, ["Wr", tk, "Mi"], ["T1"])
            P.tt(T2, Wi, SINb, ALU.mult, ["Wi", tk, "Mi"], ["T2"])
            if NCH > 1:
                P.tt(SS[0:64, 0, gsl, 1:NCH], g3(T1, 0)[:, :, 0:NCH - 1], g3(T2, 0)[:, :, 0:NCH - 1], ALU.subtract, ["T1", "T2"], ["SS"])
                P.tt(SS[64:128, 0, gsl, 0:NCH - 1], g3(T1, 64)[:, :, 1:NCH], g3(T2, 64)[:, :, 1:NCH], ALU.subtract, ["T1", "T2"], ["SS"])
            P.tt(T1, Wi, COSb, ALU.mult, ["Wi", tk, "SS"], ["T1"])
            P.tt(T2, Wr, SINb, ALU.mult, ["Wr", tk, "SS"], ["T2"])
            if NCH > 1:
                P.tt(SS[0:64, 1, gsl, 1:NCH], g3(T1, 0)[:, :, 0:NCH - 1], g3(T2, 0)[:, :, 0:NCH - 1], ALU.add, ["T1", "T2"], ["SS"])
                P.tt(SS[64:128, 1, gsl, 0:NCH - 1], g3(T1, 64)[:, :, 1:NCH], g3(T2, 64)[:, :, 1:NCH], ALU.add, ["T1", "T2"], ["SS"])

        s2_front(0)
        for b in range(NBATCH):
            if b + 1 < NBATCH:
                s2_front(b + 1)
            s2_back(b)
        NS3, PD3 = 4, 3
        TZs = [b16v(O_QT + 8 * KiB + i * 4 * KiB, 512, 4) for i in range(NS3)]
        WCs_ = [b16v(O_QT + 24 * KiB + i * 2 * KiB, 2, 512) for i in range(NS3)]

        def s3_load(g):
            sl = g % NS3
            srcap = AP(KD.tensor, g * 16384, [[4096, 4], [64, 32], [1, 2048]])
            P.dma(TZs[sl].rearrange("p c t -> p (c t)"), srcap, "tz%d" % sl, ["KD"], ["TZb%d" % sl], eng=("sp" if g % 2 == 0 else "act"))
            P.dma(WCs_[sl].rearrange("p a f -> p (a f)"), WCs[g], "wcl%d" % sl, ["WCs"], ["WCb%d" % sl])

        for g in range(min(PD3, 32)):
            s3_load(g)
        for g in range(32):
            if g + PD3 < 32:
                s3_load(g + PD3)
            sl = g % NS3
            bank = 2 + (g % 2)
            pk = "ps%d" % bank
            def f(e, g=g, sl=sl, bank=bank):
                for ht in range(4):
                    e.matmul(ps32(bank)[0:NCH, :], UT[:, ht, :, g], TZs[sl][:, :, ht], start=(ht == 0), stop=False)
                e.matmul(ps32(bank)[0:NCH, :], SS[:, 0, g, :], WCs_[sl][:, 0, :], start=False, stop=False)
                return e.matmul(ps32(bank)[0:NCH, :], SS[:, 1, g, :], WCs_[sl][:, 1, :], start=False, stop=True)
            P.pe(f, ["UT", "TZb%d" % sl, "SS", "WCb%d" % sl], [pk])
            P.act(YTOK[0:NCH, g // 8, :, g % 8, :], ps32(bank)[0:NCH, :].rearrange("p (j h) -> p j h", j=32), AF.Copy, [pk], ["YTOK"])
        if "ytok" in dbg_slot:
            tmpf = f32v(O_QT, 4096)
            for gq in range(4):
                pass
        it = 0
        for gt in range(4):
            for jb in range(4):
                bank = 4 + (it % 2)
                pk = "ps%d" % bank
                it += 1
                def f(e, gt=gt, jb=jb, bank=bank):
                    ins = None
                    for jj in range(8):
                        jr = jb * 8 + jj
                        src = YTOK[0:NCH, gt, jr].rearrange("p g h -> p (g h)")
                        ins = e.transpose(ps16(bank)[:, jj * NCH:(jj + 1) * NCH], src, ident[0:NCH, 0:NCH])
                    return ins
                P.pe(f, ["YTOK", "ident"], [pk])
                zt = ZT[:, gt, :]
                j0 = 31 - jb * 8
                dst = AP(zt.tensor, zt.offset + j0, [list(zt.ap[0]), [-1, 8], [32, NCH]])
                P.act(dst, ps16(bank)[:, 0:8 * NCH].rearrange("p (j c) -> p j c", j=8), AF.Gelu_apprx_tanh, [pk], ["ZT"])
        P.barrier()
        wload(b16v(O_KV, KC, 2048), w_in[l], 1408, 3456, "wc1g", "WGA")
        WG = b16v(O_UT, 4, 512)
        wload(WG, w_glu[l], 0, 512, "wg", "WG")
        SG = [b16v(O_UT + 4 * KiB, 512), b16v(O_UT + 5 * KiB, 512)]
        it = 0
        for tg in range(NG):
            tok = slice(tg * 512, (tg + 1) * 512)
            for ot in range(4):
                bank = 6 + (it % 2)
                pk = "ps%d" % bank
                sg = SG[it % 2]
                sk = "SG%d" % (it % 2)
                it += 1
                def f(e, ot=ot, bank=bank, tok=tok):
                    ins = None
                    for kt in range(4):
                        ins = e.matmul(ps32(bank), WG[:, kt, ot * 128:(ot + 1) * 128], ZT[:, kt, tok], start=(kt == 0), stop=(kt == 3))
                    return ins
                P.pe(f, ["WG", "ZT"], [pk])
                P.act(sg, ps32(bank), AF.Sigmoid, [pk], [sk])
                P.tt(YA[:, ot, tok], ZT[:, ot, tok], sg, ALU.mult, ["ZT", sk], ["YA"])
        P.barrier()

    def phase_C1(l):
        o = 0
        def nx(nb):
            nonlocal o
            r = o
            o += nb
            return r
        WGA = b16v(O_KV, KC, 2048)
        WBA = b16v(nx(8 * KiB), 4, D)
        WBB = b16v(nx(8 * KiB), 4, D)
        WO = b16v(nx(16 * KiB), KC, D)
        wload(WBA, w_a[l], 0, D, "wc1a", "WBA")
        wload(WBB, w_b[l], 0, D, "wc1b", "WBB")
        wload(WO, w_out[l], 0, D, "wc1o", "WO")
        assert o == 32 * KiB
        HTs = [f32v(nx(16 * KiB), KC, 512), f32v(nx(16 * KiB), KC, 512)]
        assert o == 64 * KiB
        o = O_W
        SQ = b16v(nx(8 * KiB), KC, 512)
        HN = b16v(nx(8 * KiB), KC, 512)
        MG = SQ
        src = xT if l == 0 else hscr
        RSTD = V(s32, 2400, 512)
        TA = V(s32, 2912, 512)
        TM = V(s32, 4096, 512)
        GAs = [V(s16, 2 * 4608 + i * 512, 512) for i in range(2)]
        GBs = [V(s16, 2 * 5120 + i * 512, 512) for i in range(2)]
        P.dma(HTs[0], src[0], "hC0", ["hscr"], ["Ch0"])
        for tg in range(NG):
            tok = slice(tg * 512, (tg + 1) * 512)
            HT = HTs[tg % 2]
            ck = "Ch%d" % (tg % 2)
            if tg + 1 < NG:
                P.dma(HTs[(tg + 1) % 2], src[tg + 1], "hC%d" % ((tg + 1) % 2), ["hscr"], ["Ch%d" % ((tg + 1) % 2)])
            P.act(SQ, HT, AF.Square, [ck], ["Csq"])
            def f(e):
                ins = None
                for kc in range(KC):
                    ins = e.matmul(ps32(0), ONES, SQ[:, kc, :], start=(kc == 0), stop=(kc == KC - 1))
                return ins
            P.pe(f, ["Csq", "ONES"], ["ps0"])
            P.act(RSTD, ps32(0), AF.Sqrt, ["ps0"], ["Crstd"], bias=EPSB, scale=1.0 / D)
            P.add("dve", lambda e: e.reciprocal(out=RSTD, in_=RSTD), ["Crstd"], ["Crstd"])
            for kc in range(KC):
                P.stt(HN[:, kc, :], HT[:, kc, :], GN[:, l, kc:kc + 1], RSTD, ALU.mult, ALU.mult, [ck, "Crstd", "GN"], ["Chn"])
            for ot in range(8):
                sl = ot % 2
                ba, bb_ = 1 + sl, 3 + sl
                def f(e, ot=ot, ba=ba, bb_=bb_):
                    ins = None
                    for kc in range(KC):
                        e.matmul(ps32(ba), WGA[:, kc, ot * 128:(ot + 1) * 128], HN[:, kc, :], start=(kc == 0), stop=(kc == KC - 1))
                    for kc in range(KC):
                        ins = e.matmul(ps32(bb_), WGA[:, kc, (8 + ot) * 128:(9 + ot) * 128], HN[:, kc, :], start=(kc == 0), stop=(kc == KC - 1))
                    return ins
                P.pe(f, ["WGA", "Chn"], ["ps%d" % ba, "ps%d" % bb_])
                P.act(GAs[sl], ps32(ba), AF.Sigmoid, ["ps%d" % ba], ["GA%d" % sl])
                P.act(GBs[sl], ps32(bb_), AF.Sigmoid, ["ps%d" % bb_], ["GB%d" % sl])
                def f(e, ot=ot, tok=tok):
                    ins = None
                    for kt in range(4):
                        e.matmul(ps32(5), WBA[:, kt, ot * 128:(ot + 1) * 128], YA[:, kt, tok], start=(kt == 0), stop=(kt == 3))
                    for kt in range(4):
                        ins = e.matmul(ps32(6), WBB[:, kt, ot * 128:(ot + 1) * 128], YB[:, kt, tok], start=(kt == 0), stop=(kt == 3))
                    return ins
                P.pe(f, ["WBA", "WBB", "YA", "YB"], ["ps5", "ps6"])
                P.tt(TA, ps32(5), GAs[sl], ALU.mult, ["ps5", "GA%d" % sl], ["TA"])
                P.tt(TM, ps32(6), GBs[sl], ALU.mult, ["ps6", "GB%d" % sl], ["TM"])
                P.tt(MG[:, ot, :], TM, TA, ALU.add, ["TM", "TA"], ["Csq"], eng="pool")
            for ot in range(8):
                bank = 5 + (ot % 3)
                pk = "ps%d" % bank
                def f(e, ot=ot, bank=bank):
                    ins = None
                    for kc in range(KC):
                        ins = e.matmul(ps32(bank), WO[:, kc, ot * 128:(ot + 1) * 128], MG[:, kc, :], start=(kc == 0), stop=(kc == KC - 1))
                    return ins
                P.pe(f, ["WO", "Csq"], [pk])
                P.tt(HT[:, ot, :], HT[:, ot, :], ps32(bank), ALU.add, [pk, ck, "Chn"], [ck + "2"])
            P.dma(hscr[tg], HT, "hCs", [ck + "2", ck], ["hscr"])
        P.barrier()

    def phase_C2(l):
        o = 0
        def nx(nb):
            nonlocal o
            r = o
            o += nb
            return r
        WFG = b16v(nx(KC * DFF * 2), KC, DFF)
        WFU = b16v(nx(KC * DFF * 2), KC, DFF)
        WFD = b16v(nx(NF * D * 2), NF, D)
        qi = 0
        for (dst, srcb, nk, wk) in ((WFG, wfg_b, KC, "WFG"), (WFU, wfu_b, KC, "WFU"), (WFD, wfd_b, NF, "WFD")):
            for kc in range(nk):
                P.dma(dst[:, kc, :], srcb[kc * 128:(kc + 1) * 128, :], "l" + wk, ["WFb"], [wk], eng=("sp" if qi % 2 == 0 else "act"))
                qi += 1
        TN = 256
        HT = [f32v(nx(8 * KiB), KC, TN), f32v(nx(8 * KiB), KC, TN)]
        SQ = b16v(nx(4 * KiB), KC, TN)
        HNs = [b16v(nx(4 * KiB), KC, TN), b16v(nx(4 * KiB), KC, TN)]
        FF = b16v(nx(NF * TN * 2), NF, TN)
        SL = [b16v(nx(512), TN), b16v(nx(512), TN)]
        assert o <= AW * 4, o
        RSTD = V(s32, 2400, TN)
        last = (l == DEPTH - 1)
        NT2 = NG * 2

        def load(t2):
            tg, hf = t2 // 2, t2 % 2
            P.dma(HT[t2 % 2], hscr[tg][:, :, hf * TN:(hf + 1) * TN], "hF%d" % (t2 % 2), ["hscr"], ["F%dh" % (t2 % 2)])

        def norm(t2):
            ht = HT[t2 % 2]
            hk = "F%dh" % (t2 % 2)
            HN = HNs[t2 % 2]
            nk = "Fhn%d" % (t2 % 2)
            P.act(SQ, ht, AF.Square, [hk], ["Fsq"])
            P.pe(f_final_ss(SQ, TN), ["Fsq", "ONES"], ["ps0"])
            P.act(RSTD, ps32(0)[:, 0:TN], AF.Sqrt, ["ps0"], ["Frstd"], bias=EPSB, scale=1.0 / D)
            P.add("dve", lambda e: e.reciprocal(out=RSTD, in_=RSTD), ["Frstd"], ["Frstd"])
            for kc in range(KC):
                P.stt(HN[:, kc, :], ht[:, kc, :], GN[:, 2 + l, kc:kc + 1], RSTD, ALU.mult, ALU.mult, [hk, "Frstd", "GN"], [nk])

        load(0)
        norm(0)
        for t2 in range(NT2):
            tg, hf = t2 // 2, t2 % 2
            ht = HT[t2 % 2]
            hk = "F%dh" % (t2 % 2)
            HN = HNs[t2 % 2]
            nk = "Fhn%d" % (t2 % 2)
            if t2 + 1 < NT2:
                load(t2 + 1)
            for fi in range(NF):
                bg = 1 + (fi % 2) * 2
                bu = bg + 1
                def f(e, fi=fi, bg=bg, bu=bu, HN=HN):
                    ins = None
                    for kc in range(KC):
                        e.matmul(ps32(bg)[:, 0:TN], WFG[:, kc, fi * 128:(fi + 1) * 128], HN[:, kc, :], start=(kc == 0), stop=(kc == KC - 1))
                    for kc in range(KC):
                        ins = e.matmul(ps32(bu)[:, 0:TN], WFU[:, kc, fi * 128:(fi + 1) * 128], HN[:, kc, :], start=(kc == 0), stop=(kc == KC - 1))
                    return ins
                P.pe(f, ["WFG", "WFU", nk], ["ps%d" % bg, "ps%d" % bu])
                sl = SL[fi % 2]
                sk = "SL%d" % (fi % 2)
                P.act(sl, ps32(bg)[:, 0:TN], AF.Silu, ["ps%d" % bg], [sk])
                P.tt(FF[:, fi, :], ps32(bu)[:, 0:TN], sl, ALU.mult, ["ps%d" % bu, sk], ["FF"])
            if t2 + 1 < NT2:
                norm(t2 + 1)
            for ot in range(8):
                bank = 5 + (ot % 3)
                pk = "ps%d" % bank
                def f(e, ot=ot, bank=bank):
                    ins = None
                    for fi in range(NF):
                        ins = e.matmul(ps32(bank)[:, 0:TN], WFD[:, fi, ot * 128:(ot + 1) * 128], FF[:, fi, :], start=(fi == 0), stop=(fi == NF - 1))
                    return ins
                P.pe(f, ["WFD", "FF"], [pk])
                P.tt(ht[:, ot, :], ht[:, ot, :], ps32(bank)[:, 0:TN], ALU.add, [pk, hk, nk], [hk + "2"])
            if not last:
                P.dma(hscr[tg][:, :, hf * TN:(hf + 1) * TN], ht, "hFs", [hk + "2", hk], ["hscr"])
            else:
                P.act(SQ, ht, AF.Square, [hk + "2", hk], ["Fsq"])
                P.pe(f_final_ss(SQ, TN), ["Fsq", "ONES"], ["ps0"])
                P.act(RSTD, ps32(0)[:, 0:TN], AF.Sqrt, ["ps0"], ["Frstd"], bias=EPSB, scale=1.0 / D)
                P.add("dve", lambda e: e.reciprocal(out=RSTD, in_=RSTD), ["Frstd"], ["Frstd"])
                for kc in range(KC):
                    P.stt(ht[:, kc, :], ht[:, kc, :], GN[:, 4, kc:kc + 1], RSTD, ALU.mult, ALU.mult, [hk + "2", hk, "Frstd", "GN"], [hk + "3"])
                P.dma(out[t2], ht, "outd", [hk + "3", hk, hk + "2"], ["outd"])
        P.barrier()

    def f_final_ss(SQ, TN):
        def f(e):
            ins = None
            for kc in range(KC):
                ins = e.matmul(ps32(0)[:, 0:TN], ONES, SQ[:, kc, :], start=(kc == 0), stop=(kc == KC - 1))
            return ins
        return f

    def ffn_convert(l):
        for (dst, srcw, nk) in ((wfg_b, w_fg[l], KC), (wfu_b, w_fu[l], KC), (wfd_b, w_fd[l], NF)):
            for kc in range(nk):
                P.dma(dst[kc * 128:(kc + 1) * 128, :], srcw[kc * 128:(kc + 1) * 128, :], "wcv", [], ["WFb"], eng="pool", nobar=True)

    step = 0
    for l in range(DEPTH):
        for ph in (s5_precompute, phase_A, phase_ATT, phase_S, phase_C1, phase_C2):
            step += 1
            if step <= upto:
                ph(l)
                if ph is phase_A:
                    ffn_convert(l)
    if upto < 12:
        for t2 in range(NG * 2):
            P.dma(out[t2], f32v(0, KC, 256), "outd", [], ["outd"])

    P.emit(final_keys=["outd_sp"] + (["dbg_sp"] if dbg_items else []))
    es.close()
    return nc


def _t5_bucket_np(rel):
    import jax.numpy as jnp
    NUM_BUCKETS, MAX_DISTANCE = 32, 128
    rel = jnp.asarray(rel, dtype=jnp.int32)
    half = NUM_BUCKETS // 2
    max_exact = half // 2
    ret = jnp.where(rel > 0, half, 0)
    n = jnp.abs(rel)
    nf = jnp.maximum(n, 1).astype(jnp.float32)
    large = max_exact + (jnp.log(nf / max_exact) / math.log(MAX_DISTANCE / max_exact) * (half - max_exact)).astype(jnp.int32)
    large = jnp.minimum(large, half - 1)
    return np.asarray(ret + jnp.where(n < max_exact, n, large))


def prepare_shared(inp, L):
    f = np.float32
    NCH = L // 32
    sh = {}
    g = np.stack([inp["norm1_g"][0], inp["norm1_g"][1], inp["norm2_g"][0], inp["norm2_g"][1], inp["final_g"]], 0)
    sh["gains"] = np.ascontiguousarray(g.reshape(5, KC, 128).transpose(2, 0, 1)).astype(f)
    w = inp["w_in"]
    u_perm = np.array([(c % 32) * 16 + (c // 32) for c in range(512)])
    cols = [w[:, :, u_perm], w[:, :, 512:1024],
            w[:, :, 1024:1088], w[:, :, 1024:1088], w[:, :, 1088:1152], w[:, :, 1088:1152],
            w[:, :, 1152:1280], w[:, :, 1280:3328]]
    sh["w_in"] = np.ascontiguousarray(np.concatenate(cols, axis=2)).astype(f)
    lam_re, lam_im, log_dt = inp["s5_lambda_re"], inp["s5_lambda_im"], inp["s5_log_dt"]
    s5a = np.zeros((DEPTH, 128, 3, 32), f)
    for l in range(DEPTH):
        s5a[l, :, 0, :] = lam_re[l].transpose(0, 2, 1).reshape(128, 32)
        s5a[l, :, 1, :] = lam_im[l].transpose(0, 2, 1).reshape(128, 32)
        s5a[l, :, 2, :] = np.repeat(log_dt[l][:, None, :], 64, axis=1).reshape(128, 32)
    sh["s5a"] = s5a
    s5b = np.zeros((DEPTH, 128, 4, 32, 16), f)
    for l in range(DEPTH):
        s5b[l, :, 0] = inp["s5_b_re"][l].transpose(0, 2, 1, 3).reshape(128, 32, 16)
        s5b[l, :, 1] = inp["s5_b_im"][l].transpose(0, 2, 1, 3).reshape(128, 32, 16)
        s5b[l, :, 2] = inp["s5_c_re"][l].transpose(0, 3, 1, 2).reshape(128, 32, 16)
        s5b[l, :, 3] = inp["s5_c_im"][l].transpose(0, 3, 1, 2).reshape(128, 32, 16)
    sh["s5b"] = s5b
    s5d = np.zeros((DEPTH, 4, 32, 16, 4), f)
    dd = inp["s5_d"].reshape(DEPTH, 32, 16)
    for ho in range(16):
        s5d[:, ho % 4, :, ho, ho // 4] = dd[:, :, ho]
    sh["s5d"] = s5d.reshape(DEPTH, 4, 32, 64)
    sh["w_glu"] = inp["s5_w_glu"].astype(f)
    sh["w_a"] = inp["w_branch_a"].astype(f)
    sh["w_b"] = inp["w_branch_b"].astype(f)
    sh["w_out"] = inp["w_out"].astype(f)
    sh["w_fg"] = inp["ffn_w_gate"].astype(f)
    sh["w_fu"] = inp["ffn_w_up"].astype(f)
    sh["w_fd"] = inp["ffn_w_down"].astype(f)
    sh["sinkb"] = np.ascontiguousarray(np.broadcast_to(inp["attn_sink"][None], (128, DEPTH, 8))).astype(f)
    s_ = np.arange(128)[:, None, None]
    r_ = np.arange(3)[None, :, None]
    q_ = np.arange(128)[None, None, :]
    rel = (r_ - 1) * 128 + s_ - q_
    bucket = _t5_bucket_np(rel)
    bg = inp["rel_bias"][bucket]
    sh["biasg"] = np.ascontiguousarray(bg.transpose(0, 1, 3, 2)).astype(f)
    sh["maskc"] = np.where(np.abs(rel) <= 128, 0.0, -30000.0).astype(f)
    cst = np.zeros((128, NTK + 128 + 1 + 16), f)
    k = np.arange(32)
    for d in range(2):
        rows = slice(64 * d, 64 * d + 64)
        cst[rows, 0:32] = (31 - k) if d == 0 else k
        cst[rows, 32:64] = (32 - k) if d == 0 else (k + 1)
        k8 = np.arange(8)
        cst[rows, 64:72] = 4 * (7 - k8) if d == 0 else 4 * k8
        k4 = np.arange(4)
        cst[rows, 72:76] = (3 - k4) if d == 0 else k4
        cst[rows, 76] = 1.0
        cst[rows, NTK + 128] = 1.0 if d == 0 else -1.0
    cst[:, NTK:NTK + 128] = np.arange(128)[None, :]
    cst[0:16, NTK + 129:NTK + 145] = np.eye(16)
    sh["cst"] = cst
    import ml_dtypes
    sh["identd"] = np.eye(128, dtype=f)
    return sh


_NC_CACHE = {}


def kernel(**inputs):
    inp = {k: np.asarray(v) for k, v in inputs.items()}
    x = inp["x"].astype(np.float32)
    B, L, _ = x.shape
    NG = L // 512
    if L not in _NC_CACHE:
        import os
        _NC_CACHE[L] = build(L, upto=int(os.environ.get("K_UPTO", "99")))
    nc = _NC_CACHE[L]
    sh = prepare_shared(inp, L)
    in_maps = []
    for b in range(B):
        m = dict(sh)
        m["xT"] = np.ascontiguousarray(x[b].reshape(NG, 512, KC, 128).transpose(0, 3, 2, 1))
        in_maps.append(m)
    res = run_bass_kernel_spmd(nc, in_maps, core_ids=list(range(B)))
    outs = []
    for b in range(B):
        o = res.results[b]["out"]
        outs.append(o.transpose(0, 3, 2, 1).reshape(L, D))
    return np.stack(outs, 0).astype(np.float32)
```
